# Optimizing a Trainium2 kernel written in Bass

```python
import jax, jax.numpy as jnp
from jax import lax
import numpy as np

D_MODEL = 2048
BATCH = 8
SEQ = 4096
DEPTH = 1
DEC_BATCH = 4
DEC_SEQ = 8192
PAST_LEN = 128

N_META = 16
GRID_W = 64
HEAD_DIM = 128
N_Q_HEADS = (D_MODEL // 2) // HEAD_DIM
N_KV_HEADS = N_Q_HEADS // 4
ATTN_WIDTH = N_Q_HEADS * HEAD_DIM
KV_WIDTH = N_KV_HEADS * HEAD_DIM
LRU_WIDTH = D_MODEL - ATTN_WIDTH
LRU_BLOCKS = 8
LRU_BLOCK = LRU_WIDTH // LRU_BLOCKS
LRU_CONV_W = 4
LRU_C = 8.0
MIX_WIDTH = ATTN_WIDTH + LRU_WIDTH
IN_WIDTH = ATTN_WIDTH + 2 * KV_WIDTH + 2 * LRU_WIDTH
FFN_DIM = 5632
FFN_CONV_W = 3
Q_BLOCK = 128
ROPE_THETA = 10000.0
EPS = 1e-6

kernel_name = 'hymba_axial_gqa_rglru_encoder'


def rms_norm(x, g):
    xf = x.astype(jnp.float32)
    y = xf * lax.rsqrt(jnp.mean(xf * xf, axis=-1, keepdims=True) + EPS)
    return (y * g.astype(jnp.float32)).astype(x.dtype)


def dwconv(x, w, b, pad_left):
    K = w.shape[0]
    L = x.shape[1]
    xp = jnp.pad(x, ((0, 0), (pad_left, K - 1 - pad_left), (0, 0)))
    out = xp[:, 0:L] * w[0]
    for k in range(1, K):
        out = out + xp[:, k:k + L] * w[k]
    return out + b


def axial_rope_tables(n_tokens):
    rows = n_tokens // GRID_W
    t_row = jnp.repeat(jnp.arange(rows), GRID_W).astype(jnp.float32)
    t_col = jnp.tile(jnp.arange(GRID_W), rows).astype(jnp.float32)
    half = HEAD_DIM // 2
    inv = ROPE_THETA ** (-jnp.arange(0, half, 2, dtype=jnp.float32) / half)
    ang = jnp.concatenate([t_row[:, None] * inv, t_col[:, None] * inv], axis=-1)
    ang = jnp.concatenate([jnp.zeros((N_META, half), jnp.float32), ang], axis=0)
    return jnp.cos(ang), jnp.sin(ang)


def apply_axial_rope(x, cos, sin):
    B, L, H, D = x.shape
    q = D // 4
    xf = x.astype(jnp.float32).reshape(B, L, H, 2, 2, q)
    x1 = xf[..., 0, :]
    x2 = xf[..., 1, :]
    c = cos.reshape(L, 1, 2, q)
    s = sin.reshape(L, 1, 2, q)
    out = jnp.stack([x1 * c - x2 * s, x2 * c + x1 * s], axis=-2)
    return out.reshape(B, L, H, D).astype(x.dtype)


def gqa_block_attention(q, k, v):
    B, L = q.shape[0], q.shape[1]
    G = N_Q_HEADS // N_KV_HEADS
    nblk = -(-L // Q_BLOCK)
    qp = jnp.pad(q, ((0, 0), (0, nblk * Q_BLOCK - L), (0, 0), (0, 0)))
    qb = qp.reshape(B, nblk, Q_BLOCK, N_KV_HEADS, G, HEAD_DIM).transpose(1, 0, 2, 3, 4, 5)
    scale = HEAD_DIM ** -0.5

    def block(qi):
        s = jnp.einsum('bqkgd,bskd->bkgqs', qi, k).astype(jnp.float32) * scale
        p = jax.nn.softmax(s, axis=-1).astype(v.dtype)
        return jnp.einsum('bkgqs,bskd->bqkgd', p, v)

    o = lax.map(block, qb)
    o = o.transpose(1, 0, 2, 3, 4, 5).reshape(B, nblk * Q_BLOCK, ATTN_WIDTH)
    return o[:, :L]


def rg_lru(x, w_r, b_r, w_i, b_i, lam, reverse):
    B, L, C = x.shape
    xb = x.reshape(B, L, LRU_BLOCKS, LRU_BLOCK)
    r = jax.nn.sigmoid((jnp.einsum('blnc,ncd->blnd', xb, w_r).reshape(B, L, C) + b_r).astype(jnp.float32))
    i = jax.nn.sigmoid((jnp.einsum('blnc,ncd->blnd', xb, w_i).reshape(B, L, C) + b_i).astype(jnp.float32))
    log_a = -LRU_C * r * jax.nn.softplus(-lam.astype(jnp.float32))
    a = jnp.exp(log_a)
    u = jnp.sqrt(-jnp.expm1(2.0 * log_a)) * (i * x.astype(jnp.float32))

    def combine(e1, e2):
        a1, b1 = e1
        a2, b2 = e2
        return a1 * a2, a2 * b1 + b2

    _, h = lax.associative_scan(combine, (a, u), reverse=reverse, axis=1)
    return h


def encoder_layer(h, norm_pre_mix, w_in, q_norm, k_norm, lru_conv_w, lru_conv_b, lru_w_r, lru_b_r,
                  lru_w_i, lru_b_i, lru_lambda, attn_out_norm, lru_out_norm, w_out, norm_post_mix,
                  norm_pre_ffn, w_up, ffn_conv_w, ffn_conv_b, w_down, norm_post_ffn):
    B, L, _ = h.shape
    z = rms_norm(h, norm_pre_mix) @ w_in
    o1 = ATTN_WIDTH
    o2 = o1 + KV_WIDTH
    o3 = o2 + KV_WIDTH
    o4 = o3 + LRU_WIDTH
    q = z[..., :o1].reshape(B, L, N_Q_HEADS, HEAD_DIM)
    k = z[..., o1:o2].reshape(B, L, N_KV_HEADS, HEAD_DIM)
    v = z[..., o2:o3].reshape(B, L, N_KV_HEADS, HEAD_DIM)
    xr = z[..., o3:o4]
    gy = z[..., o4:]
    cos, sin = axial_rope_tables(L - N_META)
    q = apply_axial_rope(rms_norm(q, q_norm), cos, sin)
    k = apply_axial_rope(rms_norm(k, k_norm), cos, sin)
    attn = gqa_block_attention(q, k, v)
    xc = dwconv(xr, lru_conv_w, lru_conv_b, LRU_CONV_W // 2)
    hr = (rg_lru(xc, lru_w_r[0], lru_b_r[0], lru_w_i[0], lru_b_i[0], lru_lambda[0], False)
          + rg_lru(xc, lru_w_r[1], lru_b_r[1], lru_w_i[1], lru_b_i[1], lru_lambda[1], True))
    lru = (hr * jax.nn.gelu(gy.astype(jnp.float32))).astype(h.dtype)
    mixed = jnp.concatenate([rms_norm(attn, attn_out_norm), rms_norm(lru, lru_out_norm)], axis=-1)
    h = h + rms_norm(mixed @ w_out, norm_post_mix)
    up = rms_norm(h, norm_pre_ffn) @ w_up
    g = dwconv(up[..., :FFN_DIM], ffn_conv_w, ffn_conv_b, FFN_CONV_W // 2)
    f = (jax.nn.silu(g) * up[..., FFN_DIM:]) @ w_down
    return h + rms_norm(f, norm_post_ffn)


def run_trunk(x, meta_tokens, weights):
    B = x.shape[0]
    meta = jnp.broadcast_to(meta_tokens.astype(x.dtype)[None], (B, N_META, D_MODEL))
    h = jnp.concatenate([meta, x], axis=1)
    for l in range(DEPTH):
        h = encoder_layer(h, *[w[l] for w in weights])
    return h[:, N_META:]


def setup_inputs(seed: int = 0) -> dict:
    key = jax.random.key(seed)
    ks = jax.random.split(key, 24)
    f32 = jnp.float32

    def nrm(k, shape, scale):
        return jax.random.normal(k, shape, f32) * scale

    def gain(k, n):
        return 1.0 + 0.05 * jax.random.normal(k, (DEPTH, n), f32)

    u = jax.random.uniform(ks[13], (DEPTH, 2, LRU_WIDTH), f32, 0.9, 0.999)
    a = u ** (1.0 / LRU_C)
    lam = jnp.log(a) - jnp.log1p(-a)
    return {
        'x_prompt': nrm(ks[0], (BATCH, SEQ, D_MODEL), 1.0),
        'x_sample': nrm(ks[1], (DEC_BATCH, DEC_SEQ, D_MODEL), 1.0),
        'meta_tokens': nrm(ks[2], (N_META, D_MODEL), 1.0),
        'norm_pre_mix': gain(ks[3], D_MODEL),
        'w_in': nrm(ks[4], (DEPTH, D_MODEL, IN_WIDTH), D_MODEL ** -0.5),
        'q_norm': gain(ks[5], HEAD_DIM),
        'k_norm': gain(ks[6], HEAD_DIM),
        'lru_conv_w': nrm(ks[7], (DEPTH, LRU_CONV_W, LRU_WIDTH), LRU_CONV_W ** -0.5),
        'lru_conv_b': nrm(ks[8], (DEPTH, LRU_WIDTH), 0.02),
        'lru_w_r': nrm(ks[9], (DEPTH, 2, LRU_BLOCKS, LRU_BLOCK, LRU_BLOCK), LRU_BLOCK ** -0.5),
        'lru_b_r': nrm(ks[10], (DEPTH, 2, LRU_WIDTH), 0.02),
        'lru_w_i': nrm(ks[11], (DEPTH, 2, LRU_BLOCKS, LRU_BLOCK, LRU_BLOCK), LRU_BLOCK ** -0.5),
        'lru_b_i': nrm(ks[12], (DEPTH, 2, LRU_WIDTH), 0.02),
        'lru_lambda': lam,
        'attn_out_norm': gain(ks[14], ATTN_WIDTH),
        'lru_out_norm': gain(ks[15], LRU_WIDTH),
        'w_out': nrm(ks[16], (DEPTH, MIX_WIDTH, D_MODEL), MIX_WIDTH ** -0.5),
        'norm_post_mix': gain(ks[17], D_MODEL),
        'norm_pre_ffn': gain(ks[18], D_MODEL),
        'w_up': nrm(ks[19], (DEPTH, D_MODEL, 2 * FFN_DIM), D_MODEL ** -0.5),
        'ffn_conv_w': nrm(ks[20], (DEPTH, FFN_CONV_W, FFN_DIM), FFN_CONV_W ** -0.5),
        'ffn_conv_b': nrm(ks[21], (DEPTH, FFN_DIM), 0.02),
        'w_down': nrm(ks[22], (DEPTH, FFN_DIM, D_MODEL), FFN_DIM ** -0.5),
        'norm_post_ffn': gain(ks[23], D_MODEL),
    }


def reference(x_prompt, x_sample, meta_tokens, norm_pre_mix, w_in, q_norm, k_norm, lru_conv_w, lru_conv_b,
              lru_w_r, lru_b_r, lru_w_i, lru_b_i, lru_lambda, attn_out_norm, lru_out_norm, w_out,
              norm_post_mix, norm_pre_ffn, w_up, ffn_conv_w, ffn_conv_b, w_down, norm_post_ffn):
    weights = (norm_pre_mix, w_in, q_norm, k_norm, lru_conv_w, lru_conv_b, lru_w_r, lru_b_r,
               lru_w_i, lru_b_i, lru_lambda, attn_out_norm, lru_out_norm, w_out, norm_post_mix,
               norm_pre_ffn, w_up, ffn_conv_w, ffn_conv_b, w_down, norm_post_ffn)
    y_prompt = run_trunk(x_prompt, meta_tokens, weights)
    y_sample = run_trunk(x_sample, meta_tokens, weights)
    return (y_prompt, y_sample)
```

```python
import contextlib
import numpy as np
import concourse.bass as bass
import concourse.mybir as mybir
from concourse.ap import AP
from concourse.bass_utils import run_bass_kernel_spmd

F32 = mybir.dt.float32
BF16 = mybir.dt.bfloat16
AF = mybir.ActivationFunctionType
ALU = mybir.AluOpType

D = 2048
HD = 128
NQH = 8
NKVH = 2
LRUW = 1024
FFN = 5632
NJ = FFN // 128
NMETA = 16
EPS = 1e-6
ENGS = ('pe', 'act', 'dve', 'pool', 'sp')
NDSEM = 8


class Buf:
    __slots__ = ('w', 'r')

    def __init__(self):
        self.w = None
        self.r = {}


class T:
    def __init__(self, t):
        self.t = t
        self.b = Buf()
        self.sub = {}

    def B(self, j):
        if j not in self.sub:
            self.sub[j] = Buf()
        return self.sub[j]


class Ring:
    def __init__(self, tiles):
        self.tiles = tiles
        self.i = 0

    def next(self):
        t = self.tiles[self.i % len(self.tiles)]
        self.i += 1
        return t


class Sched:
    def __init__(self, nc, sems, dsems):
        self.nc = nc
        self.sems = sems
        self.dsems = dsems
        self.cnt = {e: 0 for e in ENGS}
        self.dcnt = {e: 0 for e in ENGS}
        self.lastdma = {}
        self.seen = {e: {} for e in ENGS}
        self.ops = []
        self.q = {e: [] for e in ENGS}
        self.total_ops = 0
        self.total_waits = 0
        self.epoch = 0

    def op(self, eng, fn, reads=(), writes=(), dma=False):
        deps = set()
        ep = self.epoch
        for b in reads:
            if b.w is not None and b.w[0] == ep:
                deps.add(b.w[1])
        for b in writes:
            if b.w is not None and b.w[0] == ep:
                deps.add(b.w[1])
            for (e2, i2) in b.r.values():
                if e2 == ep:
                    deps.add(i2)
        i = len(self.ops)
        rk = ('dma', i) if dma else eng
        for b in reads:
            b.r[rk] = (ep, i)
        for b in writes:
            b.w = (ep, i)
            b.r = {}
        deps.discard(i)
        self.ops.append((eng, fn, deps, dma))
        self.q[eng].append(i)
        return i

    def _semh(self, key):
        if key[0] == 'c':
            return self.sems[key[1]]
        return self.dsems[key[1]][key[2]]

    def flush(self, final=False):
        nc = self.nc
        ops = self.ops
        n = len(ops)
        signal = [False] * n
        for i, (eng, fn, deps, dma) in enumerate(ops):
            for d in deps:
                de, _, _, ddma = ops[d]
                if ddma:
                    continue
                if de != eng or dma or eng != 'pe':
                    signal[d] = True
        for e in ENGS:
            for i in reversed(self.q[e]):
                if not ops[i][3]:
                    signal[i] = True
                    break
        bar = [(('c', e), self.cnt[e]) for e in ENGS if self.cnt[e] > 0]
        bar += list(self.lastdma.values())
        ev = [None] * n
        prevdma = [None] * n
        for i, (eng, fn, deps, dma) in enumerate(ops):
            if dma:
                k = self.dcnt[eng]
                self.dcnt[eng] += 1
                slot = k % NDSEM
                prevdma[i] = self.lastdma.get((eng, slot))
                ev[i] = (('d', eng, slot), (k // NDSEM + 1) * 16)
                self.lastdma[(eng, slot)] = ev[i]
            elif signal[i]:
                self.cnt[eng] += 1
                ev[i] = (('c', eng), self.cnt[eng])
        engobj = {'pe': nc.tensor, 'act': nc.scalar, 'dve': nc.vector, 'pool': nc.gpsimd, 'sp': nc.sync}
        endbar = None
        if final:
            endbar = [(('c', e), self.cnt[e]) for e in ENGS if self.cnt[e] > 0] + list(self.lastdma.values())

        def run_engine(eng):
            e = engobj[eng]
            seen = self.seen[eng]

            def wait(k, v):
                if seen.get(k, 0) >= v:
                    return
                seen[k] = v
                e.wait_ge(self._semh(k), v)
                self.total_waits += 1

            if self.q[eng]:
                for k, v in bar:
                    if k == ('c', eng):
                        continue
                    wait(k, v)
            for i in self.q[eng]:
                _, fn, deps, dma = ops[i]
                need = {}
                for d in deps:
                    de, _, _, ddma = ops[d]
                    if (not ddma) and de == eng and (not dma) and eng == 'pe':
                        continue
                    k, v = ev[d]
                    if need.get(k, 0) < v:
                        need[k] = v
                if dma and prevdma[i] is not None:
                    k, v = prevdma[i]
                    if need.get(k, 0) < v:
                        need[k] = v
                for k, v in need.items():
                    wait(k, v)
                ins = fn(e)
                if ev[i] is not None:
                    ins.then_inc(self._semh(ev[i][0]), 16 if dma else 1)
            if final and eng == 'sp':
                for k, v in endbar:
                    wait(k, v)

        with nc.Block() as block:
            @block.tensor
            def _(_e):
                run_engine('pe')

            @block.scalar
            def _(_e):
                run_engine('act')

            @block.vector
            def _(_e):
                run_engine('dve')

            @block.gpsimd
            def _(_e):
                run_engine('pool')

            @block.sync
            def _(_e):
                run_engine('sp')
        self.total_ops += n
        self.epoch += 1
        self.ops = []
        self.q = {e: [] for e in ENGS}


class Ph:
    def __init__(self, nc, name):
        self.nc = nc
        self.name = name
        self.st = contextlib.ExitStack()
        self.k = 0

    def sb(self, shape, dt):
        self.k += 1
        return T(self.st.enter_context(self.nc.sbuf_tensor(f"{self.name}_s{self.k}", list(shape), dt)))

    def ps(self, shape, dt):
        self.k += 1
        return T(self.st.enter_context(self.nc.psum_tensor(f"{self.name}_p{self.k}", list(shape), dt)))

    def ring(self, shape, dt, n):
        return Ring([self.sb(shape, dt) for _ in range(n)])

    def pring(self, shape, dt, n):
        return Ring([self.ps(shape, dt) for _ in range(n)])


def bl(*xs):
    out = []
    for x in xs:
        if x is None:
            continue
        if isinstance(x, Buf):
            out.append(x)
        elif isinstance(x, T):
            out.append(x.b)
        else:
            out.extend(bl(*x))
    return out


class Ops:
    def __init__(self, S):
        self.S = S

    def act(self, out, in_, func, r, w, **kw):
        self.S.op('act', lambda e: e.activation(out=out, in_=in_, func=func, **kw), bl(r), bl(w))

    def ts(self, eng, out, in0, s1, s2, op0, op1, r, w):
        self.S.op(eng, lambda e: e.tensor_scalar(out=out, in0=in0, scalar1=s1, scalar2=s2, op0=op0, op1=op1),
                  bl(r), bl(w))

    def stt(self, out, in0, sc, in1, op0, op1, r, w):
        self.S.op('dve', lambda e: e.scalar_tensor_tensor(out=out, in0=in0, scalar=sc, in1=in1, op0=op0, op1=op1),
                  bl(r), bl(w))

    def tt(self, eng, out, in0, in1, op, r, w):
        self.S.op(eng, lambda e: e.tensor_tensor(out=out, in0=in0, in1=in1, op=op), bl(r), bl(w))

    def cp(self, eng, out, in_, r, w):
        if eng == 'act':
            self.S.op('act', lambda e: e.activation(out=out, in_=in_, func=AF.Copy), bl(r), bl(w))
        else:
            self.S.op(eng, lambda e: e.tensor_copy(out=out, in_=in_), bl(r), bl(w))

    def recip(self, out, in_, r, w):
        self.S.op('dve', lambda e: e.reciprocal(out=out, in_=in_), bl(r), bl(w))

    def memset(self, eng, ap, val, w):
        self.S.op(eng, lambda e: e.memset(ap, val), (), bl(w))

    def mm(self, out, lhsT, rhs, start, stop, r, w):
        self.S.op('pe', lambda e: e.matmul(out, lhsT=lhsT, rhs=rhs, start=start, stop=stop), bl(r), bl(w))

    def tr(self, out, in_, ident, r, w):
        self.S.op('pe', lambda e: e.transpose(out=out, in_=in_, identity=ident), bl(r), bl(w))

    def scan(self, out, d0, d1, init, r, w):
        self.S.op('dve', lambda e: e.tensor_tensor_scan(out=out, data0=d0, data1=d1, initial=init,
                                                        op0=ALU.mult, op1=ALU.add), bl(r), bl(w))

    def dma(self, eng, out, in_, r, w):
        self.S.op(eng, lambda e: e.dma_start(out=out, in_=in_), bl(r), bl(w), dma=True)

    def rstd(self, out, ss, n, tmp, r, w):
        self.act(tmp, ss, AF.Ln, r, w, scale=1.0 / n, bias=EPS)
        self.act(out, tmp, AF.Exp, w, w, scale=-0.5)


def rev(v):
    (ps, pn), (fs, fn) = v.ap
    return AP(v.tensor, v.offset + (fn - 1) * fs, [[ps, pn], [-fs, fn]])


def swap_halves(v, nh):
    (ps, pn), (fs, fn) = v.ap
    assert fs == 1 and fn == nh * 128
    return AP(v.tensor, v.offset + 32, [[ps, pn], [64, 2 * nh], [-32, 2], [1, 32]])


def chunks(n, c):
    return [(i, min(c, n - i)) for i in range(0, n, c)]


def rup(n, c):
    return (n + c - 1) // c * c


class Job:
    pass


def build_program(NA_A, NB_all, NB_own, NB_out):
    nc = bass.Bass("TRN2", target_bir_lowering=False)

    def din(name, shape, dt=F32):
        return nc.dram_tensor(name, list(shape), dt, kind="ExternalInput").ap()

    def dout(name, shape, dt=F32):
        return nc.dram_tensor(name, list(shape), dt, kind="ExternalOutput").ap()

    def dscr(name, shape, dt):
        return nc.dram_tensor(name, list(shape), dt, kind="Internal").ap()

    NAmax = max(NA_A, NB_all)
    NOmax = max(NA_A, NB_own)
    NAp = rup(NAmax, 1024)
    NOp = rup(NOmax, 1024)

    w_in = din("w_in", [D, 3584])
    w_out = din("w_out", [D, D])
    w_up = din("w_up", [D, 2 * FFN])
    w_down = din("w_down", [FFN, D])
    gcols = din("gcols", [128, 48])
    gpost = din("gpost", [2, D])
    qkn = din("qkn", [2, 128])

    jobs = []
    for nm, N_all, N_own, N_out in (("A", NA_A, NA_A, NA_A), ("B", NB_all, NB_own, NB_out)):
        j = Job()
        j.name = nm
        j.N_all, j.N_own, j.N_out = N_all, N_own, N_out
        j.x = din(f"x{nm}", [N_all, D])
        j.cos = din(f"cos{nm}", [N_all, 128])
        j.sin = din(f"sin{nm}", [N_all, 128])
        j.lruP = din(f"lruP{nm}", [128, 96])
        j.gates = din(f"gates{nm}", [32 * 128, 128])
        j.ffnP = din(f"ffnP{nm}", [128, 176])
        j.out = dout(f"y{nm}", [N_out, D])
        jobs.append(j)

    class Scr:
        pass
    scr = Scr()
    scr.wb_in = dscr("wb_in", [16, 128, 3584], BF16)
    scr.wb_out = dscr("wb_out", [16, 128, D], BF16)
    scr.wb_up = dscr("wb_up", [NJ, 128, 16, 2, 128], BF16)
    scr.wb_down = dscr("wb_down", [NJ, 128, D], BF16)
    scr.xnT = dscr("s_xnT", [16, 128, NAp], BF16)
    scr.QT = dscr("s_QT", [128, 8, NOp], BF16)
    scr.KT = dscr("s_KT", [128, 2, NAp], BF16)
    scr.V = dscr("s_V", [NAp, 256], BF16)
    scr.xrT = dscr("s_xrT", [LRUW, NAp], F32)
    scr.gyT = dscr("s_gyT", [LRUW, NAp], F32)
    scr.xhT = dscr("s_xhT", [LRUW, NAp], F32)
    scr.xcbT = dscr("s_xcbT", [LRUW, NAp], BF16)
    scr.lruT = dscr("s_lruT", [LRUW, NAp], BF16)
    scr.attnT = dscr("s_attnT", [8, 128, NOp], BF16)
    scr.h1 = dscr("s_h1", [NOp, D], F32)
    scr.hnT = dscr("s_hnT", [16, 128, NOp], BF16)

    with contextlib.ExitStack() as top:
        sems = {e: top.enter_context(nc.semaphore(f"sem_{e}")) for e in ENGS}
        dsems = {e: [top.enter_context(nc.semaphore(f"dsem_{e}{i}")) for i in range(NDSEM)]
                 for e in ('sp', 'pool', 'act')}
        S = Sched(nc, sems, dsems)
        O = Ops(S)
        identf = T(top.enter_context(nc.sbuf_tensor("identf", [128, 128], F32)))
        ident = T(top.enter_context(nc.sbuf_tensor("ident", [128, 128], BF16)))
        O.memset('pool', identf.t[:], 0.0, identf)
        S.op('pool', lambda e: e.affine_select(out=identf.t[:], in_=identf.t[:], pattern=[[-1, 128]],
                                               compare_op=ALU.not_equal, fill=1.0, base=0, channel_multiplier=1),
             bl(identf), bl(identf))
        O.cp('dve', ident.t[:], identf.t[:], identf, ident)

        gt = T(top.enter_context(nc.sbuf_tensor("gcols_sb", [128, 48], F32)))
        S.flush()
        phase_W(nc, S, O, w_in, gcols, scr, gt)
        for ji, j in enumerate(jobs):
            phase_A1(nc, S, O, j, scr, ident, qkn)
            phase_A2(nc, S, O, j, scr)
            nconv = 8 * len(chunks(j.N_all, 2048))
            bg = [((lambda P, j=j: gen_conv(nc, S, O, P, j, scr)), nconv)]
            if ji == 0:
                bg.append(((lambda P: gen_W2(nc, S, O, P, w_out, w_up, w_down, scr, gt)), 124))
            phase_T(nc, S, O, j, scr, ident, bg)
            phase_L(nc, S, O, j, scr, identf)
            phase_M(nc, S, O, j, scr, ident, gpost)
            phase_F(nc, S, O, j, scr, gpost)
        S.flush(final=True)
    nc._sched_stats = (S.total_ops, S.total_waits, dict(S.cnt))
    return nc


def phase_W(nc, S, O, w_in, gcols, scr, gt):
    P = Ph(nc, "W")
    with P.st:
        O.dma('sp', gt.t[:], gcols[:, :], (), gt)
        wf = P.ring([128, 1792], F32, 3)
        wb = P.ring([128, 1792], BF16, 3)
        for kc in range(16):
            rows = slice(kc * 128, (kc + 1) * 128)
            for c0 in (0, 1792):
                a = wf.next()
                b = wb.next()
                O.dma('sp', a.t[:, :], w_in[rows, c0:c0 + 1792], (), a)
                if c0 == 0:
                    O.act(b.t[:, :], a.t[:, :], AF.Copy, (a, gt), b, scale=gt.t[:, kc:kc + 1])
                else:
                    O.ts('dve', b.t[:, :], a.t[:, :], gt.t[:, kc:kc + 1], None, ALU.mult, ALU.bypass, (a, gt), b)
                O.dma('pool', scr.wb_in[kc, :, c0:c0 + 1792], b.t[:, :], b, ())
        S.flush()


def gen_W2(nc, S, O, P, w_out, w_up, w_down, scr, gt):
    CW = 2816
    wf = P.ring([128, CW], F32, 3)
    wb = P.ring([128, CW], BF16, 3)

    def unit(src_ap, cw, scale, dst_ap, dst_view=None):
        a = wf.next()
        b = wb.next()
        O.dma('sp', a.t[:, :cw], src_ap, (), a)
        if scale is None:
            O.cp('dve', b.t[:, :cw], a.t[:, :cw], a, b)
        else:
            O.ts('dve', b.t[:, :cw], a.t[:, :cw], scale, None, ALU.mult, ALU.bypass, (a, gt), b)
        O.dma('pool', dst_ap, b.t[:, :cw] if dst_view is None else dst_view(b), b, ())

    for kc in range(16):
        rows = slice(kc * 128, (kc + 1) * 128)
        unit(w_out[rows, :], D, gt.t[:, 16 + kc:17 + kc], scr.wb_out[kc, :, :])
        yield
    for kc in range(16):
        rows = slice(kc * 128, (kc + 1) * 128)
        for half in range(2):
            for jh in range(2):
                c0 = half * FFN + jh * CW
                dst = scr.wb_up[jh * 22:(jh + 1) * 22, :, kc, half, :].rearrange("j p n -> p j n")
                unit(w_up[rows, c0:c0 + CW], CW, gt.t[:, 32 + kc:33 + kc], dst,
                     lambda b: b.t[:, :CW].rearrange("p (j n) -> p j n", n=128))
                yield
    for kc in range(NJ):
        unit(w_down[kc * 128:(kc + 1) * 128, :], D, None, scr.wb_down[kc, :, :])
        yield


def gen_conv(nc, S, O, P, job, scr):
    NA = job.N_all
    prm = P.sb([128, 96], F32)
    O.dma('sp', prm.t[:], job.lruP[:, :], (), prm)
    hw = P.sb([128, 48], F32)
    O.ts('dve', hw.t[:, :], prm.t[:, 0:48], 0.5, None, ALU.mult, ALU.bypass, prm, hw)
    xrb = P.ring([128, 2052], F32, 2)
    acc = P.ring([128, 2048], F32, 2)
    cbr = P.ring([128, 2048], BF16, 2)
    for c in range(8):
        rows = slice(c * 128, (c + 1) * 128)
        for (b0, bn) in chunks(NA, 2048):
            XR = xrb.next()
            lo = max(0, b0 - 2)
            hi = min(NA, b0 + bn + 2)
            if b0 - 2 < 0:
                O.memset('dve', XR.t[:, 0:2], 0.0, XR)
            if b0 + bn + 2 > NA:
                O.memset('dve', XR.t[:, NA - (b0 - 2):bn + 4], 0.0, XR)
            O.dma('sp', XR.t[:, lo - (b0 - 2):hi - (b0 - 2)], scr.xrT[rows, lo:hi], XR, XR)
            A = acc.next()
            O.ts('dve', A.t[:, :bn], XR.t[:, 0:bn], hw.t[:, c * 5:c * 5 + 1], hw.t[:, 40 + c:41 + c],
                 ALU.mult, ALU.add, (XR, hw), A)
            for k in range(1, 5):
                A2 = acc.next()
                O.stt(A2.t[:, :bn], XR.t[:, k:k + bn], hw.t[:, c * 5 + k:c * 5 + k + 1], A.t[:, :bn],
                      ALU.mult, ALU.add, (XR, hw, A), A2)
                A = A2
            O.dma('pool', scr.xhT[rows, b0:b0 + bn], A.t[:, :bn], A, ())
            CB = cbr.next()
            O.ts('dve', CB.t[:, :bn], A.t[:, :bn], 2.0, None, ALU.mult, ALU.bypass, A, CB)
            O.dma('pool', scr.xcbT[rows, b0:b0 + bn], CB.t[:, :bn], CB, ())
            yield


def phase_A1(nc, S, O, job, scr, ident, qkn):
    NA, NO = job.N_all, job.N_own
    P = Ph(nc, "A1" + job.name)
    with P.st:
        w = P.sb([128, 16, 1536], BF16)
        O.dma('sp', w.t[:], scr.wb_in[:, :, 0:1536].rearrange("k p n -> p k n"), (), w)
        gqk = P.sb([128, 2, 128], F32)
        O.dma('sp', gqk.t[:, 0, :], qkn[0:1, :].to_broadcast([128, 128]), (), gqk)
        O.dma('sp', gqk.t[:, 1, :], qkn[1:2, :].to_broadcast([128, 128]), gqk, gqk)
        xt = P.ring([128, D], F32, 2)
        junk = P.sb([128, D], BF16)
        xn = P.ring([128, D], BF16, 3)
        xnT = P.ring([128, 16, 512], BF16, 2)
        cst = P.ring([128, 2, 128], F32, 8)
        csg = P.ring([128, 4, 128], F32, 2)
        sm = P.ring([128, 4], F32, 4)
        ssq = P.ring([128, 32], F32, 2)
        sqt = P.sb([128, 1280], F32)
        qn = P.sb([128, 1280], F32)
        t1 = P.sb([128, 1280], F32)
        t2 = P.sb([128, 1280], F32)
        qr = P.ring([128, 1280], BF16, 2)
        qT = P.ring([128, 8, 512], BF16, 2)
        kT = P.ring([128, 2, 512], BF16, 2)
        vsb = P.ring([128, 256], BF16, 2)
        pTa = P.ps([128, 8, 128], BF16)
        pTb = P.ps([128, 8, 128], BF16)
        pq = P.ps([128, 1024], F32)
        pkv = P.ps([128, 512], F32)
        pqT = P.ps([128, 8, 128], BF16)
        pkT = P.ps([128, 8, 128], BF16)

        zq = P.ring([128, 1024], F32, 3)
        zkv = P.ring([128, 512], F32, 3)
        work = []
        for (g0, gn) in chunks(NA, 512):
            grp = {'g0': g0, 'gn': gn, 'doq': g0 < NO, 'nsub': len(chunks(gn, 128))}
            for j, (s0, tn) in enumerate(chunks(gn, 128)):
                work.append({'grp': grp, 'j': j, 's0': s0, 'tn': tn})

        def st0(c):
            grp, j, s0, tn = c['grp'], c['j'], c['s0'], c['tn']
            t0 = grp['g0'] + s0
            X = xt.next()
            O.dma('sp', X.t[:tn, :], job.x[t0:t0 + tn, :], (), X)
            CS = cst.next()
            O.dma('sp', CS.t[:tn, 0, :], job.cos[t0:t0 + tn, :], (), CS)
            O.dma('sp', CS.t[:tn, 1, :], job.sin[t0:t0 + tn, :], CS, CS)
            s = sm.next()
            O.act(junk.t[:tn, :], X.t[:tn, :], AF.Square, X, s, accum_out=s.t[:tn, 0:1])
            O.rstd(s.t[:tn, 2:3], s.t[:tn, 0:1], D, s.t[:tn, 1:2], s, s)
            XN = xn.next()
            O.act(XN.t[:tn, :], X.t[:tn, :], AF.Copy, (X, s), XN, scale=s.t[:tn, 2:3])
            c['XN'], c['CS'] = XN, CS

        def st0b(c):
            grp, j, s0, tn = c['grp'], c['j'], c['s0'], c['tn']
            if j == 0:
                grp['XT'] = xnT.next()
            XT, XN = grp['XT'], c['XN']
            for kc in range(16):
                pT = pTa if kc < 8 else pTb
                O.tr(pT.t[:, kc % 8, 0:tn], XN.t[:tn, kc * 128:(kc + 1) * 128], ident.t[:tn, :tn],
                     (XN, ident), pT)
            O.cp('dve', XT.t[:, 0:8, s0:s0 + tn], pTa.t[:, :, 0:tn], pTa, XT.B(j))
            O.cp('act', XT.t[:, 8:16, s0:s0 + tn], pTb.t[:, :, 0:tn], pTb, XT.B(j))

        def st1(c):
            grp, j, s0, tn = c['grp'], c['j'], c['s0'], c['tn']
            XT, doq = grp['XT'], grp['doq']
            if doq:
                for n in range(2):
                    for kc in range(16):
                        O.mm(pq.t[:tn, n * 512:(n + 1) * 512], XT.t[:, kc, s0:s0 + tn],
                             w.t[:, kc, n * 512:(n + 1) * 512], kc == 0, kc == 15,
                             (XT.B(j), w, pq.B(n)), pq.B(n))
            for kc in range(16):
                O.mm(pkv.t[:tn, :], XT.t[:, kc, s0:s0 + tn], w.t[:, kc, 1024:1536], kc == 0, kc == 15,
                     (XT.B(j), w, pkv), pkv)
            if doq:
                ZQ = zq.next()
                O.cp('act', ZQ.t[:tn, :], pq.t[:tn, :], (pq.B(0), pq.B(1)), ZQ)
                c['ZQ'] = ZQ
            ZKV = zkv.next()
            O.cp('dve', ZKV.t[:tn, :], pkv.t[:tn, :], pkv, ZKV)
            c['ZKV'] = ZKV
            if j == grp['nsub'] - 1:
                g0, gn = grp['g0'], grp['gn']
                O.dma('pool', scr.xnT[:, :, g0:g0 + gn].rearrange("k p t -> p k t"), XT.t[:, :, :gn],
                      [XT.B(i) for i in range(grp['nsub'])], ())

        def st2a(c):
            grp, j, s0, tn = c['grp'], c['j'], c['s0'], c['tn']
            doq = grp['doq']
            ZKV, CS = c['ZKV'], c['CS']
            G = csg.next()
            O.tt('pool', G.t[:tn, 0:2, :], CS.t[:tn, :, :], gqk.t[:tn, 0:1, :].to_broadcast([tn, 2, 128]),
                 ALU.mult, (CS, gqk), G)
            O.tt('pool', G.t[:tn, 2:4, :], CS.t[:tn, :, :], gqk.t[:tn, 1:2, :].to_broadcast([tn, 2, 128]),
                 ALU.mult, (CS, gqk, G), G)
            QR = qr.next()
            sq = ssq.next()
            c['QR'] = QR

            def normrope(src, nh, c0, col0, srcbufs):
                O.tt('pool', sqt.t[:tn, col0:col0 + nh * 128], src, src, ALU.mult, srcbufs, sqt.B(c0))
                S.op('dve', lambda e: e.tensor_reduce(
                    out=sq.t[:tn, c0:c0 + nh],
                    in_=sqt.t[:tn, col0:col0 + nh * 128].rearrange("p (h d) -> p h d", d=128),
                    axis=mybir.AxisListType.X, op=ALU.add), bl(sqt.B(c0)), bl(sq))
                O.rstd(sq.t[:tn, 20 + c0:20 + c0 + nh], sq.t[:tn, c0:c0 + nh], 128,
                       sq.t[:tn, 10 + c0:10 + c0 + nh], sq, sq)
                cols = slice(col0, col0 + nh * 128)
                O.tt('dve', qn.t[:tn, cols].rearrange("p (h d) -> p h d", d=128),
                     src.rearrange("p (h d) -> p h d", d=128),
                     sq.t[:tn, 20 + c0:20 + c0 + nh].unsqueeze(2).to_broadcast([tn, nh, 128]),
                     ALU.mult, (srcbufs, sq), qn.B(c0))
                ti = 0 if c0 == 0 else 2
                O.tt('dve', t1.t[:tn, cols].rearrange("p (h d) -> p h d", d=128),
                     qn.t[:tn, cols].rearrange("p (h d) -> p h d", d=128),
                     G.t[:tn, ti:ti + 1, :].to_broadcast([tn, nh, 128]), ALU.mult, (qn.B(c0), G), t1.B(c0))
                O.tt('dve', t2.t[:tn, cols].rearrange("p (h d) -> p h d", d=128),
                     qn.t[:tn, cols].rearrange("p (h d) -> p h d", d=128),
                     G.t[:tn, ti + 1:ti + 2, :].to_broadcast([tn, nh, 128]), ALU.mult, (qn.B(c0), G),
                     t2.B(c0))
                O.tt('dve', QR.t[:tn, cols].rearrange("p (a h j) -> p a h j", h=2, j=32),
                     t1.t[:tn, cols].rearrange("p (a h j) -> p a h j", h=2, j=32),
                     swap_halves(t2.t[:tn, cols], nh), ALU.add, (t1.B(c0), t2.B(c0)), QR.B(c0))

            if doq:
                normrope(c['ZQ'].t[:tn, :], 8, 0, 0, (c['ZQ'],))
            normrope(ZKV.t[:tn, 0:256], 2, 8, 1024, (ZKV,))

        def st2b(c):
            grp, j, s0, tn = c['grp'], c['j'], c['s0'], c['tn']
            doq = grp['doq']
            t0 = grp['g0'] + s0
            if j == 0:
                grp['QT'] = qT.next() if doq else None
                grp['KT'] = kT.next()
            QT, KT, QR, ZKV = grp['QT'], grp['KT'], c['QR'], c['ZKV']
            Vt = vsb.next()
            O.cp('act', Vt.t[:tn, :], ZKV.t[:tn, 256:512], ZKV, Vt)
            O.dma('pool', scr.V[t0:t0 + tn, :], Vt.t[:tn, :], Vt, ())
            if doq:
                for h in range(8):
                    O.tr(pqT.t[:, h, 0:tn], QR.t[:tn, h * 128:(h + 1) * 128], ident.t[:tn, :tn],
                         (QR.B(0), ident), pqT)
                O.cp('act', QT.t[:, :, s0:s0 + tn], pqT.t[:, :, 0:tn], pqT, QT.B(j))
            for h in range(2):
                O.tr(pkT.t[:, h, 0:tn], QR.t[:tn, 1024 + h * 128:1024 + (h + 1) * 128], ident.t[:tn, :tn],
                     (QR.B(8), ident), pkT)
            O.cp('dve', KT.t[:, :, s0:s0 + tn], pkT.t[:, 0:2, 0:tn], pkT, KT.B(j))
            if j == grp['nsub'] - 1:
                g0, gn, nsub = grp['g0'], grp['gn'], grp['nsub']
                if doq:
                    O.dma('pool', scr.QT[:, :, g0:g0 + gn], QT.t[:, :, :gn], [QT.B(i) for i in range(nsub)], ())
                O.dma('pool', scr.KT[:, :, g0:g0 + gn], KT.t[:, :, :gn], [KT.B(i) for i in range(nsub)], ())

        nw = len(work)
        for k in range(-3, nw + 2):
            if 0 <= k - 2 < nw:
                st2a(work[k - 2])
            if 0 <= k < nw:
                st1(work[k])
            if 0 <= k + 2 < nw:
                st0b(work[k + 2])
            if 0 <= k - 2 < nw:
                st2b(work[k - 2])
            if 0 <= k + 3 < nw:
                st0(work[k + 3])
        S.flush()


def phase_A2(nc, S, O, job, scr):
    NA, NO = job.N_all, job.N_own
    P = Ph(nc, "A2" + job.name)
    with P.st:
        w = P.sb([128, 16, 2048], BF16)
        O.dma('sp', w.t[:], scr.wb_in[:, :, 1536:3584].rearrange("k p n -> p k n"), (), w)
        xg = P.ring([128, 16, 512], BF16, 2)
        stg = P.ring([128, 512], F32, 4)
        pz = P.pring([128, 512], F32, 6)
        for (g0, gn) in chunks(NA, 512):
            X = xg.next()
            O.dma('sp', X.t[:, :, :gn], scr.xnT[:, :, g0:g0 + gn].rearrange("k p t -> p k t"), (), X)
            nch = 16 if g0 < NO else 8
            for c in range(nch):
                pz_ = pz.next()
                for kc in range(16):
                    O.mm(pz_.t[:, :gn], w.t[:, kc, c * 128:(c + 1) * 128], X.t[:, kc, :gn], kc == 0, kc == 15,
                         (w, X, pz_), pz_)
                sg = stg.next()
                O.cp('act' if c % 2 else 'dve', sg.t[:, :gn], pz_.t[:, :gn], pz_, sg)
                if c < 8:
                    dst = scr.xrT[c * 128:(c + 1) * 128, g0:g0 + gn]
                else:
                    dst = scr.gyT[(c - 8) * 128:(c - 7) * 128, g0:g0 + gn]
                O.dma('pool', dst, sg.t[:, :gn], sg, ())
        S.flush()


def phase_L(nc, S, O, job, scr, identf):
    NA, NO = job.N_all, job.N_own
    NAp = rup(NA, 1024)
    GB = 3
    P = Ph(nc, "L" + job.name)
    with P.st:
        prm = P.sb([128, 96], F32)
        O.dma('sp', prm.t[:], job.lruP[:, :], (), prm)
        der = P.sb([128, 72], F32)
        gw = P.sb([128, 32, 128], BF16)
        with contextlib.ExitStack() as inner:
            gwf = T(inner.enter_context(nc.sbuf_tensor("L_gwf" + job.name, [128, 32, 128], F32)))
            O.dma('sp', gwf.t[:], job.gates.rearrange("(b c) d -> c b d", c=128), (), gwf)
            O.cp('dve', gw.t[:], gwf.t[:], gwf, gw)
            O.ts('dve', der.t[:, 0:32], prm.t[:, 48:80], 0.5, None, ALU.mult, ALU.bypass, prm, der)
            O.ts('dve', der.t[:, 64:72], prm.t[:, 40:48], 0.5, None, ALU.mult, ALU.bypass, (prm, der), der)
            O.act(der.t[:, 48:64], prm.t[:, 80:96], AF.Exp, (prm, der), der, scale=-1.0)
            O.act(der.t[:, 48:64], der.t[:, 48:64], AF.Ln, der, der, bias=1.0)
            O.ts('dve', der.t[:, 32:48], der.t[:, 48:64], -4.0, None, ALU.mult, ALU.bypass, der, der)
            S.flush()
        xh = P.sb([128, NAp], F32)
        xcb = P.sb([128, NAp], BF16)
        hb = P.sb([128, NAp], F32)
        tl = P.ring([128, 1024], F32, 14)
        gyr = P.ring([128, 1024], F32, 4)
        hfr = P.ring([128, 1024], F32, 2)
        lo_ = P.ring([128, 1024], BF16, 2)
        pz = P.pring([128, 1024], F32, 4)
        tiles = chunks(NA, 1024)

        def load_x(q, cc, t0, tn):
            r_ = slice(cc * 128, (cc + 1) * 128)
            ti = t0 // 1024
            O.dma(q, xh.t[:, t0:t0 + tn], scr.xhT[r_, t0:t0 + tn], (), xh.B(ti))
            O.dma(q, xcb.t[:, t0:t0 + tn], scr.xcbT[r_, t0:t0 + tn], (), xcb.B(ti))

        for (t0, tn) in tiles:
            load_x('sp', 0, t0, tn)
        for c in range(8):
            rows = slice(c * 128, (c + 1) * 128)

            def gates_front(d, t0, tn):
                zr = pz.next()
                zi = pz.next()
                ti = t0 // 1024
                for (s0, sn) in chunks(tn, 512):
                    O.mm(zr.t[:, s0:s0 + sn], gw.t[:, (0 * 2 + d) * 8 + c, :], xcb.t[:, t0 + s0:t0 + s0 + sn],
                         True, True, (gw, xcb.B(ti), zr.B(s0)), zr.B(s0))
                    O.mm(zi.t[:, s0:s0 + sn], gw.t[:, (1 * 2 + d) * 8 + c, :], xcb.t[:, t0 + s0:t0 + s0 + sn],
                         True, True, (gw, xcb.B(ti), zi.B(s0)), zi.B(s0))
                zrb = [zr.B(s0) for (s0, sn) in chunks(tn, 512)]
                zib = [zi.B(s0) for (s0, sn) in chunks(tn, 512)]
                thr = tl.next()
                thi = tl.next()
                a = tl.next()
                O.act(thr.t[:, :tn], zr.t[:, :tn], AF.Tanh, (zrb, der), thr, scale=0.5,
                      bias=der.t[:, d * 8 + c:d * 8 + c + 1])
                O.act(thi.t[:, :tn], zi.t[:, :tn], AF.Tanh, (zib, der), thi, scale=0.5,
                      bias=der.t[:, 16 + d * 8 + c:16 + d * 8 + c + 1])
                O.act(a.t[:, :tn], thr.t[:, :tn], AF.Exp, (thr, der), a, scale=der.t[:, 32 + d * 8 + c:33 + d * 8 + c],
                      bias=der.t[:, 32 + d * 8 + c:33 + d * 8 + c])
                if d == 1:
                    O.tt('dve', thr.t[:, :tn], a.t[:, :tn], a.t[:, :tn], ALU.mult, a, thr)
                else:
                    O.act(thr.t[:, :tn], a.t[:, :tn], AF.Square, a, thr)
                O.stt(thi.t[:, :tn], thi.t[:, :tn], 1.0, xh.t[:, t0:t0 + tn], ALU.add, ALU.mult,
                      (thi, xh.B(ti)), thi)
                return a, thr, thi

            def batches(tl_):
                return [tl_[i:i + GB] for i in range(0, len(tl_), GB)]

            carry = None
            for batch in batches(list(reversed(tiles))):
                items = [(t0, tn) + gates_front(1, t0, tn) for (t0, tn) in batch]
                for (t0, tn, a, a2, u1) in items:
                    O.act(a2.t[:, :tn], a2.t[:, :tn], AF.Sqrt, a2, a2, scale=-1.0, bias=1.0)
                for (t0, tn, a, a2, u1) in items:
                    ti = t0 // 1024
                    O.tt('dve', u1.t[:, :tn], u1.t[:, :tn], a2.t[:, :tn], ALU.mult, (u1, a2), u1)
                    init = 0.0 if carry is None else carry[0]
                    O.scan(rev(hb.t[:, t0:t0 + tn]), rev(a.t[:, :tn]), rev(u1.t[:, :tn]), init,
                           (a, u1, carry[1] if carry else None), hb.B(ti))
                    carry = (hb.t[:, t0:t0 + 1], hb.B(ti))
                if c + 1 < 8:
                    for (t0, tn, a, a2, u1) in items:
                        if t0 >= NO:
                            load_x('pool', c + 1, t0, tn)
            carry = None
            ftiles = [(t0, tn) for (t0, tn) in tiles if t0 < NO]
            for batch in batches(ftiles):
                items = [(t0, tn) + gates_front(0, t0, tn) for (t0, tn) in batch]
                gys = []
                for (t0, tn, a, a2, u1) in items:
                    GY = gyr.next()
                    O.dma('sp', GY.t[:, :tn], scr.gyT[rows, t0:t0 + tn], (), GY)
                    gys.append(GY)
                for (t0, tn, a, a2, u1) in items:
                    O.act(a2.t[:, :tn], a2.t[:, :tn], AF.Sqrt, a2, a2, scale=-1.0, bias=1.0)
                for (t0, tn, a, a2, u1), GY in zip(items, gys):
                    O.act(GY.t[:, :tn], GY.t[:, :tn], AF.Gelu_apprx_tanh, GY, GY)
                for (t0, tn, a, a2, u1), GY in zip(items, gys):
                    ti = t0 // 1024
                    O.tt('dve', u1.t[:, :tn], u1.t[:, :tn], a2.t[:, :tn], ALU.mult, (u1, a2), u1)
                    HF = hfr.next()
                    init = 0.0 if carry is None else carry[0]
                    O.scan(HF.t[:, :tn], a.t[:, :tn], u1.t[:, :tn], init, (a, u1, carry[1] if carry else None), HF)
                    carry = (HF.t[:, tn - 1:tn], HF)
                    O.tt('dve', a2.t[:, :tn], HF.t[:, :tn], hb.t[:, t0:t0 + tn], ALU.add, (HF, hb.B(ti)), a2)
                    LO = lo_.next()
                    O.tt('dve', LO.t[:, :tn], a2.t[:, :tn], GY.t[:, :tn], ALU.mult, (a2, GY), LO)
                    O.dma('pool', scr.lruT[rows, t0:t0 + tn], LO.t[:, :tn], LO, ())
                if c + 1 < 8:
                    for (t0, tn, a, a2, u1) in items:
                        load_x('pool', c + 1, t0, tn)
        S.flush()


def phase_T(nc, S, O, job, scr, ident, bgf=None):
    NA, NO = job.N_all, job.N_own
    kch = chunks(NA, 128)
    nkc = len(kch)
    nfull = NA // 128
    P = Ph(nc, "T" + job.name)
    with P.st:
        KT = P.sb([128, 2, NA], BF16)
        O.dma('sp', KT.t[:], scr.KT[:, :, 0:NA], (), KT)
        V = P.sb([128, nkc, 2, 132], BF16)
        O.memset('dve', V.t[:, :, :, 128:132], 1.0, V)
        for hh in range(2):
            if nfull:
                O.dma('sp', V.t[:, 0:nfull, hh, 0:128],
                      scr.V[0:nfull * 128, hh * 128:(hh + 1) * 128].rearrange("(c p) d -> p c d", p=128), V, V)
            if nkc > nfull:
                kn = NA - nfull * 128
                O.dma('sp', V.t[:kn, nfull, hh, 0:128], scr.V[nfull * 128:NA, hh * 128:(hh + 1) * 128], V, V)
        QTb = P.ring([128, 8, 512], BF16, 2)
        PT = P.ring([128, 512], BF16, 4)
        oall = P.sb([128, 4, 1024], F32)
        junk = P.sb([128, 1024], BF16)
        sm = P.ring([128, 4], F32, 8)
        abf = P.ring([128, 1024], BF16, 2)
        aTg = P.ring([128, 8, 512], BF16, 2)
        pS = P.pring([128, 512], F32, 3)
        pO = [P.ps([128, 512], F32) for _ in range(4)]
        pT = P.ps([128, 8, 128], BF16)
        scale = float(HD) ** -0.5
        LOOK = 2
        nsteps_total = len(chunks(NO, 512)) * 8 * nkc
        bgs = []
        for (mk, nunits) in (bgf or []):
            bgs.append([mk(P), max(1, (nsteps_total - 40) // nunits)])
        stepno = 0
        for (q0, qn) in chunks(NO, 512):
            Q = QTb.next()
            O.dma('sp', Q.t[:, :, :qn], scr.QT[:, :, q0:q0 + qn], (), Q)
            subs = chunks(qn, 128)
            steps = [(h, ci) for h in range(8) for ci in range(nkc)]

            def emitS(idx):
                h, ci = steps[idx]
                k0, kn = kch[ci]
                ps = pS.next()
                O.mm(ps.t[:kn, :qn], KT.t[:, h // 4, k0:k0 + kn], Q.t[:, h, :qn], True, True, (KT, Q, ps), ps)
                pt = PT.next()
                O.act(pt.t[:kn, :qn], ps.t[:kn, :qn], AF.Exp, ps, pt, scale=scale)
                return pt

            pts = {}
            for idx in range(min(LOOK, len(steps))):
                pts[idx] = emitS(idx)
            for idx in range(len(steps)):
                if idx + LOOK < len(steps):
                    pts[idx + LOOK] = emitS(idx + LOOK)
                h, ci = steps[idx]
                k0, kn = kch[ci]
                kvh = h // 4
                pt = pts.pop(idx)
                stepno += 1
                for g_, ev_ in bgs:
                    if stepno % ev_ == 0:
                        next(g_, None)
                for si, (s0, sn) in enumerate(subs):
                    O.mm(pO[si].t[:sn, 0:129], pt.t[:kn, s0:s0 + sn], V.t[:kn, ci, kvh, 0:129],
                         ci == 0, ci == nkc - 1, (pt, V, pO[si]), pO[si])
                if ci == nkc - 1:
                    for si, (s0, sn) in enumerate(subs):
                        s = sm.next()
                        O.recip(s.t[:sn, 0:1], pO[si].t[:sn, 128:129], pO[si], s)
                        O.ts('dve', oall.t[:sn, si, h * 128:(h + 1) * 128], pO[si].t[:sn, 0:128], s.t[:sn, 0:1], None,
                             ALU.mult, ALU.bypass, (pO[si], s), oall.B(si))
            G = aTg.next()
            for si, (s0, sn) in enumerate(subs):
                s = sm.next()
                O.act(junk.t[:sn, :], oall.t[:sn, si, :], AF.Square, oall.B(si), s, accum_out=s.t[:sn, 0:1])
                O.rstd(s.t[:sn, 2:3], s.t[:sn, 0:1], 1024, s.t[:sn, 1:2], s, s)
                AB = abf.next()
                O.act(AB.t[:sn, :], oall.t[:sn, si, :], AF.Copy, (oall.B(si), s), AB, scale=s.t[:sn, 2:3])
                for h in range(8):
                    O.tr(pT.t[:, h, 0:sn], AB.t[:sn, h * 128:(h + 1) * 128], ident.t[:sn, :sn], (AB, ident), pT)
                O.cp('dve', G.t[:, :, s0:s0 + sn], pT.t[:, :, 0:sn], pT, G.B(si))
            O.dma('pool', scr.attnT[:, :, q0:q0 + qn].rearrange("k p t -> p k t"), G.t[:, :, :qn],
                  [G.B(si) for si in range(len(subs))], ())
        for g_, ev_ in bgs:
            for _ in g_:
                pass
        S.flush()


def phase_M(nc, S, O, job, scr, ident, gpost):
    NO = job.N_own
    P = Ph(nc, "M" + job.name)
    with P.st:
        wo = P.sb([128, 16, D], BF16)
        O.dma('sp', wo.t[:], scr.wb_out.rearrange("k p n -> p k n"), (), wo)
        gp = P.sb([128, D], F32)
        O.dma('sp', gp.t[:], gpost[0:1, :].to_broadcast([128, D]), (), gp)
        ones2 = P.sb([128, 2], F32)
        O.memset('dve', ones2.t[:], 1.0, ones2)
        aT = P.ring([128, 8, 128], BF16, 2)
        lT = P.ring([128, 8, 128], BF16, 2)
        xt = P.ring([128, D], F32, 3)
        sqr = P.ring([128, 8, 128], F32, 2)
        Y = P.ring([128, D], F32, 3)
        hnb = P.ring([128, D], BF16, 2)
        hnTg = P.ring([128, 16, 512], BF16, 2)
        junk = P.sb([128, D], BF16)
        sm = P.ring([128, 12], F32, 4)
        pA = P.pring([128, 512], F32, 2)
        pB = P.pring([128, 512], F32, 2)
        pss = P.ps([128, 512], F32)
        pTa = P.ps([128, 8, 128], BF16)
        pTb = P.ps([128, 8, 128], BF16)
        lruT3 = scr.lruT.rearrange("(k p) t -> p k t", p=128)
        work = []
        for (g0, gn) in chunks(NO, 512):
            grp = {'g0': g0, 'gn': gn, 'nsub': len(chunks(gn, 128))}
            for j, (s0, tn) in enumerate(chunks(gn, 128)):
                work.append({'grp': grp, 'j': j, 's0': s0, 'tn': tn})

        def s1(c, inter):
            grp, j, s0, tn = c['grp'], c['j'], c['s0'], c['tn']
            t0 = grp['g0'] + s0
            A = aT.next()
            L = lT.next()
            X = xt.next()
            O.dma('sp', A.t[:, :, :tn], scr.attnT[:, :, t0:t0 + tn].rearrange("k p t -> p k t"), (), A)
            O.dma('sp', L.t[:, :, :tn], lruT3[:, :, t0:t0 + tn], (), L)
            O.dma('sp', X.t[:tn, :], job.x[t0:t0 + tn, :], (), X)
            s = sm.next()
            SQ = sqr.next()
            O.tt('pool', SQ.t[:, :, :tn], L.t[:, :, :tn], L.t[:, :, :tn], ALU.mult, L, SQ)
            for kc in range(8):
                O.mm(pss.t[:tn, 0:2], SQ.t[:, kc, :tn], ones2.t[:, 0:2], kc == 0, kc == 7, (SQ, ones2, pss), pss)
            O.rstd(s.t[:tn, 2:3], pss.t[:tn, 0:1], LRUW, s.t[:tn, 1:2], (pss, s), s)
            y = Y.next()
            for n in range(4):
                cs = slice(n * 512, (n + 1) * 512)
                a_ = pA.next()
                b_ = pB.next()
                for kc in range(8):
                    O.mm(a_.t[:tn, :], A.t[:, kc, :tn], wo.t[:, kc, cs], kc == 0, kc == 7, (A, wo, a_), a_)
                for kc in range(8):
                    O.mm(b_.t[:tn, :], L.t[:, kc, :tn], wo.t[:, 8 + kc, cs], kc == 0, kc == 7, (L, wo, b_), b_)
                O.act(y.t[:tn, cs], a_.t[:tn, :], AF.Copy, a_, y.B(n))
                O.stt(y.t[:tn, cs], b_.t[:tn, :], s.t[:tn, 2:3], y.t[:tn, cs], ALU.mult, ALU.add,
                      (b_, s, y.B(n)), y.B(n))
                if n < len(inter) and inter[n] is not None:
                    inter[n]()
            c['X'], c['s'], c['y'] = X, s, y

        def s2a(c):
            grp, j, s0, tn = c['grp'], c['j'], c['s0'], c['tn']
            t0 = grp['g0'] + s0
            X, s, y = c['X'], c['s'], c['y']
            yb = [y.B(n) for n in range(4)]
            O.act(junk.t[:tn, :], y.t[:tn, :], AF.Square, yb, s, accum_out=s.t[:tn, 4:5])
            O.rstd(s.t[:tn, 6:7], s.t[:tn, 4:5], D, s.t[:tn, 5:6], s, s)
            O.stt(y.t[:tn, :], y.t[:tn, :], s.t[:tn, 6:7], gp.t[:tn, :], ALU.mult, ALU.mult, (yb, s, gp), yb)
            O.stt(X.t[:tn, :], y.t[:tn, :], 1.0, X.t[:tn, :], ALU.mult, ALU.add, (yb, X), X)
            O.dma('pool', scr.h1[t0:t0 + tn, :], X.t[:tn, :], X, ())

        def s2b(c):
            grp, j, s0, tn = c['grp'], c['j'], c['s0'], c['tn']
            X, s = c['X'], c['s']
            O.act(junk.t[:tn, :], X.t[:tn, :], AF.Square, X, s, accum_out=s.t[:tn, 8:9])
            O.rstd(s.t[:tn, 10:11], s.t[:tn, 8:9], D, s.t[:tn, 9:10], s, s)
            HN = hnb.next()
            O.act(HN.t[:tn, :], X.t[:tn, :], AF.Copy, (X, s), HN, scale=s.t[:tn, 10:11])
            c['HN'] = HN

        def s3(c):
            grp, j, s0, tn = c['grp'], c['j'], c['s0'], c['tn']
            if j == 0:
                grp['HG'] = hnTg.next()
            HG, HN = grp['HG'], c['HN']
            for kc in range(16):
                pT = pTa if kc < 8 else pTb
                O.tr(pT.t[:, kc % 8, 0:tn], HN.t[:tn, kc * 128:(kc + 1) * 128], ident.t[:tn, :tn],
                     (HN, ident), pT)
            O.cp('dve', HG.t[:, 0:8, s0:s0 + tn], pTa.t[:, :, 0:tn], pTa, HG.B(j))
            O.cp('act', HG.t[:, 8:16, s0:s0 + tn], pTb.t[:, :, 0:tn], pTb, HG.B(j))
            if j == grp['nsub'] - 1:
                g0, gn = grp['g0'], grp['gn']
                O.dma('pool', scr.hnT[:, :, g0:g0 + gn].rearrange("k p t -> p k t"), HG.t[:, :, :gn],
                      [HG.B(i) for i in range(grp['nsub'])], ())

        nw = len(work)
        for k in range(nw + 3):
            f2b = (lambda c=work[k - 2]: s2b(c)) if 0 <= k - 2 < nw else None
            f2a = (lambda c=work[k - 1]: s2a(c)) if 0 <= k - 1 < nw else None
            if k < nw:
                s1(work[k], [f2b, f2a])
            else:
                if f2b:
                    f2b()
                if f2a:
                    f2a()
            if 0 <= k - 3 < nw:
                s3(work[k - 3])
        S.flush()


def phase_F(nc, S, O, job, scr, gpost):
    NO, NOUT = job.N_own, job.N_out
    P = Ph(nc, "F" + job.name)
    with P.st:
        gp = P.sb([128, D], F32)
        O.dma('sp', gp.t[:], gpost[1:2, :].to_broadcast([128, D]), (), gp)
        fp = P.sb([128, 176], F32)
        O.dma('sp', fp.t[:], job.ffnP[:, :], (), fp)
        hb = P.ring([128, 16, 512], BF16, 2)
        wu = P.ring([128, 16, 2, 128], BF16, 3)
        wd = P.ring([128, 4, 512], BF16, 3)
        aT = P.sb([128, NJ, 512], BF16)
        Fr = P.ring([128, D], F32, 4)
        h1t = P.ring([128, D], F32, 4)
        cr = P.ring([128, 512], F32, 4)
        slr = P.ring([128, 512], F32, 2)
        junk = P.sb([128, D], BF16)
        sm = P.ring([128, 4], F32, 4)
        pG = P.pring([128, 512], F32, 2)
        pU = P.pring([128, 512], F32, 2)
        pD = [P.ps([128, 512], F32) for _ in range(4)]
        evi = 0
        blocks = chunks(NOUT, 510)

        def loadH(bi):
            b0, bn = blocks[bi]
            H = hb.next()
            lo = max(b0 - 1, 0)
            hi = min(b0 + bn + 1, NO)
            if b0 == 0:
                O.memset('pool', H.t[:, :, 0:1], 0.0, H)
            if b0 + bn + 1 > NO:
                O.memset('pool', H.t[:, :, bn + 1:bn + 2], 0.0, H)
            O.dma('sp', H.t[:, :, lo - (b0 - 1):hi - (b0 - 1)],
                  scr.hnT[:, :, lo:hi].rearrange("k p t -> p k t"), H, H)
            return H

        Hnext = loadH(0)
        pending = []
        for bi, (b0, bn) in enumerate(blocks):
            H = Hnext
            nb = bn + 2
            for j in range(NJ):
                Wu = wu.next()
                O.dma('sp', Wu.t[:], scr.wb_up[j], (), Wu)
                G = pG.next()
                U = pU.next()
                for kc in range(16):
                    O.mm(G.t[:, :nb], Wu.t[:, kc, 0, :], H.t[:, kc, :nb], kc == 0, kc == 15, (Wu, H, G), G)
                for kc in range(16):
                    O.mm(U.t[:, :nb], Wu.t[:, kc, 1, :], H.t[:, kc, :nb], kc == 0, kc == 15, (Wu, H, U), U)
                c0 = cr.next()
                O.ts('dve', c0.t[:, :bn], G.t[:, 0:bn], fp.t[:, j * 3:j * 3 + 1], fp.t[:, 132 + j:133 + j],
                     ALU.mult, ALU.add, (G, fp), c0)
                c1 = cr.next()
                O.stt(c1.t[:, :bn], G.t[:, 1:bn + 1], fp.t[:, j * 3 + 1:j * 3 + 2], c0.t[:, :bn], ALU.mult, ALU.add,
                      (G, fp, c0), c1)
                c2 = cr.next()
                O.stt(c2.t[:, :bn], G.t[:, 2:bn + 2], fp.t[:, j * 3 + 2:j * 3 + 3], c1.t[:, :bn], ALU.mult, ALU.add,
                      (G, fp, c1), c2)
                sl = slr.next()
                O.act(sl.t[:, :bn], c2.t[:, :bn], AF.Silu, c2, sl)
                O.tt('dve', aT.t[:, j, :bn], sl.t[:, :bn], U.t[:, 1:bn + 1], ALU.mult, (sl, U), aT.B(j))
                if pending and j % 3 == 2:
                    pending.pop(0)()
            while pending:
                pending.pop(0)()
            if bi + 1 < len(blocks):
                Hnext = loadH(bi + 1)
            subs = chunks(bn, 128)
            Fs = [Fr.next() for _ in subs]
            H1s = []
            for si, (s0, sn) in enumerate(subs):
                H1 = h1t.next()
                O.dma('sp', H1.t[:sn, :], scr.h1[b0 + s0:b0 + s0 + sn, :], (), H1)
                H1s.append(H1)
            for n in range(4):
                cs = slice(n * 512, (n + 1) * 512)
                for q in range(11):
                    Wd = wd.next()
                    O.dma('sp', Wd.t[:], scr.wb_down[q * 4:(q + 1) * 4, :, cs].rearrange("k p c -> p k c"), (), Wd)
                    for kk in range(4):
                        kc = q * 4 + kk
                        for si, (s0, sn) in enumerate(subs):
                            O.mm(pD[si].t[:sn, :], aT.t[:, kc, s0:s0 + sn], Wd.t[:, kk, :], kc == 0, kc == NJ - 1,
                                 (aT.B(kc), Wd, pD[si]), pD[si])
                for si, (s0, sn) in enumerate(subs):
                    evi += 1
                    O.cp('act' if evi % 2 else 'dve', Fs[si].t[:sn, cs], pD[si].t[:sn, :], pD[si], Fs[si].B(n))
            def chain(si, s0, sn, Ft, H1, b0=b0):
                fb = [Ft.B(n) for n in range(4)]
                s = sm.next()
                O.act(junk.t[:sn, :], Ft.t[:sn, :], AF.Square, fb, s, accum_out=s.t[:sn, 0:1])
                O.rstd(s.t[:sn, 2:3], s.t[:sn, 0:1], D, s.t[:sn, 1:2], s, s)
                O.stt(Ft.t[:sn, :], Ft.t[:sn, :], s.t[:sn, 2:3], gp.t[:sn, :], ALU.mult, ALU.mult, (fb, s, gp), fb)
                O.stt(Ft.t[:sn, :], Ft.t[:sn, :], 1.0, H1.t[:sn, :], ALU.mult, ALU.add, (fb, H1), fb)
                O.dma('pool', job.out[b0 + s0:b0 + s0 + sn, :], Ft.t[:sn, :], fb, ())

            pending = [(lambda si=si, s0=s0, sn=sn, Ft=Fs[si], H1=H1s[si]: chain(si, s0, sn, Ft, H1))
                       for si, (s0, sn) in enumerate(subs)]
        for f in pending:
            f()
        S.flush()


def _rope_tables(n_tokens):
    rows = n_tokens // 64
    t_row = np.repeat(np.arange(rows), 64).astype(np.float32)
    t_col = np.tile(np.arange(64), rows).astype(np.float32)
    inv = (np.float32(10000.0) ** (-(np.arange(0, 64, 2, dtype=np.float32) / np.float32(64)))).astype(np.float32)
    ang = np.concatenate([t_row[:, None] * inv, t_col[:, None] * inv], axis=-1)
    ang = np.concatenate([np.zeros((NMETA, 64), np.float32), ang], axis=0).astype(np.float32)
    c = np.cos(ang).astype(np.float32).reshape(-1, 2, 32)
    s = np.sin(ang).astype(np.float32).reshape(-1, 2, 32)
    C = np.stack([c, c], axis=2).reshape(-1, 128)
    Sg = np.stack([s, -s], axis=2).reshape(-1, 128)
    return np.ascontiguousarray(C), np.ascontiguousarray(Sg)


def _pcol(v, k):
    return np.ascontiguousarray(np.asarray(v, np.float32).reshape(k, 128).T)


def _lru_pack(conv_w, conv_b, b_r, b_i, lam, flip):
    z = np.zeros((1, LRUW), np.float32)
    w5 = np.concatenate([conv_w, z], axis=0)
    if flip:
        w5 = w5[::-1]
        b_r, b_i, lam = b_r[::-1], b_i[::-1], lam[::-1]
    out = np.zeros((128, 96), np.float32)
    out[:, 0:40] = np.stack([_pcol(w5[k], 8) for k in range(5)], axis=-1).reshape(128, 40)
    out[:, 40:48] = _pcol(conv_b, 8)
    out[:, 48:64] = np.concatenate([_pcol(b_r[d], 8) for d in range(2)], axis=1)
    out[:, 64:80] = np.concatenate([_pcol(b_i[d], 8) for d in range(2)], axis=1)
    out[:, 80:96] = np.concatenate([_pcol(lam[d], 8) for d in range(2)], axis=1)
    return out


def _gates_pack(w_r, w_i, flip):
    if flip:
        w_r, w_i = w_r[::-1], w_i[::-1]
    return np.ascontiguousarray(np.stack([w_r, w_i], axis=0).reshape(32 * 128, 128).astype(np.float32))


def _ffn_pack(conv_w, conv_b, flip):
    if flip:
        conv_w = conv_w[::-1]
    out = np.zeros((128, 176), np.float32)
    out[:, 0:132] = np.stack([_pcol(conv_w[k], NJ) for k in range(3)], axis=-1).reshape(128, 132)
    out[:, 132:176] = _pcol(conv_b, NJ)
    return out


_PROG_CACHE = {}


def kernel(x_prompt, x_sample, meta_tokens, norm_pre_mix, w_in, q_norm, k_norm, lru_conv_w, lru_conv_b,
           lru_w_r, lru_b_r, lru_w_i, lru_b_i, lru_lambda, attn_out_norm, lru_out_norm, w_out,
           norm_post_mix, norm_pre_ffn, w_up, ffn_conv_w, ffn_conv_b, w_down, norm_post_ffn):
    f = lambda a: np.asarray(a, dtype=np.float32)
    x_prompt, x_sample, meta = f(x_prompt), f(x_sample), f(meta_tokens)
    Bp, Sp, _ = x_prompt.shape
    Bs, Ss, _ = x_sample.shape
    ncores = 8
    assert Bp == ncores and Bs * 2 == ncores
    LA = Sp + NMETA
    LB = Ss + NMETA
    assert LB % 2 == 0
    NBh = LB // 2
    NB_own = NBh + 1
    key = (LA, LB)
    if key not in _PROG_CACHE:
        _PROG_CACHE[key] = build_program(LA, LB, NB_own, NBh)
    nc = _PROG_CACHE[key]

    CA, SA = _rope_tables(Sp)
    CB, SB = _rope_tables(Ss)
    gcols = np.concatenate([_pcol(f(norm_pre_mix)[0], 16),
                            _pcol(np.concatenate([f(attn_out_norm)[0], f(lru_out_norm)[0]]), 16),
                            _pcol(f(norm_pre_ffn)[0], 16)], axis=1)
    gpost = np.ascontiguousarray(np.stack([f(norm_post_mix)[0], f(norm_post_ffn)[0]], axis=0))
    qkn = np.ascontiguousarray(np.stack([f(q_norm)[0], f(k_norm)[0]], axis=0))
    lp = [_lru_pack(f(lru_conv_w)[0], f(lru_conv_b)[0], f(lru_b_r)[0], f(lru_b_i)[0], f(lru_lambda)[0], fl)
          for fl in (False, True)]
    gp = [_gates_pack(f(lru_w_r)[0], f(lru_w_i)[0], fl) for fl in (False, True)]
    fpk = [_ffn_pack(f(ffn_conv_w)[0], f(ffn_conv_b)[0], fl) for fl in (False, True)]
    shared = {"w_in": np.ascontiguousarray(f(w_in)[0]), "w_out": np.ascontiguousarray(f(w_out)[0]),
              "w_up": np.ascontiguousarray(f(w_up)[0]), "w_down": np.ascontiguousarray(f(w_down)[0]),
              "gcols": gcols, "gpost": gpost, "qkn": qkn}
    in_maps = []
    for c in range(ncores):
        s, half = c // 2, c % 2
        xa = np.concatenate([meta, x_prompt[c]], axis=0)
        xb = np.concatenate([meta, x_sample[s]], axis=0)
        cb, sb = CB, SB
        if half:
            xb, cb, sb = xb[::-1], CB[::-1], SB[::-1]
        m = dict(shared)
        m.update({"xA": np.ascontiguousarray(xa), "cosA": CA, "sinA": SA, "lruPA": lp[0], "gatesA": gp[0],
                  "ffnPA": fpk[0],
                  "xB": np.ascontiguousarray(xb), "cosB": np.ascontiguousarray(cb), "sinB": np.ascontiguousarray(sb),
                  "lruPB": lp[half], "gatesB": gp[half], "ffnPB": fpk[half]})
        in_maps.append(m)
    res = run_bass_kernel_spmd(nc, in_maps, core_ids=list(range(ncores)))
    y_prompt = np.empty((Bp, Sp, D), np.float32)
    y_sample = np.empty((Bs, Ss, D), np.float32)
    for c in range(ncores):
        r = res.results[c]
        s, half = c // 2, c % 2
        y_prompt[c] = r["yA"][NMETA:]
        yb = r["yB"]
        if half == 0:
            y_sample[s, 0:NBh - NMETA] = yb[NMETA:]
        else:
            y_sample[s, NBh - NMETA:] = yb[::-1]
    return (y_prompt, y_sample)
```

```python
import contextlib
import numpy as np
import concourse.bass as bass
import concourse.mybir as mybir
from concourse.ap import AP
from concourse.bass_utils import run_bass_kernel_spmd

F32 = mybir.dt.float32
BF16 = mybir.dt.bfloat16
AF = mybir.ActivationFunctionType
ALU = mybir.AluOpType

D = 2048
HD = 128
NQH = 8
NKVH = 2
LRUW = 1024
FFN = 5632
NJ = FFN // 128
NMETA = 16
EPS = 1e-6
ENGS = ('pe', 'act', 'dve', 'pool', 'sp')
NDSEM = 8


class Buf:
    __slots__ = ('w', 'r')

    def __init__(self):
        self.w = None
        self.r = {}


class T:
    def __init__(self, t):
        self.t = t
        self.b = Buf()
        self.sub = {}

    def B(self, j):
        if j not in self.sub:
            self.sub[j] = Buf()
        return self.sub[j]


class Ring:
    def __init__(self, tiles):
        self.tiles = tiles
        self.i = 0

    def next(self):
        t = self.tiles[self.i % len(self.tiles)]
        self.i += 1
        return t


class Sched:
    def __init__(self, nc, sems, dsems):
        self.nc = nc
        self.sems = sems
        self.dsems = dsems
        self.cnt = {e: 0 for e in ENGS}
        self.dcnt = {e: 0 for e in ENGS}
        self.lastdma = {}
        self.seen = {e: {} for e in ENGS}
        self.ops = []
        self.q = {e: [] for e in ENGS}
        self.total_ops = 0
        self.total_waits = 0
        self.epoch = 0

    def op(self, eng, fn, reads=(), writes=(), dma=False):
        deps = set()
        ep = self.epoch
        for b in reads:
            if b.w is not None and b.w[0] == ep:
                deps.add(b.w[1])
        for b in writes:
            if b.w is not None and b.w[0] == ep:
                deps.add(b.w[1])
            for (e2, i2) in b.r.values():
                if e2 == ep:
                    deps.add(i2)
        i = len(self.ops)
        rk = ('dma', i) if dma else eng
        for b in reads:
            b.r[rk] = (ep, i)
        for b in writes:
            b.w = (ep, i)
            b.r = {}
        deps.discard(i)
        self.ops.append((eng, fn, deps, dma))
        self.q[eng].append(i)
        return i

    def _semh(self, key):
        if key[0] == 'c':
            return self.sems[key[1]]
        return self.dsems[key[1]][key[2]]

    def flush(self, final=False):
        nc = self.nc
        ops = self.ops
        n = len(ops)
        signal = [False] * n
        for i, (eng, fn, deps, dma) in enumerate(ops):
            for d in deps:
                de, _, _, ddma = ops[d]
                if ddma:
                    continue
                if de != eng or dma or eng != 'pe':
                    signal[d] = True
        for e in ENGS:
            for i in reversed(self.q[e]):
                if not ops[i][3]:
                    signal[i] = True
                    break
        bar = [(('c', e), self.cnt[e]) for e in ENGS if self.cnt[e] > 0]
        bar += list(self.lastdma.values())
        ev = [None] * n
        prevdma = [None] * n
        for i, (eng, fn, deps, dma) in enumerate(ops):
            if dma:
                k = self.dcnt[eng]
                self.dcnt[eng] += 1
                slot = k % NDSEM
                prevdma[i] = self.lastdma.get((eng, slot))
                ev[i] = (('d', eng, slot), (k // NDSEM + 1) * 16)
                self.lastdma[(eng, slot)] = ev[i]
            elif signal[i]:
                self.cnt[eng] += 1
                ev[i] = (('c', eng), self.cnt[eng])
        engobj = {'pe': nc.tensor, 'act': nc.scalar, 'dve': nc.vector, 'pool': nc.gpsimd, 'sp': nc.sync}
        endbar = None
        if final:
            endbar = [(('c', e), self.cnt[e]) for e in ENGS if self.cnt[e] > 0] + list(self.lastdma.values())

        def run_engine(eng):
            e = engobj[eng]
            seen = self.seen[eng]

            def wait(k, v):
                if seen.get(k, 0) >= v:
                    return
                seen[k] = v
                e.wait_ge(self._semh(k), v)
                self.total_waits += 1

            if self.q[eng]:
                for k, v in bar:
                    if k == ('c', eng):
                        continue
                    wait(k, v)
            for i in self.q[eng]:
                _, fn, deps, dma = ops[i]
                need = {}
                for d in deps:
                    de, _, _, ddma = ops[d]
                    if (not ddma) and de == eng and (not dma) and eng == 'pe':
                        continue
                    k, v = ev[d]
                    if need.get(k, 0) < v:
                        need[k] = v
                if dma and prevdma[i] is not None:
                    k, v = prevdma[i]
                    if need.get(k, 0) < v:
                        need[k] = v
                for k, v in need.items():
                    wait(k, v)
                ins = fn(e)
                if ev[i] is not None:
                    ins.then_inc(self._semh(ev[i][0]), 16 if dma else 1)
            if final and eng == 'sp':
                for k, v in endbar:
                    wait(k, v)

        with nc.Block() as block:
            @block.tensor
            def _(_e):
                run_engine('pe')

            @block.scalar
            def _(_e):
                run_engine('act')

            @block.vector
            def _(_e):
                run_engine('dve')

            @block.gpsimd
            def _(_e):
                run_engine('pool')

            @block.sync
            def _(_e):
                run_engine('sp')
        self.total_ops += n
        self.epoch += 1
        self.ops = []
        self.q = {e: [] for e in ENGS}


class Ph:
    def __init__(self, nc, name):
        self.nc = nc
        self.name = name
        self.st = contextlib.ExitStack()
        self.k = 0

    def sb(self, shape, dt):
        self.k += 1
        return T(self.st.enter_context(self.nc.sbuf_tensor(f"{self.name}_s{self.k}", list(shape), dt)))

    def ps(self, shape, dt):
        self.k += 1
        return T(self.st.enter_context(self.nc.psum_tensor(f"{self.name}_p{self.k}", list(shape), dt)))

    def ring(self, shape, dt, n):
        return Ring([self.sb(shape, dt) for _ in range(n)])

    def pring(self, shape, dt, n):
        return Ring([self.ps(shape, dt) for _ in range(n)])


def bl(*xs):
    out = []
    for x in xs:
        if x is None:
            continue
        if isinstance(x, Buf):
            out.append(x)
        elif isinstance(x, T):
            out.append(x.b)
        else:
            out.extend(bl(*x))
    return out


class Ops:
    def __init__(self, S):
        self.S = S

    def act(self, out, in_, func, r, w, **kw):
        self.S.op('act', lambda e: e.activation(out=out, in_=in_, func=func, **kw), bl(r), bl(w))

    def ts(self, eng, out, in0, s1, s2, op0, op1, r, w):
        self.S.op(eng, lambda e: e.tensor_scalar(out=out, in0=in0, scalar1=s1, scalar2=s2, op0=op0, op1=op1),
                  bl(r), bl(w))

    def stt(self, out, in0, sc, in1, op0, op1, r, w):
        self.S.op('dve', lambda e: e.scalar_tensor_tensor(out=out, in0=in0, scalar=sc, in1=in1, op0=op0, op1=op1),
                  bl(r), bl(w))

    def tt(self, eng, out, in0, in1, op, r, w):
        self.S.op(eng, lambda e: e.tensor_tensor(out=out, in0=in0, in1=in1, op=op), bl(r), bl(w))

    def cp(self, eng, out, in_, r, w):
        if eng == 'act':
            self.S.op('act', lambda e: e.activation(out=out, in_=in_, func=AF.Copy), bl(r), bl(w))
        else:
            self.S.op(eng, lambda e: e.tensor_copy(out=out, in_=in_), bl(r), bl(w))

    def recip(self, out, in_, r, w):
        self.S.op('dve', lambda e: e.reciprocal(out=out, in_=in_), bl(r), bl(w))

    def memset(self, eng, ap, val, w):
        self.S.op(eng, lambda e: e.memset(ap, val), (), bl(w))

    def mm(self, out, lhsT, rhs, start, stop, r, w):
        self.S.op('pe', lambda e: e.matmul(out, lhsT=lhsT, rhs=rhs, start=start, stop=stop), bl(r), bl(w))

    def tr(self, out, in_, ident, r, w):
        self.S.op('pe', lambda e: e.transpose(out=out, in_=in_, identity=ident), bl(r), bl(w))

    def scan(self, out, d0, d1, init, r, w):
        self.S.op('dve', lambda e: e.tensor_tensor_scan(out=out, data0=d0, data1=d1, initial=init,
                                                        op0=ALU.mult, op1=ALU.add), bl(r), bl(w))

    def dma(self, eng, out, in_, r, w):
        self.S.op(eng, lambda e: e.dma_start(out=out, in_=in_), bl(r), bl(w), dma=True)

    def rstd(self, out, ss, n, tmp, r, w):
        self.act(tmp, ss, AF.Ln, r, w, scale=1.0 / n, bias=EPS)
        self.act(out, tmp, AF.Exp, w, w, scale=-0.5)


def rev(v):
    (ps, pn), (fs, fn) = v.ap
    return AP(v.tensor, v.offset + (fn - 1) * fs, [[ps, pn], [-fs, fn]])


def swap_halves(v, nh):
    (ps, pn), (fs, fn) = v.ap
    assert fs == 1 and fn == nh * 128
    return AP(v.tensor, v.offset + 32, [[ps, pn], [64, 2 * nh], [-32, 2], [1, 32]])


def chunks(n, c):
    return [(i, min(c, n - i)) for i in range(0, n, c)]


def rup(n, c):
    return (n + c - 1) // c * c


def balanced(n, c):
    nb = (n + c - 1) // c
    base, rem = divmod(n, nb)
    out, o = [], 0
    for i in range(nb):
        sz = base + (1 if i < rem else 0)
        out.append((o, sz))
        o += sz
    return out


def tail_balanced(n, c):
    ch = chunks(n, c)
    if len(ch) >= 2 and ch[-1][1] < c // 2:
        o = ch[-2][0]
        tot = ch[-2][1] + ch[-1][1]
        a = (tot + 1) // 2
        ch = ch[:-2] + [(o, a), (o + a, tot - a)]
    return ch


class Job:
    pass


def build_program(NA_A, NB_all, NB_own, NB_out):
    nc = bass.Bass("TRN2", target_bir_lowering=False)

    def din(name, shape, dt=F32):
        return nc.dram_tensor(name, list(shape), dt, kind="ExternalInput").ap()

    def dout(name, shape, dt=F32):
        return nc.dram_tensor(name, list(shape), dt, kind="ExternalOutput").ap()

    def dscr(name, shape, dt):
        return nc.dram_tensor(name, list(shape), dt, kind="Internal").ap()

    NAmax = max(NA_A, NB_all)
    NOmax = max(NA_A, NB_own)
    NAp = rup(NAmax, 1024)
    NOp = rup(NOmax, 1024)

    w_in = din("w_in", [D, 3584])
    w_out = din("w_out", [D, D])
    w_up = din("w_up", [D, 2 * FFN])
    w_down = din("w_down", [FFN, D])
    gcols = din("gcols", [128, 48])
    gpost = din("gpost", [2, D])
    qkn = din("qkn", [2, 128])

    jobs = []
    for nm, N_all, N_own, N_out in (("A", NA_A, NA_A, NA_A), ("B", NB_all, NB_own, NB_out)):
        j = Job()
        j.name = nm
        j.N_all, j.N_own, j.N_out = N_all, N_own, N_out
        j.x = din(f"x{nm}", [N_all, D])
        j.cos = din(f"cos{nm}", [N_all, 128])
        j.sin = din(f"sin{nm}", [N_all, 128])
        j.lruP = din(f"lruP{nm}", [128, 96])
        j.gates = din(f"gates{nm}", [32 * 128, 128])
        j.ffnP = din(f"ffnP{nm}", [128, 176])
        j.out = dout(f"y{nm}", [N_out, D])
        jobs.append(j)

    class Scr:
        pass
    scr = Scr()
    scr.wb_in = dscr("wb_in", [16, 128, 3584], BF16)
    scr.wb_out = dscr("wb_out", [16, 128, D], BF16)
    scr.wb_up = dscr("wb_up", [NJ, 128, 16, 2, 128], BF16)
    scr.wb_down = dscr("wb_down", [NJ, 128, D], BF16)
    scr.xnT = dscr("s_xnT", [16, 128, NAp], BF16)
    scr.QT = dscr("s_QT", [128, 8, NOp], BF16)
    scr.KT = dscr("s_KT", [128, 2, NAp], BF16)
    scr.V = dscr("s_V", [NAp, 256], BF16)
    scr.xrT = dscr("s_xrT", [LRUW, NAp], F32)
    scr.gyT = dscr("s_gyT", [LRUW, NAp], F32)
    scr.xhT = dscr("s_xhT", [LRUW, NAp], F32)
    scr.xcbT = dscr("s_xcbT", [LRUW, NAp], BF16)
    scr.lruT = dscr("s_lruT", [LRUW, NAp], BF16)
    scr.attnT = dscr("s_attnT", [8, 128, NOp], BF16)
    scr.h1 = dscr("s_h1", [NOp, D], F32)
    scr.hnT = dscr("s_hnT", [16, 128, NOp], BF16)

    with contextlib.ExitStack() as top:
        sems = {e: top.enter_context(nc.semaphore(f"sem_{e}")) for e in ENGS}
        dsems = {e: [top.enter_context(nc.semaphore(f"dsem_{e}{i}")) for i in range(NDSEM)]
                 for e in ('sp', 'pool', 'act')}
        S = Sched(nc, sems, dsems)
        O = Ops(S)
        identf = T(top.enter_context(nc.sbuf_tensor("identf", [128, 128], F32)))
        ident = T(top.enter_context(nc.sbuf_tensor("ident", [128, 128], BF16)))
        O.memset('pool', identf.t[:], 0.0, identf)
        S.op('pool', lambda e: e.affine_select(out=identf.t[:], in_=identf.t[:], pattern=[[-1, 128]],
                                               compare_op=ALU.not_equal, fill=1.0, base=0, channel_multiplier=1),
             bl(identf), bl(identf))
        O.cp('dve', ident.t[:], identf.t[:], identf, ident)

        gt = T(top.enter_context(nc.sbuf_tensor("gcols_sb", [128, 48], F32)))
        S.flush()
        phase_W(nc, S, O, w_in, gcols, scr, gt)
        for ji, j in enumerate(jobs):
            phase_A1(nc, S, O, j, scr, ident, qkn)
            phase_A2(nc, S, O, j, scr)
            nconv = 8 * len(chunks(j.N_all, 2048))
            bg = [((lambda P, j=j: gen_conv(nc, S, O, P, j, scr)), nconv)]
            if ji == 0:
                bg.append(((lambda P: gen_W2(nc, S, O, P, w_out, w_up, w_down, scr, gt)), 124))
            phase_T(nc, S, O, j, scr, ident, bg)
            phase_L(nc, S, O, j, scr, identf)
            phase_M(nc, S, O, j, scr, ident, gpost)
            phase_F(nc, S, O, j, scr, gpost)
        S.flush(final=True)
    nc._sched_stats = (S.total_ops, S.total_waits, dict(S.cnt))
    return nc


def phase_W(nc, S, O, w_in, gcols, scr, gt):
    P = Ph(nc, "W")
    with P.st:
        O.dma('sp', gt.t[:], gcols[:, :], (), gt)
        wf = P.ring([128, 1792], F32, 3)
        wb = P.ring([128, 1792], BF16, 3)
        for kc in range(16):
            rows = slice(kc * 128, (kc + 1) * 128)
            for c0 in (0, 1792):
                a = wf.next()
                b = wb.next()
                O.dma('sp', a.t[:, :], w_in[rows, c0:c0 + 1792], (), a)
                if c0 == 0:
                    O.act(b.t[:, :], a.t[:, :], AF.Copy, (a, gt), b, scale=gt.t[:, kc:kc + 1])
                else:
                    O.ts('dve', b.t[:, :], a.t[:, :], gt.t[:, kc:kc + 1], None, ALU.mult, ALU.bypass, (a, gt), b)
                O.dma('pool', scr.wb_in[kc, :, c0:c0 + 1792], b.t[:, :], b, ())
        S.flush()


def gen_W2(nc, S, O, P, w_out, w_up, w_down, scr, gt):
    CW = 2816
    wf = P.ring([128, CW], F32, 3)
    wb = P.ring([128, CW], BF16, 3)

    def unit(src_ap, cw, scale, dst_ap, dst_view=None):
        a = wf.next()
        b = wb.next()
        O.dma('sp', a.t[:, :cw], src_ap, (), a)
        if scale is None:
            O.cp('dve', b.t[:, :cw], a.t[:, :cw], a, b)
        else:
            O.ts('dve', b.t[:, :cw], a.t[:, :cw], scale, None, ALU.mult, ALU.bypass, (a, gt), b)
        O.dma('pool', dst_ap, b.t[:, :cw] if dst_view is None else dst_view(b), b, ())

    for kc in range(16):
        rows = slice(kc * 128, (kc + 1) * 128)
        unit(w_out[rows, :], D, gt.t[:, 16 + kc:17 + kc], scr.wb_out[kc, :, :])
        yield
    for kc in range(16):
        rows = slice(kc * 128, (kc + 1) * 128)
        for half in range(2):
            for jh in range(2):
                c0 = half * FFN + jh * CW
                dst = scr.wb_up[jh * 22:(jh + 1) * 22, :, kc, half, :].rearrange("j p n -> p j n")
                unit(w_up[rows, c0:c0 + CW], CW, gt.t[:, 32 + kc:33 + kc], dst,
                     lambda b: b.t[:, :CW].rearrange("p (j n) -> p j n", n=128))
                yield
    for kc in range(NJ):
        unit(w_down[kc * 128:(kc + 1) * 128, :], D, None, scr.wb_down[kc, :, :])
        yield


def gen_conv(nc, S, O, P, job, scr):
    NA = job.N_all
    prm = P.sb([128, 96], F32)
    O.dma('sp', prm.t[:], job.lruP[:, :], (), prm)
    hw = P.sb([128, 48], F32)
    O.ts('dve', hw.t[:, :], prm.t[:, 0:48], 0.5, None, ALU.mult, ALU.bypass, prm, hw)
    xrb = P.ring([128, 2052], F32, 2)
    acc = P.ring([128, 2048], F32, 2)
    cbr = P.ring([128, 2048], BF16, 2)
    for c in range(8):
        rows = slice(c * 128, (c + 1) * 128)
        for (b0, bn) in chunks(NA, 2048):
            XR = xrb.next()
            lo = max(0, b0 - 2)
            hi = min(NA, b0 + bn + 2)
            if b0 - 2 < 0:
                O.memset('dve', XR.t[:, 0:2], 0.0, XR)
            if b0 + bn + 2 > NA:
                O.memset('dve', XR.t[:, NA - (b0 - 2):bn + 4], 0.0, XR)
            O.dma('sp', XR.t[:, lo - (b0 - 2):hi - (b0 - 2)], scr.xrT[rows, lo:hi], XR, XR)
            A = acc.next()
            O.ts('dve', A.t[:, :bn], XR.t[:, 0:bn], hw.t[:, c * 5:c * 5 + 1], hw.t[:, 40 + c:41 + c],
                 ALU.mult, ALU.add, (XR, hw), A)
            for k in range(1, 5):
                A2 = acc.next()
                O.stt(A2.t[:, :bn], XR.t[:, k:k + bn], hw.t[:, c * 5 + k:c * 5 + k + 1], A.t[:, :bn],
                      ALU.mult, ALU.add, (XR, hw, A), A2)
                A = A2
            O.dma('pool', scr.xhT[rows, b0:b0 + bn], A.t[:, :bn], A, ())
            CB = cbr.next()
            O.ts('dve', CB.t[:, :bn], A.t[:, :bn], 2.0, None, ALU.mult, ALU.bypass, A, CB)
            O.dma('pool', scr.xcbT[rows, b0:b0 + bn], CB.t[:, :bn], CB, ())
            yield


def phase_A1(nc, S, O, job, scr, ident, qkn):
    NA, NO = job.N_all, job.N_own
    P = Ph(nc, "A1" + job.name)
    with P.st:
        w = P.sb([128, 16, 1536], BF16)
        for kc in range(16):
            O.dma('sp', w.t[:, kc, :], scr.wb_in[kc, :, 0:1536], (), w.B(kc))
        gqk = P.sb([128, 2, 128], F32)
        O.dma('sp', gqk.t[:, 0, :], qkn[0:1, :].to_broadcast([128, 128]), (), gqk)
        O.dma('sp', gqk.t[:, 1, :], qkn[1:2, :].to_broadcast([128, 128]), gqk, gqk)
        xt = P.ring([128, D], F32, 2)
        junk = P.sb([128, D], BF16)
        xn = P.ring([128, D], BF16, 3)
        xnT = P.ring([128, 16, 512], BF16, 2)
        cst = P.ring([128, 2, 128], F32, 8)
        csg = P.ring([128, 4, 128], F32, 2)
        sm = P.ring([128, 4], F32, 4)
        ssq = P.ring([128, 32], F32, 2)
        sqt = P.sb([128, 1280], F32)
        qn = P.sb([128, 1280], F32)
        t1 = P.sb([128, 1280], F32)
        t2 = P.sb([128, 1280], F32)
        qr = P.ring([128, 1280], BF16, 2)
        qT = P.ring([128, 8, 512], BF16, 2)
        kT = P.ring([128, 2, 512], BF16, 2)
        vsb = P.ring([128, 256], BF16, 2)
        pTa = P.ps([128, 8, 128], BF16)
        pTb = P.ps([128, 8, 128], BF16)
        pq = P.ps([128, 1024], F32)
        pkv = P.ps([128, 512], F32)
        pqT = P.ps([128, 8, 128], BF16)
        pkT = P.ps([128, 8, 128], BF16)

        zq = P.ring([128, 1024], F32, 3)
        zkv = P.ring([128, 512], F32, 3)
        work = []
        for (g0, gn) in chunks(NA, 512):
            grp = {'g0': g0, 'gn': gn, 'doq': g0 < NO, 'nsub': len(chunks(gn, 128))}
            for j, (s0, tn) in enumerate(chunks(gn, 128)):
                work.append({'grp': grp, 'j': j, 's0': s0, 'tn': tn})

        def st0(c):
            grp, j, s0, tn = c['grp'], c['j'], c['s0'], c['tn']
            t0 = grp['g0'] + s0
            X = xt.next()
            O.dma('sp', X.t[:tn, :], job.x[t0:t0 + tn, :], (), X)
            CS = cst.next()
            O.dma('sp', CS.t[:tn, 0, :], job.cos[t0:t0 + tn, :], (), CS)
            O.dma('sp', CS.t[:tn, 1, :], job.sin[t0:t0 + tn, :], CS, CS)
            s = sm.next()
            O.act(junk.t[:tn, :], X.t[:tn, :], AF.Square, X, s, accum_out=s.t[:tn, 0:1])
            O.rstd(s.t[:tn, 2:3], s.t[:tn, 0:1], D, s.t[:tn, 1:2], s, s)
            XN = xn.next()
            O.act(XN.t[:tn, :], X.t[:tn, :], AF.Copy, (X, s), XN, scale=s.t[:tn, 2:3])
            c['XN'], c['CS'] = XN, CS

        def st0b(c):
            grp, j, s0, tn = c['grp'], c['j'], c['s0'], c['tn']
            if j == 0:
                grp['XT'] = xnT.next()
            XT, XN = grp['XT'], c['XN']
            for kc in range(16):
                pT = pTa if kc < 8 else pTb
                O.tr(pT.t[:, kc % 8, 0:tn], XN.t[:tn, kc * 128:(kc + 1) * 128], ident.t[:tn, :tn],
                     (XN, ident), pT)
            O.cp('dve', XT.t[:, 0:8, s0:s0 + tn], pTa.t[:, :, 0:tn], pTa, XT.B(j))
            O.cp('act', XT.t[:, 8:16, s0:s0 + tn], pTb.t[:, :, 0:tn], pTb, XT.B(j))

        def st1(c):
            grp, j, s0, tn = c['grp'], c['j'], c['s0'], c['tn']
            XT, doq = grp['XT'], grp['doq']
            if doq:
                for n in range(2):
                    for kc in range(16):
                        O.mm(pq.t[:tn, n * 512:(n + 1) * 512], XT.t[:, kc, s0:s0 + tn],
                             w.t[:, kc, n * 512:(n + 1) * 512], kc == 0, kc == 15,
                             (XT.B(j), w.B(kc), pq.B(n)), pq.B(n))
            for kc in range(16):
                O.mm(pkv.t[:tn, :], XT.t[:, kc, s0:s0 + tn], w.t[:, kc, 1024:1536], kc == 0, kc == 15,
                     (XT.B(j), w.B(kc), pkv), pkv)
            if doq:
                ZQ = zq.next()
                O.cp('act', ZQ.t[:tn, :], pq.t[:tn, :], (pq.B(0), pq.B(1)), ZQ)
                c['ZQ'] = ZQ
            ZKV = zkv.next()
            O.cp('dve', ZKV.t[:tn, :], pkv.t[:tn, :], pkv, ZKV)
            c['ZKV'] = ZKV
            if j == grp['nsub'] - 1:
                g0, gn = grp['g0'], grp['gn']
                O.dma('pool', scr.xnT[:, :, g0:g0 + gn].rearrange("k p t -> p k t"), XT.t[:, :, :gn],
                      [XT.B(i) for i in range(grp['nsub'])], ())

        def st2a(c):
            grp, j, s0, tn = c['grp'], c['j'], c['s0'], c['tn']
            doq = grp['doq']
            ZKV, CS = c['ZKV'], c['CS']
            G = csg.next()
            O.tt('dve', G.t[:tn, 0:2, :], CS.t[:tn, :, :], gqk.t[:tn, 0:1, :].to_broadcast([tn, 2, 128]),
                 ALU.mult, (CS, gqk), G)
            O.tt('dve', G.t[:tn, 2:4, :], CS.t[:tn, :, :], gqk.t[:tn, 1:2, :].to_broadcast([tn, 2, 128]),
                 ALU.mult, (CS, gqk, G), G)
            QR = qr.next()
            sq = ssq.next()
            c['QR'] = QR

            def normrope(src, nh, c0, col0, srcbufs):
                O.tt('dve', sqt.t[:tn, col0:col0 + nh * 128], src, src, ALU.mult, srcbufs, sqt.B(c0))
                S.op('dve', lambda e: e.tensor_reduce(
                    out=sq.t[:tn, c0:c0 + nh],
                    in_=sqt.t[:tn, col0:col0 + nh * 128].rearrange("p (h d) -> p h d", d=128),
                    axis=mybir.AxisListType.X, op=ALU.add), bl(sqt.B(c0)), bl(sq))
                O.rstd(sq.t[:tn, 20 + c0:20 + c0 + nh], sq.t[:tn, c0:c0 + nh], 128,
                       sq.t[:tn, 10 + c0:10 + c0 + nh], sq, sq)
                cols = slice(col0, col0 + nh * 128)
                O.tt('dve', qn.t[:tn, cols].rearrange("p (h d) -> p h d", d=128),
                     src.rearrange("p (h d) -> p h d", d=128),
                     sq.t[:tn, 20 + c0:20 + c0 + nh].unsqueeze(2).to_broadcast([tn, nh, 128]),
                     ALU.mult, (srcbufs, sq), qn.B(c0))
                ti = 0 if c0 == 0 else 2
                O.tt('dve', t1.t[:tn, cols].rearrange("p (h d) -> p h d", d=128),
                     qn.t[:tn, cols].rearrange("p (h d) -> p h d", d=128),
                     G.t[:tn, ti:ti + 1, :].to_broadcast([tn, nh, 128]), ALU.mult, (qn.B(c0), G), t1.B(c0))
                O.tt('dve', t2.t[:tn, cols].rearrange("p (h d) -> p h d", d=128),
                     qn.t[:tn, cols].rearrange("p (h d) -> p h d", d=128),
                     G.t[:tn, ti + 1:ti + 2, :].to_broadcast([tn, nh, 128]), ALU.mult, (qn.B(c0), G),
                     t2.B(c0))
                O.tt('dve', QR.t[:tn, cols].rearrange("p (a h j) -> p a h j", h=2, j=32),
                     t1.t[:tn, cols].rearrange("p (a h j) -> p a h j", h=2, j=32),
                     swap_halves(t2.t[:tn, cols], nh), ALU.add, (t1.B(c0), t2.B(c0)), QR.B(c0))

            if doq:
                normrope(c['ZQ'].t[:tn, :], 8, 0, 0, (c['ZQ'],))
            normrope(ZKV.t[:tn, 0:256], 2, 8, 1024, (ZKV,))

        def st2b(c):
            grp, j, s0, tn = c['grp'], c['j'], c['s0'], c['tn']
            doq = grp['doq']
            t0 = grp['g0'] + s0
            if j == 0:
                grp['QT'] = qT.next() if doq else None
                grp['KT'] = kT.next()
            QT, KT, QR, ZKV = grp['QT'], grp['KT'], c['QR'], c['ZKV']
            Vt = vsb.next()
            O.cp('act', Vt.t[:tn, :], ZKV.t[:tn, 256:512], ZKV, Vt)
            O.dma('pool', scr.V[t0:t0 + tn, :], Vt.t[:tn, :], Vt, ())
            if doq:
                for h in range(8):
                    O.tr(pqT.t[:, h, 0:tn], QR.t[:tn, h * 128:(h + 1) * 128], ident.t[:tn, :tn],
                         (QR.B(0), ident), pqT)
                O.cp('act', QT.t[:, :, s0:s0 + tn], pqT.t[:, :, 0:tn], pqT, QT.B(j))
            for h in range(2):
                O.tr(pkT.t[:, h, 0:tn], QR.t[:tn, 1024 + h * 128:1024 + (h + 1) * 128], ident.t[:tn, :tn],
                     (QR.B(8), ident), pkT)
            O.cp('dve', KT.t[:, :, s0:s0 + tn], pkT.t[:, 0:2, 0:tn], pkT, KT.B(j))
            if j == grp['nsub'] - 1:
                g0, gn, nsub = grp['g0'], grp['gn'], grp['nsub']
                if doq:
                    O.dma('pool', scr.QT[:, :, g0:g0 + gn], QT.t[:, :, :gn], [QT.B(i) for i in range(nsub)], ())
                O.dma('pool', scr.KT[:, :, g0:g0 + gn], KT.t[:, :, :gn], [KT.B(i) for i in range(nsub)], ())

        nw = len(work)
        for k in range(-3, nw + 2):
            if 0 <= k - 2 < nw:
                st2a(work[k - 2])
            if 0 <= k < nw:
                st1(work[k])
            if 0 <= k + 2 < nw:
                st0b(work[k + 2])
            if 0 <= k - 2 < nw:
                st2b(work[k - 2])
            if 0 <= k + 3 < nw:
                st0(work[k + 3])
        S.flush()


def phase_A2(nc, S, O, job, scr):
    NA, NO = job.N_all, job.N_own
    P = Ph(nc, "A2" + job.name)
    with P.st:
        w = P.sb([128, 16, 2048], BF16)
        for kc in range(16):
            O.dma('sp', w.t[:, kc, :], scr.wb_in[kc, :, 1536:3584], (), w.B(kc))
        xg = P.ring([128, 16, 512], BF16, 2)
        stg = P.ring([128, 512], F32, 4)
        pz = P.pring([128, 512], F32, 6)
        for (g0, gn) in chunks(NA, 512):
            X = xg.next()
            O.dma('sp', X.t[:, :, :gn], scr.xnT[:, :, g0:g0 + gn].rearrange("k p t -> p k t"), (), X)
            nch = 16 if g0 < NO else 8
            for c in range(nch):
                pz_ = pz.next()
                for kc in range(16):
                    O.mm(pz_.t[:, :gn], w.t[:, kc, c * 128:(c + 1) * 128], X.t[:, kc, :gn], kc == 0, kc == 15,
                         (w.B(kc), X, pz_), pz_)
                sg = stg.next()
                O.cp('act' if c % 2 else 'dve', sg.t[:, :gn], pz_.t[:, :gn], pz_, sg)
                if c < 8:
                    dst = scr.xrT[c * 128:(c + 1) * 128, g0:g0 + gn]
                else:
                    dst = scr.gyT[(c - 8) * 128:(c - 7) * 128, g0:g0 + gn]
                O.dma('pool', dst, sg.t[:, :gn], sg, ())
        S.flush()


def phase_L(nc, S, O, job, scr, identf):
    NA, NO = job.N_all, job.N_own
    NAp = rup(NA, 1024)
    GB = 3
    P = Ph(nc, "L" + job.name)
    with P.st:
        prm = P.sb([128, 96], F32)
        O.dma('sp', prm.t[:], job.lruP[:, :], (), prm)
        der = P.sb([128, 72], F32)
        gw = P.sb([128, 32, 128], BF16)
        with contextlib.ExitStack() as inner:
            gwf = T(inner.enter_context(nc.sbuf_tensor("L_gwf" + job.name, [128, 32, 128], F32)))
            O.dma('sp', gwf.t[:], job.gates.rearrange("(b c) d -> c b d", c=128), (), gwf)
            O.cp('dve', gw.t[:], gwf.t[:], gwf, gw)
            O.ts('dve', der.t[:, 0:32], prm.t[:, 48:80], 0.5, None, ALU.mult, ALU.bypass, prm, der)
            O.ts('dve', der.t[:, 64:72], prm.t[:, 40:48], 0.5, None, ALU.mult, ALU.bypass, (prm, der), der)
            O.act(der.t[:, 48:64], prm.t[:, 80:96], AF.Exp, (prm, der), der, scale=-1.0)
            O.act(der.t[:, 48:64], der.t[:, 48:64], AF.Ln, der, der, bias=1.0)
            O.ts('dve', der.t[:, 32:48], der.t[:, 48:64], -4.0, None, ALU.mult, ALU.bypass, der, der)
            S.flush()
        xh = P.sb([128, NAp], F32)
        xcb = P.sb([128, NAp], BF16)
        hb = P.sb([128, NAp], F32)
        tl = P.ring([128, 1024], F32, 14)
        gyr = P.ring([128, 1024], F32, 4)
        hfr = P.ring([128, 1024], F32, 2)
        lo_ = P.ring([128, 1024], BF16, 2)
        pz = P.pring([128, 1024], F32, 4)
        tiles = chunks(NA, 1024)

        def load_x(q, cc, t0, tn):
            r_ = slice(cc * 128, (cc + 1) * 128)
            ti = t0 // 1024
            O.dma(q, xh.t[:, t0:t0 + tn], scr.xhT[r_, t0:t0 + tn], (), xh.B(ti))
            O.dma(q, xcb.t[:, t0:t0 + tn], scr.xcbT[r_, t0:t0 + tn], (), xcb.B(ti))

        for (t0, tn) in tiles:
            load_x('sp', 0, t0, tn)
        for c in range(8):
            rows = slice(c * 128, (c + 1) * 128)

            def gates_front(d, t0, tn):
                zr = pz.next()
                zi = pz.next()
                ti = t0 // 1024
                for (s0, sn) in chunks(tn, 512):
                    O.mm(zr.t[:, s0:s0 + sn], gw.t[:, (0 * 2 + d) * 8 + c, :], xcb.t[:, t0 + s0:t0 + s0 + sn],
                         True, True, (gw, xcb.B(ti), zr.B(s0)), zr.B(s0))
                    O.mm(zi.t[:, s0:s0 + sn], gw.t[:, (1 * 2 + d) * 8 + c, :], xcb.t[:, t0 + s0:t0 + s0 + sn],
                         True, True, (gw, xcb.B(ti), zi.B(s0)), zi.B(s0))
                zrb = [zr.B(s0) for (s0, sn) in chunks(tn, 512)]
                zib = [zi.B(s0) for (s0, sn) in chunks(tn, 512)]
                thr = tl.next()
                thi = tl.next()
                a = tl.next()
                O.act(thr.t[:, :tn], zr.t[:, :tn], AF.Tanh, (zrb, der), thr, scale=0.5,
                      bias=der.t[:, d * 8 + c:d * 8 + c + 1])
                O.act(thi.t[:, :tn], zi.t[:, :tn], AF.Tanh, (zib, der), thi, scale=0.5,
                      bias=der.t[:, 16 + d * 8 + c:16 + d * 8 + c + 1])
                O.act(a.t[:, :tn], thr.t[:, :tn], AF.Exp, (thr, der), a, scale=der.t[:, 32 + d * 8 + c:33 + d * 8 + c],
                      bias=der.t[:, 32 + d * 8 + c:33 + d * 8 + c])
                if d == 1:
                    O.tt('dve', thr.t[:, :tn], a.t[:, :tn], a.t[:, :tn], ALU.mult, a, thr)
                else:
                    O.act(thr.t[:, :tn], a.t[:, :tn], AF.Square, a, thr)
                O.stt(thi.t[:, :tn], thi.t[:, :tn], 1.0, xh.t[:, t0:t0 + tn], ALU.add, ALU.mult,
                      (thi, xh.B(ti)), thi)
                return a, thr, thi

            def batches(tl_):
                return [tl_[i:i + GB] for i in range(0, len(tl_), GB)]

            carry = None
            for batch in batches(list(reversed(tiles))):
                items = [(t0, tn) + gates_front(1, t0, tn) for (t0, tn) in batch]
                for (t0, tn, a, a2, u1) in items:
                    O.act(a2.t[:, :tn], a2.t[:, :tn], AF.Sqrt, a2, a2, scale=-1.0, bias=1.0)
                for (t0, tn, a, a2, u1) in items:
                    ti = t0 // 1024
                    O.tt('dve', u1.t[:, :tn], u1.t[:, :tn], a2.t[:, :tn], ALU.mult, (u1, a2), u1)
                    init = 0.0 if carry is None else carry[0]
                    O.scan(rev(hb.t[:, t0:t0 + tn]), rev(a.t[:, :tn]), rev(u1.t[:, :tn]), init,
                           (a, u1, carry[1] if carry else None), hb.B(ti))
                    carry = (hb.t[:, t0:t0 + 1], hb.B(ti))
                if c + 1 < 8:
                    for (t0, tn, a, a2, u1) in items:
                        if t0 >= NO:
                            load_x('pool', c + 1, t0, tn)
            carry = None
            ftiles = [(t0, tn) for (t0, tn) in tiles if t0 < NO]
            for batch in batches(ftiles):
                items = [(t0, tn) + gates_front(0, t0, tn) for (t0, tn) in batch]
                gys = []
                for (t0, tn, a, a2, u1) in items:
                    GY = gyr.next()
                    O.dma('sp', GY.t[:, :tn], scr.gyT[rows, t0:t0 + tn], (), GY)
                    gys.append(GY)
                for (t0, tn, a, a2, u1) in items:
                    O.act(a2.t[:, :tn], a2.t[:, :tn], AF.Sqrt, a2, a2, scale=-1.0, bias=1.0)
                for (t0, tn, a, a2, u1), GY in zip(items, gys):
                    O.act(GY.t[:, :tn], GY.t[:, :tn], AF.Gelu_apprx_tanh, GY, GY)
                for (t0, tn, a, a2, u1), GY in zip(items, gys):
                    ti = t0 // 1024
                    O.tt('dve', u1.t[:, :tn], u1.t[:, :tn], a2.t[:, :tn], ALU.mult, (u1, a2), u1)
                    HF = hfr.next()
                    init = 0.0 if carry is None else carry[0]
                    O.scan(HF.t[:, :tn], a.t[:, :tn], u1.t[:, :tn], init, (a, u1, carry[1] if carry else None), HF)
                    carry = (HF.t[:, tn - 1:tn], HF)
                    O.tt('dve', a2.t[:, :tn], HF.t[:, :tn], hb.t[:, t0:t0 + tn], ALU.add, (HF, hb.B(ti)), a2)
                    LO = lo_.next()
                    O.tt('dve', LO.t[:, :tn], a2.t[:, :tn], GY.t[:, :tn], ALU.mult, (a2, GY), LO)
                    O.dma('pool', scr.lruT[rows, t0:t0 + tn], LO.t[:, :tn], LO, ())
                if c + 1 < 8:
                    for (t0, tn, a, a2, u1) in items:
                        load_x('pool', c + 1, t0, tn)
        S.flush()


def phase_T(nc, S, O, job, scr, ident, bgf=None):
    NA, NO = job.N_all, job.N_own
    kch = chunks(NA, 128)
    nkc = len(kch)
    nfull = NA // 128
    P = Ph(nc, "T" + job.name)
    with P.st:
        KT = P.sb([128, 2, NA], BF16)
        O.dma('sp', KT.t[:], scr.KT[:, :, 0:NA], (), KT)
        V = P.sb([128, nkc, 2, 132], BF16)
        O.memset('dve', V.t[:, :, :, 128:132], 1.0, V)
        for hh in range(2):
            if nfull:
                O.dma('sp', V.t[:, 0:nfull, hh, 0:128],
                      scr.V[0:nfull * 128, hh * 128:(hh + 1) * 128].rearrange("(c p) d -> p c d", p=128), V, V)
            if nkc > nfull:
                kn = NA - nfull * 128
                O.dma('sp', V.t[:kn, nfull, hh, 0:128], scr.V[nfull * 128:NA, hh * 128:(hh + 1) * 128], V, V)
        QTb = P.ring([128, 8, 512], BF16, 2)
        PT = P.ring([128, 512], BF16, 4)
        oall = P.sb([128, 4, 1024], F32)
        junk = P.sb([128, 1024], BF16)
        sm = P.ring([128, 4], F32, 8)
        abf = P.ring([128, 1024], BF16, 2)
        aTg = P.ring([128, 8, 512], BF16, 2)
        pS = P.pring([128, 512], F32, 3)
        pO = [P.ps([128, 512], F32) for _ in range(4)]
        pT = P.ps([128, 8, 128], BF16)
        scale = float(HD) ** -0.5
        LOOK = 2
        qblocks = balanced(NO, 512)
        nsteps_total = len(qblocks) * 8 * nkc
        bgs = []
        for (mk, nunits) in (bgf or []):
            bgs.append([mk(P), max(1, (nsteps_total - 40) // nunits)])
        stepno = 0
        for (q0, qn) in qblocks:
            Q = QTb.next()
            O.dma('sp', Q.t[:, :, :qn], scr.QT[:, :, q0:q0 + qn], (), Q)
            subs = chunks(qn, 128)
            steps = [(h, ci) for h in range(8) for ci in range(nkc)]

            def emitS(idx):
                h, ci = steps[idx]
                k0, kn = kch[ci]
                ps = pS.next()
                O.mm(ps.t[:kn, :qn], KT.t[:, h // 4, k0:k0 + kn], Q.t[:, h, :qn], True, True, (KT, Q, ps), ps)
                pt = PT.next()
                O.act(pt.t[:kn, :qn], ps.t[:kn, :qn], AF.Exp, ps, pt, scale=scale)
                return pt

            pts = {}
            for idx in range(min(LOOK, len(steps))):
                pts[idx] = emitS(idx)
            for idx in range(len(steps)):
                if idx + LOOK < len(steps):
                    pts[idx + LOOK] = emitS(idx + LOOK)
                h, ci = steps[idx]
                k0, kn = kch[ci]
                kvh = h // 4
                pt = pts.pop(idx)
                stepno += 1
                for g_, ev_ in bgs:
                    if stepno % ev_ == 0:
                        next(g_, None)
                for si, (s0, sn) in enumerate(subs):
                    O.mm(pO[si].t[:sn, 0:129], pt.t[:kn, s0:s0 + sn], V.t[:kn, ci, kvh, 0:129],
                         ci == 0, ci == nkc - 1, (pt, V, pO[si]), pO[si])
                if ci == nkc - 1:
                    for si, (s0, sn) in enumerate(subs):
                        s = sm.next()
                        O.recip(s.t[:sn, 0:1], pO[si].t[:sn, 128:129], pO[si], s)
                        O.ts('dve', oall.t[:sn, si, h * 128:(h + 1) * 128], pO[si].t[:sn, 0:128], s.t[:sn, 0:1], None,
                             ALU.mult, ALU.bypass, (pO[si], s), oall.B(si))
            G = aTg.next()
            for si, (s0, sn) in enumerate(subs):
                s = sm.next()
                O.act(junk.t[:sn, :], oall.t[:sn, si, :], AF.Square, oall.B(si), s, accum_out=s.t[:sn, 0:1])
                O.rstd(s.t[:sn, 2:3], s.t[:sn, 0:1], 1024, s.t[:sn, 1:2], s, s)
                AB = abf.next()
                O.act(AB.t[:sn, :], oall.t[:sn, si, :], AF.Copy, (oall.B(si), s), AB, scale=s.t[:sn, 2:3])
                for h in range(8):
                    O.tr(pT.t[:, h, 0:sn], AB.t[:sn, h * 128:(h + 1) * 128], ident.t[:sn, :sn], (AB, ident), pT)
                O.cp('dve', G.t[:, :, s0:s0 + sn], pT.t[:, :, 0:sn], pT, G.B(si))
            O.dma('pool', scr.attnT[:, :, q0:q0 + qn].rearrange("k p t -> p k t"), G.t[:, :, :qn],
                  [G.B(si) for si in range(len(subs))], ())
        for g_, ev_ in bgs:
            for _ in g_:
                pass
        S.flush()


def phase_M(nc, S, O, job, scr, ident, gpost):
    NO = job.N_own
    P = Ph(nc, "M" + job.name)
    with P.st:
        wo = P.sb([128, 16, D], BF16)
        for kc in range(16):
            O.dma('sp', wo.t[:, kc, :], scr.wb_out[kc, :, :], (), wo.B(kc))
        gp = P.sb([128, D], F32)
        O.dma('sp', gp.t[:], gpost[0:1, :].to_broadcast([128, D]), (), gp)
        ones2 = P.sb([128, 2], F32)
        O.memset('dve', ones2.t[:], 1.0, ones2)
        aT = P.ring([128, 8, 128], BF16, 2)
        lT = P.ring([128, 8, 128], BF16, 2)
        xt = P.ring([128, D], F32, 3)
        sqr = P.ring([128, 8, 128], F32, 2)
        Y = P.ring([128, D], F32, 3)
        hnb = P.ring([128, D], BF16, 2)
        hnTg = P.ring([128, 16, 512], BF16, 2)
        junk = P.sb([128, D], BF16)
        sm = P.ring([128, 12], F32, 4)
        pA = P.pring([128, 512], F32, 2)
        pB = P.pring([128, 512], F32, 2)
        pss = P.ps([128, 512], F32)
        pTa = P.ps([128, 8, 128], BF16)
        pTb = P.ps([128, 8, 128], BF16)
        lruT3 = scr.lruT.rearrange("(k p) t -> p k t", p=128)
        work = []
        for (g0, gn) in chunks(NO, 512):
            grp = {'g0': g0, 'gn': gn, 'nsub': len(chunks(gn, 128))}
            for j, (s0, tn) in enumerate(chunks(gn, 128)):
                work.append({'grp': grp, 'j': j, 's0': s0, 'tn': tn})

        def s1(c, inter):
            grp, j, s0, tn = c['grp'], c['j'], c['s0'], c['tn']
            t0 = grp['g0'] + s0
            A = aT.next()
            L = lT.next()
            X = xt.next()
            O.dma('sp', A.t[:, :, :tn], scr.attnT[:, :, t0:t0 + tn].rearrange("k p t -> p k t"), (), A)
            O.dma('sp', L.t[:, :, :tn], lruT3[:, :, t0:t0 + tn], (), L)
            O.dma('sp', X.t[:tn, :], job.x[t0:t0 + tn, :], (), X)
            s = sm.next()
            SQ = sqr.next()
            O.tt('dve', SQ.t[:, :, :tn], L.t[:, :, :tn], L.t[:, :, :tn], ALU.mult, L, SQ)
            for kc in range(8):
                O.mm(pss.t[:tn, 0:2], SQ.t[:, kc, :tn], ones2.t[:, 0:2], kc == 0, kc == 7, (SQ, ones2, pss), pss)
            O.rstd(s.t[:tn, 2:3], pss.t[:tn, 0:1], LRUW, s.t[:tn, 1:2], (pss, s), s)
            y = Y.next()
            for n in range(4):
                cs = slice(n * 512, (n + 1) * 512)
                a_ = pA.next()
                b_ = pB.next()
                for kc in range(8):
                    O.mm(a_.t[:tn, :], A.t[:, kc, :tn], wo.t[:, kc, cs], kc == 0, kc == 7, (A, wo.B(kc), a_), a_)
                for kc in range(8):
                    O.mm(b_.t[:tn, :], L.t[:, kc, :tn], wo.t[:, 8 + kc, cs], kc == 0, kc == 7, (L, wo.B(8 + kc), b_), b_)
                O.act(y.t[:tn, cs], a_.t[:tn, :], AF.Copy, a_, y.B(n))
                O.stt(y.t[:tn, cs], b_.t[:tn, :], s.t[:tn, 2:3], y.t[:tn, cs], ALU.mult, ALU.add,
                      (b_, s, y.B(n)), y.B(n))
                if n < len(inter) and inter[n] is not None:
                    inter[n]()
            c['X'], c['s'], c['y'] = X, s, y

        def s2a(c):
            grp, j, s0, tn = c['grp'], c['j'], c['s0'], c['tn']
            t0 = grp['g0'] + s0
            X, s, y = c['X'], c['s'], c['y']
            yb = [y.B(n) for n in range(4)]
            O.act(junk.t[:tn, :], y.t[:tn, :], AF.Square, yb, s, accum_out=s.t[:tn, 4:5])
            O.rstd(s.t[:tn, 6:7], s.t[:tn, 4:5], D, s.t[:tn, 5:6], s, s)
            O.stt(y.t[:tn, :], y.t[:tn, :], s.t[:tn, 6:7], gp.t[:tn, :], ALU.mult, ALU.mult, (yb, s, gp), yb)
            O.stt(X.t[:tn, :], y.t[:tn, :], 1.0, X.t[:tn, :], ALU.mult, ALU.add, (yb, X), X)
            O.dma('pool', scr.h1[t0:t0 + tn, :], X.t[:tn, :], X, ())

        def s2b(c):
            grp, j, s0, tn = c['grp'], c['j'], c['s0'], c['tn']
            X, s = c['X'], c['s']
            O.act(junk.t[:tn, :], X.t[:tn, :], AF.Square, X, s, accum_out=s.t[:tn, 8:9])
            O.rstd(s.t[:tn, 10:11], s.t[:tn, 8:9], D, s.t[:tn, 9:10], s, s)
            HN = hnb.next()
            O.act(HN.t[:tn, :], X.t[:tn, :], AF.Copy, (X, s), HN, scale=s.t[:tn, 10:11])
            c['HN'] = HN

        def s3(c):
            grp, j, s0, tn = c['grp'], c['j'], c['s0'], c['tn']
            if j == 0:
                grp['HG'] = hnTg.next()
            HG, HN = grp['HG'], c['HN']
            for kc in range(16):
                pT = pTa if kc < 8 else pTb
                O.tr(pT.t[:, kc % 8, 0:tn], HN.t[:tn, kc * 128:(kc + 1) * 128], ident.t[:tn, :tn],
                     (HN, ident), pT)
            O.cp('dve', HG.t[:, 0:8, s0:s0 + tn], pTa.t[:, :, 0:tn], pTa, HG.B(j))
            O.cp('act', HG.t[:, 8:16, s0:s0 + tn], pTb.t[:, :, 0:tn], pTb, HG.B(j))
            if j == grp['nsub'] - 1:
                g0, gn = grp['g0'], grp['gn']
                O.dma('pool', scr.hnT[:, :, g0:g0 + gn].rearrange("k p t -> p k t"), HG.t[:, :, :gn],
                      [HG.B(i) for i in range(grp['nsub'])], ())

        nw = len(work)
        for k in range(nw + 3):
            f2b = (lambda c=work[k - 2]: s2b(c)) if 0 <= k - 2 < nw else None
            f2a = (lambda c=work[k - 1]: s2a(c)) if 0 <= k - 1 < nw else None
            if k < nw:
                s1(work[k], [f2b, f2a])
            else:
                if f2b:
                    f2b()
                if f2a:
                    f2a()
            if 0 <= k - 3 < nw:
                s3(work[k - 3])
        S.flush()


def phase_F(nc, S, O, job, scr, gpost):
    NO, NOUT = job.N_own, job.N_out
    P = Ph(nc, "F" + job.name)
    with P.st:
        gp = P.sb([128, D], F32)
        O.dma('sp', gp.t[:], gpost[1:2, :].to_broadcast([128, D]), (), gp)
        fp = P.sb([128, 176], F32)
        O.dma('sp', fp.t[:], job.ffnP[:, :], (), fp)
        hb = P.ring([128, 16, 512], BF16, 2)
        wu = P.ring([128, 16, 2, 128], BF16, 3)
        wd = P.ring([128, 4, 512], BF16, 3)
        aT = P.sb([128, NJ, 512], BF16)
        Fr = P.ring([128, D], F32, 4)
        h1t = P.ring([128, D], F32, 4)
        cr = P.ring([128, 512], F32, 4)
        slr = P.ring([128, 512], F32, 2)
        junk = P.sb([128, D], BF16)
        sm = P.ring([128, 4], F32, 4)
        pG = P.pring([128, 512], F32, 2)
        pU = P.pring([128, 512], F32, 2)
        pD = [P.ps([128, 512], F32) for _ in range(4)]
        evi = 0
        blocks = tail_balanced(NOUT, 510)

        def loadH(bi):
            b0, bn = blocks[bi]
            H = hb.next()
            lo = max(b0 - 1, 0)
            hi = min(b0 + bn + 1, NO)
            if b0 == 0:
                O.memset('pool', H.t[:, :, 0:1], 0.0, H)
            if b0 + bn + 1 > NO:
                O.memset('pool', H.t[:, :, bn + 1:bn + 2], 0.0, H)
            O.dma('sp', H.t[:, :, lo - (b0 - 1):hi - (b0 - 1)],
                  scr.hnT[:, :, lo:hi].rearrange("k p t -> p k t"), H, H)
            return H

        Hnext = loadH(0)
        pending = []
        for bi, (b0, bn) in enumerate(blocks):
            H = Hnext
            nb = bn + 2
            for j in range(NJ):
                Wu = wu.next()
                O.dma('sp', Wu.t[:], scr.wb_up[j], (), Wu)
                G = pG.next()
                U = pU.next()
                for kc in range(16):
                    O.mm(G.t[:, :nb], Wu.t[:, kc, 0, :], H.t[:, kc, :nb], kc == 0, kc == 15, (Wu, H, G), G)
                for kc in range(16):
                    O.mm(U.t[:, :nb], Wu.t[:, kc, 1, :], H.t[:, kc, :nb], kc == 0, kc == 15, (Wu, H, U), U)
                c0 = cr.next()
                O.ts('dve', c0.t[:, :bn], G.t[:, 0:bn], fp.t[:, j * 3:j * 3 + 1], fp.t[:, 132 + j:133 + j],
                     ALU.mult, ALU.add, (G, fp), c0)
                c1 = cr.next()
                O.stt(c1.t[:, :bn], G.t[:, 1:bn + 1], fp.t[:, j * 3 + 1:j * 3 + 2], c0.t[:, :bn], ALU.mult, ALU.add,
                      (G, fp, c0), c1)
                c2 = cr.next()
                O.stt(c2.t[:, :bn], G.t[:, 2:bn + 2], fp.t[:, j * 3 + 2:j * 3 + 3], c1.t[:, :bn], ALU.mult, ALU.add,
                      (G, fp, c1), c2)
                sl = slr.next()
                O.act(sl.t[:, :bn], c2.t[:, :bn], AF.Silu, c2, sl)
                O.tt('dve', aT.t[:, j, :bn], sl.t[:, :bn], U.t[:, 1:bn + 1], ALU.mult, (sl, U), aT.B(j))
                if pending and j % 3 == 2:
                    pending.pop(0)()
            while pending:
                pending.pop(0)()
            if bi + 1 < len(blocks):
                Hnext = loadH(bi + 1)
            subs = chunks(bn, 128)
            Fs = [Fr.next() for _ in subs]
            H1s = []
            for si, (s0, sn) in enumerate(subs):
                H1 = h1t.next()
                O.dma('sp', H1.t[:sn, :], scr.h1[b0 + s0:b0 + s0 + sn, :], (), H1)
                H1s.append(H1)
            for n in range(4):
                cs = slice(n * 512, (n + 1) * 512)
                for q in range(11):
                    Wd = wd.next()
                    O.dma('sp', Wd.t[:], scr.wb_down[q * 4:(q + 1) * 4, :, cs].rearrange("k p c -> p k c"), (), Wd)
                    for kk in range(4):
                        kc = q * 4 + kk
                        for si, (s0, sn) in enumerate(subs):
                            O.mm(pD[si].t[:sn, :], aT.t[:, kc, s0:s0 + sn], Wd.t[:, kk, :], kc == 0, kc == NJ - 1,
                                 (aT.B(kc), Wd, pD[si]), pD[si])
                for si, (s0, sn) in enumerate(subs):
                    evi += 1
                    O.cp('act' if evi % 2 else 'dve', Fs[si].t[:sn, cs], pD[si].t[:sn, :], pD[si], Fs[si].B(n))
            def chain(si, s0, sn, Ft, H1, b0=b0):
                fb = [Ft.B(n) for n in range(4)]
                s = sm.next()
                O.act(junk.t[:sn, :], Ft.t[:sn, :], AF.Square, fb, s, accum_out=s.t[:sn, 0:1])
                O.rstd(s.t[:sn, 2:3], s.t[:sn, 0:1], D, s.t[:sn, 1:2], s, s)
                O.stt(Ft.t[:sn, :], Ft.t[:sn, :], s.t[:sn, 2:3], gp.t[:sn, :], ALU.mult, ALU.mult, (fb, s, gp), fb)
                O.stt(Ft.t[:sn, :], Ft.t[:sn, :], 1.0, H1.t[:sn, :], ALU.mult, ALU.add, (fb, H1), fb)
                O.dma('pool', job.out[b0 + s0:b0 + s0 + sn, :], Ft.t[:sn, :], fb, ())

            pending = [(lambda si=si, s0=s0, sn=sn, Ft=Fs[si], H1=H1s[si]: chain(si, s0, sn, Ft, H1))
                       for si, (s0, sn) in enumerate(subs)]
        for f in pending:
            f()
        S.flush()


def _rope_tables(n_tokens):
    rows = n_tokens // 64
    t_row = np.repeat(np.arange(rows), 64).astype(np.float32)
    t_col = np.tile(np.arange(64), rows).astype(np.float32)
    inv = (np.float32(10000.0) ** (-(np.arange(0, 64, 2, dtype=np.float32) / np.float32(64)))).astype(np.float32)
    ang = np.concatenate([t_row[:, None] * inv, t_col[:, None] * inv], axis=-1)
    ang = np.concatenate([np.zeros((NMETA, 64), np.float32), ang], axis=0).astype(np.float32)
    c = np.cos(ang).astype(np.float32).reshape(-1, 2, 32)
    s = np.sin(ang).astype(np.float32).reshape(-1, 2, 32)
    C = np.stack([c, c], axis=2).reshape(-1, 128)
    Sg = np.stack([s, -s], axis=2).reshape(-1, 128)
    return np.ascontiguousarray(C), np.ascontiguousarray(Sg)


def _pcol(v, k):
    return np.ascontiguousarray(np.asarray(v, np.float32).reshape(k, 128).T)


def _lru_pack(conv_w, conv_b, b_r, b_i, lam, flip):
    z = np.zeros((1, LRUW), np.float32)
    w5 = np.concatenate([conv_w, z], axis=0)
    if flip:
        w5 = w5[::-1]
        b_r, b_i, lam = b_r[::-1], b_i[::-1], lam[::-1]
    out = np.zeros((128, 96), np.float32)
    out[:, 0:40] = np.stack([_pcol(w5[k], 8) for k in range(5)], axis=-1).reshape(128, 40)
    out[:, 40:48] = _pcol(conv_b, 8)
    out[:, 48:64] = np.concatenate([_pcol(b_r[d], 8) for d in range(2)], axis=1)
    out[:, 64:80] = np.concatenate([_pcol(b_i[d], 8) for d in range(2)], axis=1)
    out[:, 80:96] = np.concatenate([_pcol(lam[d], 8) for d in range(2)], axis=1)
    return out


def _gates_pack(w_r, w_i, flip):
    if flip:
        w_r, w_i = w_r[::-1], w_i[::-1]
    return np.ascontiguousarray(np.stack([w_r, w_i], axis=0).reshape(32 * 128, 128).astype(np.float32))


def _ffn_pack(conv_w, conv_b, flip):
    if flip:
        conv_w = conv_w[::-1]
    out = np.zeros((128, 176), np.float32)
    out[:, 0:132] = np.stack([_pcol(conv_w[k], NJ) for k in range(3)], axis=-1).reshape(128, 132)
    out[:, 132:176] = _pcol(conv_b, NJ)
    return out


_PROG_CACHE = {}


def kernel(x_prompt, x_sample, meta_tokens, norm_pre_mix, w_in, q_norm, k_norm, lru_conv_w, lru_conv_b,
           lru_w_r, lru_b_r, lru_w_i, lru_b_i, lru_lambda, attn_out_norm, lru_out_norm, w_out,
           norm_post_mix, norm_pre_ffn, w_up, ffn_conv_w, ffn_conv_b, w_down, norm_post_ffn):
    f = lambda a: np.asarray(a, dtype=np.float32)
    x_prompt, x_sample, meta = f(x_prompt), f(x_sample), f(meta_tokens)
    Bp, Sp, _ = x_prompt.shape
    Bs, Ss, _ = x_sample.shape
    ncores = 8
    assert Bp == ncores and Bs * 2 == ncores
    LA = Sp + NMETA
    LB = Ss + NMETA
    assert LB % 2 == 0
    NBh = LB // 2
    NB_own = NBh + 1
    key = (LA, LB)
    if key not in _PROG_CACHE:
        _PROG_CACHE[key] = build_program(LA, LB, NB_own, NBh)
    nc = _PROG_CACHE[key]

    CA, SA = _rope_tables(Sp)
    CB, SB = _rope_tables(Ss)
    gcols = np.concatenate([_pcol(f(norm_pre_mix)[0], 16),
                            _pcol(np.concatenate([f(attn_out_norm)[0], f(lru_out_norm)[0]]), 16),
                            _pcol(f(norm_pre_ffn)[0], 16)], axis=1)
    gpost = np.ascontiguousarray(np.stack([f(norm_post_mix)[0], f(norm_post_ffn)[0]], axis=0))
    qkn = np.ascontiguousarray(np.stack([f(q_norm)[0], f(k_norm)[0]], axis=0))
    lp = [_lru_pack(f(lru_conv_w)[0], f(lru_conv_b)[0], f(lru_b_r)[0], f(lru_b_i)[0], f(lru_lambda)[0], fl)
          for fl in (False, True)]
    gp = [_gates_pack(f(lru_w_r)[0], f(lru_w_i)[0], fl) for fl in (False, True)]
    fpk = [_ffn_pack(f(ffn_conv_w)[0], f(ffn_conv_b)[0], fl) for fl in (False, True)]
    shared = {"w_in": np.ascontiguousarray(f(w_in)[0]), "w_out": np.ascontiguousarray(f(w_out)[0]),
              "w_up": np.ascontiguousarray(f(w_up)[0]), "w_down": np.ascontiguousarray(f(w_down)[0]),
              "gcols": gcols, "gpost": gpost, "qkn": qkn}
    in_maps = []
    for c in range(ncores):
        s, half = c // 2, c % 2
        xa = np.concatenate([meta, x_prompt[c]], axis=0)
        xb = np.concatenate([meta, x_sample[s]], axis=0)
        cb, sb = CB, SB
        if half:
            xb, cb, sb = xb[::-1], CB[::-1], SB[::-1]
        m = dict(shared)
        m.update({"xA": np.ascontiguousarray(xa), "cosA": CA, "sinA": SA, "lruPA": lp[0], "gatesA": gp[0],
                  "ffnPA": fpk[0],
                  "xB": np.ascontiguousarray(xb), "cosB": np.ascontiguousarray(cb), "sinB": np.ascontiguousarray(sb),
                  "lruPB": lp[half], "gatesB": gp[half], "ffnPB": fpk[half]})
        in_maps.append(m)
    res = run_bass_kernel_spmd(nc, in_maps, core_ids=list(range(ncores)))
    y_prompt = np.empty((Bp, Sp, D), np.float32)
    y_sample = np.empty((Bs, Ss, D), np.float32)
    for c in range(ncores):
        r = res.results[c]
        s, half = c // 2, c % 2
        y_prompt[c] = r["yA"][NMETA:]
        yb = r["yB"]
        if half == 0:
            y_sample[s, 0:NBh - NMETA] = yb[NMETA:]
        else:
            y_sample[s, NBh - NMETA:] = yb[::-1]
    return (y_prompt, y_sample)
```

```python
import contextlib
import numpy as np
import concourse.bass as bass
import concourse.mybir as mybir
from concourse.ap import AP
from concourse.bass_utils import run_bass_kernel_spmd

F32 = mybir.dt.float32
BF16 = mybir.dt.bfloat16
AF = mybir.ActivationFunctionType
ALU = mybir.AluOpType

D = 2048
HD = 128
NQH = 8
NKVH = 2
LRUW = 1024
FFN = 5632
NJ = FFN // 128
NMETA = 16
EPS = 1e-6
ENGS = ('pe', 'act', 'dve', 'pool', 'sp')
NDSEM = 8


class Buf:
    __slots__ = ('w', 'r')

    def __init__(self):
        self.w = None
        self.r = {}


class T:
    def __init__(self, t):
        self.t = t
        self.b = Buf()
        self.sub = {}

    def B(self, j):
        if j not in self.sub:
            self.sub[j] = Buf()
        return self.sub[j]


class Ring:
    def __init__(self, tiles):
        self.tiles = tiles
        self.i = 0

    def next(self):
        t = self.tiles[self.i % len(self.tiles)]
        self.i += 1
        return t


class Sched:
    def __init__(self, nc, sems, dsems):
        self.nc = nc
        self.sems = sems
        self.dsems = dsems
        self.cnt = {e: 0 for e in ENGS}
        self.dcnt = {e: 0 for e in ENGS}
        self.lastdma = {}
        self.seen = {e: {} for e in ENGS}
        self.ops = []
        self.q = {e: [] for e in ENGS}
        self.total_ops = 0
        self.total_waits = 0
        self.epoch = 0

    def op(self, eng, fn, reads=(), writes=(), dma=False):
        deps = set()
        ep = self.epoch
        for b in reads:
            if b.w is not None and b.w[0] == ep:
                deps.add(b.w[1])
        for b in writes:
            if b.w is not None and b.w[0] == ep:
                deps.add(b.w[1])
            for (e2, i2) in b.r.values():
                if e2 == ep:
                    deps.add(i2)
        i = len(self.ops)
        rk = ('dma', i) if dma else eng
        for b in reads:
            b.r[rk] = (ep, i)
        for b in writes:
            b.w = (ep, i)
            b.r = {}
        deps.discard(i)
        self.ops.append((eng, fn, deps, dma))
        self.q[eng].append(i)
        return i

    def _semh(self, key):
        if key[0] == 'c':
            return self.sems[key[1]]
        return self.dsems[key[1]][key[2]]

    def flush(self, final=False):
        nc = self.nc
        ops = self.ops
        n = len(ops)
        signal = [False] * n
        for i, (eng, fn, deps, dma) in enumerate(ops):
            for d in deps:
                de, _, _, ddma = ops[d]
                if ddma:
                    continue
                if de != eng or dma or eng != 'pe':
                    signal[d] = True
        for e in ENGS:
            for i in reversed(self.q[e]):
                if not ops[i][3]:
                    signal[i] = True
                    break
        bar = [(('c', e), self.cnt[e]) for e in ENGS if self.cnt[e] > 0]
        bar += list(self.lastdma.values())
        ev = [None] * n
        prevdma = [None] * n
        for i, (eng, fn, deps, dma) in enumerate(ops):
            if dma:
                k = self.dcnt[eng]
                self.dcnt[eng] += 1
                slot = k % NDSEM
                prevdma[i] = self.lastdma.get((eng, slot))
                ev[i] = (('d', eng, slot), (k // NDSEM + 1) * 16)
                self.lastdma[(eng, slot)] = ev[i]
            elif signal[i]:
                self.cnt[eng] += 1
                ev[i] = (('c', eng), self.cnt[eng])
        engobj = {'pe': nc.tensor, 'act': nc.scalar, 'dve': nc.vector, 'pool': nc.gpsimd, 'sp': nc.sync}
        endbar = None
        if final:
            endbar = [(('c', e), self.cnt[e]) for e in ENGS if self.cnt[e] > 0] + list(self.lastdma.values())

        def run_engine(eng):
            e = engobj[eng]
            seen = self.seen[eng]

            def wait(k, v):
                if seen.get(k, 0) >= v:
                    return
                seen[k] = v
                e.wait_ge(self._semh(k), v)
                self.total_waits += 1

            if self.q[eng]:
                for k, v in bar:
                    if k == ('c', eng):
                        continue
                    wait(k, v)
            for i in self.q[eng]:
                _, fn, deps, dma = ops[i]
                need = {}
                for d in deps:
                    de, _, _, ddma = ops[d]
                    if (not ddma) and de == eng and (not dma) and eng == 'pe':
                        continue
                    k, v = ev[d]
                    if need.get(k, 0) < v:
                        need[k] = v
                if dma and prevdma[i] is not None:
                    k, v = prevdma[i]
                    if need.get(k, 0) < v:
                        need[k] = v
                for k, v in need.items():
                    wait(k, v)
                ins = fn(e)
                if ev[i] is not None:
                    ins.then_inc(self._semh(ev[i][0]), 16 if dma else 1)
            if final and eng == 'sp':
                for k, v in endbar:
                    wait(k, v)

        with nc.Block() as block:
            @block.tensor
            def _(_e):
                run_engine('pe')

            @block.scalar
            def _(_e):
                run_engine('act')

            @block.vector
            def _(_e):
                run_engine('dve')

            @block.gpsimd
            def _(_e):
                run_engine('pool')

            @block.sync
            def _(_e):
                run_engine('sp')
        self.total_ops += n
        self.epoch += 1
        self.ops = []
        self.q = {e: [] for e in ENGS}


class Ph:
    def __init__(self, nc, name):
        self.nc = nc
        self.name = name
        self.st = contextlib.ExitStack()
        self.k = 0

    def sb(self, shape, dt):
        self.k += 1
        return T(self.st.enter_context(self.nc.sbuf_tensor(f"{self.name}_s{self.k}", list(shape), dt)))

    def ps(self, shape, dt):
        self.k += 1
        return T(self.st.enter_context(self.nc.psum_tensor(f"{self.name}_p{self.k}", list(shape), dt)))

    def ring(self, shape, dt, n):
        return Ring([self.sb(shape, dt) for _ in range(n)])

    def pring(self, shape, dt, n):
        return Ring([self.ps(shape, dt) for _ in range(n)])


def bl(*xs):
    out = []
    for x in xs:
        if x is None:
            continue
        if isinstance(x, Buf):
            out.append(x)
        elif isinstance(x, T):
            out.append(x.b)
        else:
            out.extend(bl(*x))
    return out


class Ops:
    def __init__(self, S):
        self.S = S

    def act(self, out, in_, func, r, w, **kw):
        self.S.op('act', lambda e: e.activation(out=out, in_=in_, func=func, **kw), bl(r), bl(w))

    def ts(self, eng, out, in0, s1, s2, op0, op1, r, w):
        self.S.op(eng, lambda e: e.tensor_scalar(out=out, in0=in0, scalar1=s1, scalar2=s2, op0=op0, op1=op1),
                  bl(r), bl(w))

    def stt(self, out, in0, sc, in1, op0, op1, r, w):
        self.S.op('dve', lambda e: e.scalar_tensor_tensor(out=out, in0=in0, scalar=sc, in1=in1, op0=op0, op1=op1),
                  bl(r), bl(w))

    def tt(self, eng, out, in0, in1, op, r, w):
        self.S.op(eng, lambda e: e.tensor_tensor(out=out, in0=in0, in1=in1, op=op), bl(r), bl(w))

    def cp(self, eng, out, in_, r, w):
        if eng == 'act':
            self.S.op('act', lambda e: e.activation(out=out, in_=in_, func=AF.Copy), bl(r), bl(w))
        else:
            self.S.op(eng, lambda e: e.tensor_copy(out=out, in_=in_), bl(r), bl(w))

    def recip(self, out, in_, r, w):
        self.S.op('dve', lambda e: e.reciprocal(out=out, in_=in_), bl(r), bl(w))

    def memset(self, eng, ap, val, w):
        self.S.op(eng, lambda e: e.memset(ap, val), (), bl(w))

    def mm(self, out, lhsT, rhs, start, stop, r, w):
        self.S.op('pe', lambda e: e.matmul(out, lhsT=lhsT, rhs=rhs, start=start, stop=stop), bl(r), bl(w))

    def tr(self, out, in_, ident, r, w):
        self.S.op('pe', lambda e: e.transpose(out=out, in_=in_, identity=ident), bl(r), bl(w))

    def scan(self, out, d0, d1, init, r, w):
        self.S.op('dve', lambda e: e.tensor_tensor_scan(out=out, data0=d0, data1=d1, initial=init,
                                                        op0=ALU.mult, op1=ALU.add), bl(r), bl(w))

    def dma(self, eng, out, in_, r, w):
        self.S.op(eng, lambda e: e.dma_start(out=out, in_=in_), bl(r), bl(w), dma=True)

    def rstd(self, out, ss, n, tmp, r, w):
        self.act(tmp, ss, AF.Ln, r, w, scale=1.0 / n, bias=EPS)
        self.act(out, tmp, AF.Exp, w, w, scale=-0.5)


def rev(v):
    (ps, pn), (fs, fn) = v.ap
    return AP(v.tensor, v.offset + (fn - 1) * fs, [[ps, pn], [-fs, fn]])


def swap_halves(v, nh):
    (ps, pn), (fs, fn) = v.ap
    assert fs == 1 and fn == nh * 128
    return AP(v.tensor, v.offset + 32, [[ps, pn], [64, 2 * nh], [-32, 2], [1, 32]])


def chunks(n, c):
    return [(i, min(c, n - i)) for i in range(0, n, c)]


def rup(n, c):
    return (n + c - 1) // c * c


def balanced(n, c):
    nb = (n + c - 1) // c
    base, rem = divmod(n, nb)
    out, o = [], 0
    for i in range(nb):
        sz = base + (1 if i < rem else 0)
        out.append((o, sz))
        o += sz
    return out


def tail_balanced(n, c):
    ch = chunks(n, c)
    if len(ch) >= 2 and ch[-1][1] < c // 2:
        o = ch[-2][0]
        tot = ch[-2][1] + ch[-1][1]
        a = (tot + 1) // 2
        ch = ch[:-2] + [(o, a), (o + a, tot - a)]
    return ch


class Job:
    pass


def build_program(NA_A, NB_all, NB_own, NB_out):
    nc = bass.Bass("TRN2", target_bir_lowering=False)

    def din(name, shape, dt=F32):
        return nc.dram_tensor(name, list(shape), dt, kind="ExternalInput").ap()

    def dout(name, shape, dt=F32):
        return nc.dram_tensor(name, list(shape), dt, kind="ExternalOutput").ap()

    def dscr(name, shape, dt):
        return nc.dram_tensor(name, list(shape), dt, kind="Internal").ap()

    NAmax = max(NA_A, NB_all)
    NOmax = max(NA_A, NB_own)
    NAp = rup(NAmax, 1024)
    NOp = rup(NOmax, 1024)

    w_in = din("w_in", [D, 3584])
    w_out = din("w_out", [D, D])
    w_up = din("w_up", [D, 2 * FFN])
    w_down = din("w_down", [FFN, D])
    gcols = din("gcols", [128, 48])
    gpost = din("gpost", [2, D])
    qkn = din("qkn", [2, 128])

    jobs = []
    for nm, N_all, N_own, N_out in (("A", NA_A, NA_A, NA_A), ("B", NB_all, NB_own, NB_out)):
        j = Job()
        j.name = nm
        j.N_all, j.N_own, j.N_out = N_all, N_own, N_out
        j.x = din(f"x{nm}", [N_all, D])
        j.cos = din(f"cos{nm}", [N_all, 128])
        j.sin = din(f"sin{nm}", [N_all, 128])
        j.lruP = din(f"lruP{nm}", [128, 96])
        j.gates = din(f"gates{nm}", [32 * 128, 128])
        j.ffnP = din(f"ffnP{nm}", [128, 176])
        j.out = dout(f"y{nm}", [N_out, D])
        jobs.append(j)

    class Scr:
        pass
    scr = Scr()
    scr.wb_in = dscr("wb_in", [16, 128, 3584], BF16)
    scr.wb_out = dscr("wb_out", [16, 128, D], BF16)
    scr.wb_up = dscr("wb_up", [NJ, 128, 16, 2, 128], BF16)
    scr.wb_down = dscr("wb_down", [NJ, 128, D], BF16)
    scr.xnT = dscr("s_xnT", [16, 128, NAp], BF16)
    scr.QT = dscr("s_QT", [128, 8, NOp], BF16)
    scr.KT = dscr("s_KT", [128, 2, NAp], BF16)
    scr.V = dscr("s_V", [NAp, 256], BF16)
    scr.xrT = dscr("s_xrT", [LRUW, NAp], F32)
    scr.gyT = dscr("s_gyT", [LRUW, NAp], F32)
    scr.xhT = dscr("s_xhT", [LRUW, NAp], F32)
    scr.xcbT = dscr("s_xcbT", [LRUW, NAp], BF16)
    scr.lruT = dscr("s_lruT", [LRUW, NAp], BF16)
    scr.attnT = dscr("s_attnT", [8, 128, NOp], BF16)
    scr.h1 = dscr("s_h1", [NOp, D], F32)
    scr.hnT = dscr("s_hnT", [16, 128, NOp], BF16)

    with contextlib.ExitStack() as top:
        sems = {e: top.enter_context(nc.semaphore(f"sem_{e}")) for e in ENGS}
        dsems = {e: [top.enter_context(nc.semaphore(f"dsem_{e}{i}")) for i in range(NDSEM)]
                 for e in ('sp', 'pool', 'act')}
        S = Sched(nc, sems, dsems)
        O = Ops(S)
        identf = T(top.enter_context(nc.sbuf_tensor("identf", [128, 128], F32)))
        ident = T(top.enter_context(nc.sbuf_tensor("ident", [128, 128], BF16)))
        O.memset('pool', identf.t[:], 0.0, identf)
        S.op('pool', lambda e: e.affine_select(out=identf.t[:], in_=identf.t[:], pattern=[[-1, 128]],
                                               compare_op=ALU.not_equal, fill=1.0, base=0, channel_multiplier=1),
             bl(identf), bl(identf))
        O.cp('dve', ident.t[:], identf.t[:], identf, ident)

        gt = T(top.enter_context(nc.sbuf_tensor("gcols_sb", [128, 48], F32)))
        S.flush()
        phase_W(nc, S, O, w_in, gcols, scr, gt)
        for ji, j in enumerate(jobs):
            phase_A1(nc, S, O, j, scr, ident, qkn)
            phase_A2(nc, S, O, j, scr)
            nconv = 8 * len(chunks(j.N_all, 2048))
            bg = [((lambda P, j=j: gen_conv(nc, S, O, P, j, scr)), nconv)]
            if ji == 0:
                bg.append(((lambda P: gen_W2(nc, S, O, P, w_out, w_up, w_down, scr, gt)), 124))
            phase_T(nc, S, O, j, scr, ident, bg)
            phase_L(nc, S, O, j, scr, identf)
            phase_M(nc, S, O, j, scr, ident, gpost)
            phase_F(nc, S, O, j, scr, gpost)
        S.flush(final=True)
    nc._sched_stats = (S.total_ops, S.total_waits, dict(S.cnt))
    return nc


def phase_W(nc, S, O, w_in, gcols, scr, gt):
    P = Ph(nc, "W")
    with P.st:
        O.dma('sp', gt.t[:], gcols[:, :], (), gt)
        wf = P.ring([128, 1792], F32, 3)
        wb = P.ring([128, 1792], BF16, 3)
        for kc in range(16):
            rows = slice(kc * 128, (kc + 1) * 128)
            for c0 in (0, 1792):
                a = wf.next()
                b = wb.next()
                O.dma('sp', a.t[:, :], w_in[rows, c0:c0 + 1792], (), a)
                if c0 == 0:
                    O.act(b.t[:, :], a.t[:, :], AF.Copy, (a, gt), b, scale=gt.t[:, kc:kc + 1])
                else:
                    O.ts('dve', b.t[:, :], a.t[:, :], gt.t[:, kc:kc + 1], None, ALU.mult, ALU.bypass, (a, gt), b)
                O.dma('pool', scr.wb_in[kc, :, c0:c0 + 1792], b.t[:, :], b, ())
        S.flush()


def gen_W2(nc, S, O, P, w_out, w_up, w_down, scr, gt):
    CW = 2816
    wf = P.ring([128, CW], F32, 3)
    wb = P.ring([128, CW], BF16, 3)

    def unit(src_ap, cw, scale, dst_ap, dst_view=None):
        a = wf.next()
        b = wb.next()
        O.dma('sp', a.t[:, :cw], src_ap, (), a)
        if scale is None:
            O.cp('dve', b.t[:, :cw], a.t[:, :cw], a, b)
        else:
            O.ts('dve', b.t[:, :cw], a.t[:, :cw], scale, None, ALU.mult, ALU.bypass, (a, gt), b)
        O.dma('pool', dst_ap, b.t[:, :cw] if dst_view is None else dst_view(b), b, ())

    for kc in range(16):
        rows = slice(kc * 128, (kc + 1) * 128)
        unit(w_out[rows, :], D, gt.t[:, 16 + kc:17 + kc], scr.wb_out[kc, :, :])
        yield
    for kc in range(16):
        rows = slice(kc * 128, (kc + 1) * 128)
        for half in range(2):
            for jh in range(2):
                c0 = half * FFN + jh * CW
                dst = scr.wb_up[jh * 22:(jh + 1) * 22, :, kc, half, :].rearrange("j p n -> p j n")
                unit(w_up[rows, c0:c0 + CW], CW, gt.t[:, 32 + kc:33 + kc], dst,
                     lambda b: b.t[:, :CW].rearrange("p (j n) -> p j n", n=128))
                yield
    for kc in range(NJ):
        unit(w_down[kc * 128:(kc + 1) * 128, :], D, None, scr.wb_down[kc, :, :])
        yield


def gen_conv(nc, S, O, P, job, scr):
    NA = job.N_all
    prm = P.sb([128, 96], F32)
    O.dma('sp', prm.t[:], job.lruP[:, :], (), prm)
    hw = P.sb([128, 48], F32)
    O.ts('dve', hw.t[:, :], prm.t[:, 0:48], 0.5, None, ALU.mult, ALU.bypass, prm, hw)
    xrb = P.ring([128, 2052], F32, 2)
    acc = P.ring([128, 2048], F32, 2)
    cbr = P.ring([128, 2048], BF16, 2)
    for c in range(8):
        rows = slice(c * 128, (c + 1) * 128)
        for (b0, bn) in chunks(NA, 2048):
            XR = xrb.next()
            lo = max(0, b0 - 2)
            hi = min(NA, b0 + bn + 2)
            if b0 - 2 < 0:
                O.memset('dve', XR.t[:, 0:2], 0.0, XR)
            if b0 + bn + 2 > NA:
                O.memset('dve', XR.t[:, NA - (b0 - 2):bn + 4], 0.0, XR)
            O.dma('sp', XR.t[:, lo - (b0 - 2):hi - (b0 - 2)], scr.xrT[rows, lo:hi], XR, XR)
            A = acc.next()
            O.ts('dve', A.t[:, :bn], XR.t[:, 0:bn], hw.t[:, c * 5:c * 5 + 1], hw.t[:, 40 + c:41 + c],
                 ALU.mult, ALU.add, (XR, hw), A)
            for k in range(1, 5):
                A2 = acc.next()
                O.stt(A2.t[:, :bn], XR.t[:, k:k + bn], hw.t[:, c * 5 + k:c * 5 + k + 1], A.t[:, :bn],
                      ALU.mult, ALU.add, (XR, hw, A), A2)
                A = A2
            O.dma('pool', scr.xhT[rows, b0:b0 + bn], A.t[:, :bn], A, ())
            CB = cbr.next()
            O.ts('dve', CB.t[:, :bn], A.t[:, :bn], 2.0, None, ALU.mult, ALU.bypass, A, CB)
            O.dma('pool', scr.xcbT[rows, b0:b0 + bn], CB.t[:, :bn], CB, ())
            yield


def phase_A1(nc, S, O, job, scr, ident, qkn):
    NA, NO = job.N_all, job.N_own
    P = Ph(nc, "A1" + job.name)
    with P.st:
        w = P.sb([128, 16, 1536], BF16)
        for kc in range(16):
            O.dma('sp', w.t[:, kc, :], scr.wb_in[kc, :, 0:1536], (), w.B(kc))
        gqk = P.sb([128, 2, 128], F32)
        O.dma('sp', gqk.t[:, 0, :], qkn[0:1, :].to_broadcast([128, 128]), (), gqk)
        O.dma('sp', gqk.t[:, 1, :], qkn[1:2, :].to_broadcast([128, 128]), gqk, gqk)
        xt = P.ring([128, D], F32, 2)
        junk = P.sb([128, D], BF16)
        xn = P.ring([128, D], BF16, 3)
        xnT = P.ring([128, 16, 512], BF16, 2)
        cst = P.ring([128, 2, 128], F32, 8)
        csg = P.ring([128, 4, 128], F32, 2)
        sm = P.ring([128, 4], F32, 4)
        ssq = P.ring([128, 32], F32, 2)
        sqt = P.sb([128, 1280], F32)
        qn = P.sb([128, 1280], F32)
        t1 = P.sb([128, 1280], F32)
        t2 = P.sb([128, 1280], F32)
        qr = P.ring([128, 1280], BF16, 3)
        qT = P.ring([128, 8, 512], BF16, 2)
        kT = P.ring([128, 2, 512], BF16, 2)
        vsb = P.ring([128, 256], BF16, 2)
        pTa = P.ps([128, 8, 128], BF16)
        pTb = P.ps([128, 8, 128], BF16)
        pq = P.ps([128, 1024], F32)
        pkv = P.ps([128, 512], F32)
        pqT = P.ps([128, 8, 128], BF16)
        pkT = P.ps([128, 8, 128], BF16)

        zq = P.ring([128, 1024], F32, 3)
        zkv = P.ring([128, 512], F32, 4)
        work = []
        for (g0, gn) in chunks(NA, 512):
            grp = {'g0': g0, 'gn': gn, 'doq': g0 < NO, 'nsub': len(chunks(gn, 128))}
            for j, (s0, tn) in enumerate(chunks(gn, 128)):
                work.append({'grp': grp, 'j': j, 's0': s0, 'tn': tn})

        def st0(c):
            grp, j, s0, tn = c['grp'], c['j'], c['s0'], c['tn']
            t0 = grp['g0'] + s0
            X = xt.next()
            O.dma('sp', X.t[:tn, :], job.x[t0:t0 + tn, :], (), X)
            CS = cst.next()
            O.dma('sp', CS.t[:tn, 0, :], job.cos[t0:t0 + tn, :], (), CS)
            O.dma('sp', CS.t[:tn, 1, :], job.sin[t0:t0 + tn, :], CS, CS)
            s = sm.next()
            O.act(junk.t[:tn, :], X.t[:tn, :], AF.Square, X, s, accum_out=s.t[:tn, 0:1])
            O.rstd(s.t[:tn, 2:3], s.t[:tn, 0:1], D, s.t[:tn, 1:2], s, s)
            XN = xn.next()
            O.act(XN.t[:tn, :], X.t[:tn, :], AF.Copy, (X, s), XN, scale=s.t[:tn, 2:3])
            c['XN'], c['CS'] = XN, CS

        def st0b(c):
            grp, j, s0, tn = c['grp'], c['j'], c['s0'], c['tn']
            if j == 0:
                grp['XT'] = xnT.next()
            XT, XN = grp['XT'], c['XN']
            for kc in range(16):
                pT = pTa if kc < 8 else pTb
                O.tr(pT.t[:, kc % 8, 0:tn], XN.t[:tn, kc * 128:(kc + 1) * 128], ident.t[:tn, :tn],
                     (XN, ident), pT)
            O.cp('dve', XT.t[:, 0:8, s0:s0 + tn], pTa.t[:, :, 0:tn], pTa, XT.B(j))
            O.cp('act', XT.t[:, 8:16, s0:s0 + tn], pTb.t[:, :, 0:tn], pTb, XT.B(j))

        def st1(c):
            grp, j, s0, tn = c['grp'], c['j'], c['s0'], c['tn']
            XT, doq = grp['XT'], grp['doq']
            if doq:
                for n in range(2):
                    for kc in range(16):
                        O.mm(pq.t[:tn, n * 512:(n + 1) * 512], XT.t[:, kc, s0:s0 + tn],
                             w.t[:, kc, n * 512:(n + 1) * 512], kc == 0, kc == 15,
                             (XT.B(j), w.B(kc), pq.B(n)), pq.B(n))
            for kc in range(16):
                O.mm(pkv.t[:tn, :], XT.t[:, kc, s0:s0 + tn], w.t[:, kc, 1024:1536], kc == 0, kc == 15,
                     (XT.B(j), w.B(kc), pkv), pkv)
            if doq:
                ZQ = zq.next()
                O.cp('act', ZQ.t[:tn, :], pq.t[:tn, :], (pq.B(0), pq.B(1)), ZQ)
                c['ZQ'] = ZQ
            ZKV = zkv.next()
            O.cp('dve', ZKV.t[:tn, :], pkv.t[:tn, :], pkv, ZKV)
            c['ZKV'] = ZKV
            if j == grp['nsub'] - 1:
                g0, gn = grp['g0'], grp['gn']
                O.dma('pool', scr.xnT[:, :, g0:g0 + gn].rearrange("k p t -> p k t"), XT.t[:, :, :gn],
                      [XT.B(i) for i in range(grp['nsub'])], ())

        def st2a(c):
            grp, j, s0, tn = c['grp'], c['j'], c['s0'], c['tn']
            doq = grp['doq']
            ZKV, CS = c['ZKV'], c['CS']
            G = csg.next()
            O.tt('dve', G.t[:tn, 0:2, :], CS.t[:tn, :, :], gqk.t[:tn, 0:1, :].to_broadcast([tn, 2, 128]),
                 ALU.mult, (CS, gqk), G)
            O.tt('dve', G.t[:tn, 2:4, :], CS.t[:tn, :, :], gqk.t[:tn, 1:2, :].to_broadcast([tn, 2, 128]),
                 ALU.mult, (CS, gqk, G), G)
            QR = qr.next()
            sq = ssq.next()
            c['QR'] = QR

            def normrope(src, nh, c0, col0, srcbufs):
                O.tt('dve', sqt.t[:tn, col0:col0 + nh * 128], src, src, ALU.mult, srcbufs, sqt.B(c0))
                S.op('dve', lambda e: e.tensor_reduce(
                    out=sq.t[:tn, c0:c0 + nh],
                    in_=sqt.t[:tn, col0:col0 + nh * 128].rearrange("p (h d) -> p h d", d=128),
                    axis=mybir.AxisListType.X, op=ALU.add), bl(sqt.B(c0)), bl(sq))
                O.rstd(sq.t[:tn, 20 + c0:20 + c0 + nh], sq.t[:tn, c0:c0 + nh], 128,
                       sq.t[:tn, 10 + c0:10 + c0 + nh], sq, sq)
                cols = slice(col0, col0 + nh * 128)
                O.tt('dve', qn.t[:tn, cols].rearrange("p (h d) -> p h d", d=128),
                     src.rearrange("p (h d) -> p h d", d=128),
                     sq.t[:tn, 20 + c0:20 + c0 + nh].unsqueeze(2).to_broadcast([tn, nh, 128]),
                     ALU.mult, (srcbufs, sq), qn.B(c0))
                ti = 0 if c0 == 0 else 2
                O.tt('dve', t1.t[:tn, cols].rearrange("p (h d) -> p h d", d=128),
                     qn.t[:tn, cols].rearrange("p (h d) -> p h d", d=128),
                     G.t[:tn, ti:ti + 1, :].to_broadcast([tn, nh, 128]), ALU.mult, (qn.B(c0), G), t1.B(c0))
                O.tt('dve', t2.t[:tn, cols].rearrange("p (h d) -> p h d", d=128),
                     qn.t[:tn, cols].rearrange("p (h d) -> p h d", d=128),
                     G.t[:tn, ti + 1:ti + 2, :].to_broadcast([tn, nh, 128]), ALU.mult, (qn.B(c0), G),
                     t2.B(c0))
                O.tt('dve', QR.t[:tn, cols].rearrange("p (a h j) -> p a h j", h=2, j=32),
                     t1.t[:tn, cols].rearrange("p (a h j) -> p a h j", h=2, j=32),
                     swap_halves(t2.t[:tn, cols], nh), ALU.add, (t1.B(c0), t2.B(c0)), QR.B(c0))

            if doq:
                normrope(c['ZQ'].t[:tn, :], 8, 0, 0, (c['ZQ'],))
            normrope(ZKV.t[:tn, 0:256], 2, 8, 1024, (ZKV,))

        def st2b(c):
            grp, j, s0, tn = c['grp'], c['j'], c['s0'], c['tn']
            doq = grp['doq']
            t0 = grp['g0'] + s0
            if j == 0:
                grp['QT'] = qT.next() if doq else None
                grp['KT'] = kT.next()
            QT, KT, QR, ZKV = grp['QT'], grp['KT'], c['QR'], c['ZKV']
            Vt = vsb.next()
            O.cp('act', Vt.t[:tn, :], ZKV.t[:tn, 256:512], ZKV, Vt)
            O.dma('pool', scr.V[t0:t0 + tn, :], Vt.t[:tn, :], Vt, ())
            if doq:
                for h in range(8):
                    O.tr(pqT.t[:, h, 0:tn], QR.t[:tn, h * 128:(h + 1) * 128], ident.t[:tn, :tn],
                         (QR.B(0), ident), pqT)
                O.cp('act', QT.t[:, :, s0:s0 + tn], pqT.t[:, :, 0:tn], pqT, QT.B(j))
            for h in range(2):
                O.tr(pkT.t[:, h, 0:tn], QR.t[:tn, 1024 + h * 128:1024 + (h + 1) * 128], ident.t[:tn, :tn],
                     (QR.B(8), ident), pkT)
            O.cp('dve', KT.t[:, :, s0:s0 + tn], pkT.t[:, 0:2, 0:tn], pkT, KT.B(j))
            if j == grp['nsub'] - 1:
                g0, gn, nsub = grp['g0'], grp['gn'], grp['nsub']
                if doq:
                    O.dma('pool', scr.QT[:, :, g0:g0 + gn], QT.t[:, :, :gn], [QT.B(i) for i in range(nsub)], ())
                O.dma('pool', scr.KT[:, :, g0:g0 + gn], KT.t[:, :, :gn], [KT.B(i) for i in range(nsub)], ())

        nw = len(work)
        for k in range(-3, nw + 3):
            if 0 <= k - 2 < nw:
                st2a(work[k - 2])
            if 0 <= k < nw:
                st1(work[k])
            if 0 <= k + 2 < nw:
                st0b(work[k + 2])
            if 0 <= k - 3 < nw:
                st2b(work[k - 3])
            if 0 <= k + 3 < nw:
                st0(work[k + 3])
        S.flush()


def phase_A2(nc, S, O, job, scr):
    NA, NO = job.N_all, job.N_own
    P = Ph(nc, "A2" + job.name)
    with P.st:
        w = P.sb([128, 16, 2048], BF16)
        for kc in range(16):
            O.dma('sp', w.t[:, kc, :], scr.wb_in[kc, :, 1536:3584], (), w.B(kc))
        xg = P.ring([128, 16, 512], BF16, 2)
        stg = P.ring([128, 512], F32, 4)
        pz = P.pring([128, 512], F32, 6)
        for (g0, gn) in chunks(NA, 512):
            X = xg.next()
            O.dma('sp', X.t[:, :, :gn], scr.xnT[:, :, g0:g0 + gn].rearrange("k p t -> p k t"), (), X)
            nch = 16 if g0 < NO else 8
            for c in range(nch):
                pz_ = pz.next()
                for kc in range(16):
                    O.mm(pz_.t[:, :gn], w.t[:, kc, c * 128:(c + 1) * 128], X.t[:, kc, :gn], kc == 0, kc == 15,
                         (w.B(kc), X, pz_), pz_)
                sg = stg.next()
                O.cp('act' if c % 2 else 'dve', sg.t[:, :gn], pz_.t[:, :gn], pz_, sg)
                if c < 8:
                    dst = scr.xrT[c * 128:(c + 1) * 128, g0:g0 + gn]
                else:
                    dst = scr.gyT[(c - 8) * 128:(c - 7) * 128, g0:g0 + gn]
                O.dma('pool', dst, sg.t[:, :gn], sg, ())
        S.flush()


def phase_L(nc, S, O, job, scr, identf):
    NA, NO = job.N_all, job.N_own
    NAp = rup(NA, 1024)
    GB = 3
    P = Ph(nc, "L" + job.name)
    with P.st:
        prm = P.sb([128, 96], F32)
        O.dma('sp', prm.t[:], job.lruP[:, :], (), prm)
        der = P.sb([128, 72], F32)
        gw = P.sb([128, 32, 128], BF16)
        with contextlib.ExitStack() as inner:
            gwf = T(inner.enter_context(nc.sbuf_tensor("L_gwf" + job.name, [128, 32, 128], F32)))
            O.dma('sp', gwf.t[:], job.gates.rearrange("(b c) d -> c b d", c=128), (), gwf)
            O.cp('dve', gw.t[:], gwf.t[:], gwf, gw)
            O.ts('dve', der.t[:, 0:32], prm.t[:, 48:80], 0.5, None, ALU.mult, ALU.bypass, prm, der)
            O.ts('dve', der.t[:, 64:72], prm.t[:, 40:48], 0.5, None, ALU.mult, ALU.bypass, (prm, der), der)
            O.act(der.t[:, 48:64], prm.t[:, 80:96], AF.Exp, (prm, der), der, scale=-1.0)
            O.act(der.t[:, 48:64], der.t[:, 48:64], AF.Ln, der, der, bias=1.0)
            O.ts('dve', der.t[:, 32:48], der.t[:, 48:64], -4.0, None, ALU.mult, ALU.bypass, der, der)
            S.flush()
        xh = P.sb([128, NAp], F32)
        xcb = P.sb([128, NAp], BF16)
        hb = P.sb([128, NAp], F32)
        tl = P.ring([128, 1024], F32, 14)
        gyr = P.ring([128, 1024], F32, 4)
        hfr = P.ring([128, 1024], F32, 2)
        lo_ = P.ring([128, 1024], BF16, 2)
        pz = P.pring([128, 1024], F32, 4)
        tiles = chunks(NA, 1024)

        def load_x(q, cc, t0, tn):
            r_ = slice(cc * 128, (cc + 1) * 128)
            ti = t0 // 1024
            O.dma(q, xh.t[:, t0:t0 + tn], scr.xhT[r_, t0:t0 + tn], (), xh.B(ti))
            O.dma(q, xcb.t[:, t0:t0 + tn], scr.xcbT[r_, t0:t0 + tn], (), xcb.B(ti))

        for (t0, tn) in tiles:
            load_x('sp', 0, t0, tn)
        for c in range(8):
            rows = slice(c * 128, (c + 1) * 128)

            def gates_front(d, t0, tn):
                zr = pz.next()
                zi = pz.next()
                ti = t0 // 1024
                for (s0, sn) in chunks(tn, 512):
                    O.mm(zr.t[:, s0:s0 + sn], gw.t[:, (0 * 2 + d) * 8 + c, :], xcb.t[:, t0 + s0:t0 + s0 + sn],
                         True, True, (gw, xcb.B(ti), zr.B(s0)), zr.B(s0))
                    O.mm(zi.t[:, s0:s0 + sn], gw.t[:, (1 * 2 + d) * 8 + c, :], xcb.t[:, t0 + s0:t0 + s0 + sn],
                         True, True, (gw, xcb.B(ti), zi.B(s0)), zi.B(s0))
                zrb = [zr.B(s0) for (s0, sn) in chunks(tn, 512)]
                zib = [zi.B(s0) for (s0, sn) in chunks(tn, 512)]
                thr = tl.next()
                thi = tl.next()
                a = tl.next()
                O.act(thr.t[:, :tn], zr.t[:, :tn], AF.Tanh, (zrb, der), thr, scale=0.5,
                      bias=der.t[:, d * 8 + c:d * 8 + c + 1])
                O.act(thi.t[:, :tn], zi.t[:, :tn], AF.Tanh, (zib, der), thi, scale=0.5,
                      bias=der.t[:, 16 + d * 8 + c:16 + d * 8 + c + 1])
                O.act(a.t[:, :tn], thr.t[:, :tn], AF.Exp, (thr, der), a, scale=der.t[:, 32 + d * 8 + c:33 + d * 8 + c],
                      bias=der.t[:, 32 + d * 8 + c:33 + d * 8 + c])
                if d == 1:
                    O.tt('dve', thr.t[:, :tn], a.t[:, :tn], a.t[:, :tn], ALU.mult, a, thr)
                else:
                    O.act(thr.t[:, :tn], a.t[:, :tn], AF.Square, a, thr)
                O.stt(thi.t[:, :tn], thi.t[:, :tn], 1.0, xh.t[:, t0:t0 + tn], ALU.add, ALU.mult,
                      (thi, xh.B(ti)), thi)
                return a, thr, thi

            def batches(tl_):
                return [tl_[i:i + GB] for i in range(0, len(tl_), GB)]

            carry = None
            for batch in batches(list(reversed(tiles))):
                items = [(t0, tn) + gates_front(1, t0, tn) for (t0, tn) in batch]
                for (t0, tn, a, a2, u1) in items:
                    O.act(a2.t[:, :tn], a2.t[:, :tn], AF.Sqrt, a2, a2, scale=-1.0, bias=1.0)
                for (t0, tn, a, a2, u1) in items:
                    ti = t0 // 1024
                    O.tt('dve', u1.t[:, :tn], u1.t[:, :tn], a2.t[:, :tn], ALU.mult, (u1, a2), u1)
                    init = 0.0 if carry is None else carry[0]
                    O.scan(rev(hb.t[:, t0:t0 + tn]), rev(a.t[:, :tn]), rev(u1.t[:, :tn]), init,
                           (a, u1, carry[1] if carry else None), hb.B(ti))
                    carry = (hb.t[:, t0:t0 + 1], hb.B(ti))
                if c + 1 < 8:
                    for (t0, tn, a, a2, u1) in items:
                        if t0 >= NO:
                            load_x('pool', c + 1, t0, tn)
            carry = None
            ftiles = [(t0, tn) for (t0, tn) in tiles if t0 < NO]
            for batch in batches(ftiles):
                items = [(t0, tn) + gates_front(0, t0, tn) for (t0, tn) in batch]
                gys = []
                for (t0, tn, a, a2, u1) in items:
                    GY = gyr.next()
                    O.dma('sp', GY.t[:, :tn], scr.gyT[rows, t0:t0 + tn], (), GY)
                    gys.append(GY)
                for (t0, tn, a, a2, u1) in items:
                    O.act(a2.t[:, :tn], a2.t[:, :tn], AF.Sqrt, a2, a2, scale=-1.0, bias=1.0)
                for (t0, tn, a, a2, u1), GY in zip(items, gys):
                    O.act(GY.t[:, :tn], GY.t[:, :tn], AF.Gelu_apprx_tanh, GY, GY)
                for (t0, tn, a, a2, u1), GY in zip(items, gys):
                    ti = t0 // 1024
                    O.tt('dve', u1.t[:, :tn], u1.t[:, :tn], a2.t[:, :tn], ALU.mult, (u1, a2), u1)
                    HF = hfr.next()
                    init = 0.0 if carry is None else carry[0]
                    O.scan(HF.t[:, :tn], a.t[:, :tn], u1.t[:, :tn], init, (a, u1, carry[1] if carry else None), HF)
                    carry = (HF.t[:, tn - 1:tn], HF)
                    O.tt('dve', a2.t[:, :tn], HF.t[:, :tn], hb.t[:, t0:t0 + tn], ALU.add, (HF, hb.B(ti)), a2)
                    LO = lo_.next()
                    O.tt('dve', LO.t[:, :tn], a2.t[:, :tn], GY.t[:, :tn], ALU.mult, (a2, GY), LO)
                    O.dma('pool', scr.lruT[rows, t0:t0 + tn], LO.t[:, :tn], LO, ())
                if c + 1 < 8:
                    for (t0, tn, a, a2, u1) in items:
                        load_x('pool', c + 1, t0, tn)
        S.flush()


def phase_T(nc, S, O, job, scr, ident, bgf=None):
    NA, NO = job.N_all, job.N_own
    kch = chunks(NA, 128)
    nkc = len(kch)
    nfull = NA // 128
    P = Ph(nc, "T" + job.name)
    with P.st:
        KT = P.sb([128, 2, NA], BF16)
        O.dma('sp', KT.t[:], scr.KT[:, :, 0:NA], (), KT)
        V = P.sb([128, nkc, 2, 132], BF16)
        O.memset('dve', V.t[:, :, :, 128:132], 1.0, V)
        for hh in range(2):
            if nfull:
                O.dma('sp', V.t[:, 0:nfull, hh, 0:128],
                      scr.V[0:nfull * 128, hh * 128:(hh + 1) * 128].rearrange("(c p) d -> p c d", p=128), V, V)
            if nkc > nfull:
                kn = NA - nfull * 128
                O.dma('sp', V.t[:kn, nfull, hh, 0:128], scr.V[nfull * 128:NA, hh * 128:(hh + 1) * 128], V, V)
        QTb = P.ring([128, 8, 512], BF16, 2)
        PT = P.ring([128, 512], BF16, 4)
        oall = P.sb([128, 4, 1024], F32)
        junk = P.sb([128, 1024], BF16)
        sm = P.ring([128, 4], F32, 8)
        abf = P.ring([128, 1024], BF16, 2)
        aTg = P.ring([128, 8, 512], BF16, 2)
        pS = P.pring([128, 512], F32, 3)
        pO = [P.ps([128, 512], F32) for _ in range(4)]
        pT = P.ps([128, 8, 128], BF16)
        scale = float(HD) ** -0.5
        LOOK = 2
        qblocks = balanced(NO, 512)
        nsteps_total = len(qblocks) * 8 * nkc
        bgs = []
        for (mk, nunits) in (bgf or []):
            bgs.append([mk(P), max(1, (nsteps_total - 40) // nunits)])
        stepno = 0
        for (q0, qn) in qblocks:
            Q = QTb.next()
            O.dma('sp', Q.t[:, :, :qn], scr.QT[:, :, q0:q0 + qn], (), Q)
            subs = chunks(qn, 128)
            steps = [(h, ci) for h in range(8) for ci in range(nkc)]

            def emitS(idx):
                h, ci = steps[idx]
                k0, kn = kch[ci]
                ps = pS.next()
                O.mm(ps.t[:kn, :qn], KT.t[:, h // 4, k0:k0 + kn], Q.t[:, h, :qn], True, True, (KT, Q, ps), ps)
                pt = PT.next()
                O.act(pt.t[:kn, :qn], ps.t[:kn, :qn], AF.Exp, ps, pt, scale=scale)
                return pt

            pts = {}
            for idx in range(min(LOOK, len(steps))):
                pts[idx] = emitS(idx)
            for idx in range(len(steps)):
                if idx + LOOK < len(steps):
                    pts[idx + LOOK] = emitS(idx + LOOK)
                h, ci = steps[idx]
                k0, kn = kch[ci]
                kvh = h // 4
                pt = pts.pop(idx)
                stepno += 1
                for g_, ev_ in bgs:
                    if stepno % ev_ == 0:
                        next(g_, None)
                for si, (s0, sn) in enumerate(subs):
                    O.mm(pO[si].t[:sn, 0:129], pt.t[:kn, s0:s0 + sn], V.t[:kn, ci, kvh, 0:129],
                         ci == 0, ci == nkc - 1, (pt, V, pO[si]), pO[si])
                if ci == nkc - 1:
                    for si, (s0, sn) in enumerate(subs):
                        s = sm.next()
                        O.recip(s.t[:sn, 0:1], pO[si].t[:sn, 128:129], pO[si], s)
                        O.ts('dve', oall.t[:sn, si, h * 128:(h + 1) * 128], pO[si].t[:sn, 0:128], s.t[:sn, 0:1], None,
                             ALU.mult, ALU.bypass, (pO[si], s), oall.B(si))
            G = aTg.next()
            for si, (s0, sn) in enumerate(subs):
                s = sm.next()
                O.act(junk.t[:sn, :], oall.t[:sn, si, :], AF.Square, oall.B(si), s, accum_out=s.t[:sn, 0:1])
                O.rstd(s.t[:sn, 2:3], s.t[:sn, 0:1], 1024, s.t[:sn, 1:2], s, s)
                AB = abf.next()
                O.act(AB.t[:sn, :], oall.t[:sn, si, :], AF.Copy, (oall.B(si), s), AB, scale=s.t[:sn, 2:3])
                for h in range(8):
                    O.tr(pT.t[:, h, 0:sn], AB.t[:sn, h * 128:(h + 1) * 128], ident.t[:sn, :sn], (AB, ident), pT)
                O.cp('dve', G.t[:, :, s0:s0 + sn], pT.t[:, :, 0:sn], pT, G.B(si))
            O.dma('pool', scr.attnT[:, :, q0:q0 + qn].rearrange("k p t -> p k t"), G.t[:, :, :qn],
                  [G.B(si) for si in range(len(subs))], ())
        for g_, ev_ in bgs:
            for _ in g_:
                pass
        S.flush()


def phase_M(nc, S, O, job, scr, ident, gpost):
    NO = job.N_own
    P = Ph(nc, "M" + job.name)
    with P.st:
        wo = P.sb([128, 16, D], BF16)
        for kc in range(16):
            O.dma('sp', wo.t[:, kc, :], scr.wb_out[kc, :, :], (), wo.B(kc))
        gp = P.sb([128, D], F32)
        O.dma('sp', gp.t[:], gpost[0:1, :].to_broadcast([128, D]), (), gp)
        ones2 = P.sb([128, 2], F32)
        O.memset('dve', ones2.t[:], 1.0, ones2)
        aT = P.ring([128, 8, 128], BF16, 2)
        lT = P.ring([128, 8, 128], BF16, 2)
        xt = P.ring([128, D], F32, 3)
        sqr = P.ring([128, 8, 128], F32, 2)
        Y = P.ring([128, D], F32, 3)
        hnb = P.ring([128, D], BF16, 2)
        hnTg = P.ring([128, 16, 512], BF16, 2)
        junk = P.sb([128, D], BF16)
        sm = P.ring([128, 12], F32, 4)
        pA = P.pring([128, 512], F32, 2)
        pB = P.pring([128, 512], F32, 2)
        pss = P.ps([128, 512], F32)
        pTa = P.ps([128, 8, 128], BF16)
        pTb = P.ps([128, 8, 128], BF16)
        lruT3 = scr.lruT.rearrange("(k p) t -> p k t", p=128)
        work = []
        for (g0, gn) in chunks(NO, 512):
            grp = {'g0': g0, 'gn': gn, 'nsub': len(chunks(gn, 128))}
            for j, (s0, tn) in enumerate(chunks(gn, 128)):
                work.append({'grp': grp, 'j': j, 's0': s0, 'tn': tn})

        def s1(c, inter):
            grp, j, s0, tn = c['grp'], c['j'], c['s0'], c['tn']
            t0 = grp['g0'] + s0
            A = aT.next()
            L = lT.next()
            X = xt.next()
            O.dma('sp', A.t[:, :, :tn], scr.attnT[:, :, t0:t0 + tn].rearrange("k p t -> p k t"), (), A)
            O.dma('sp', L.t[:, :, :tn], lruT3[:, :, t0:t0 + tn], (), L)
            O.dma('sp', X.t[:tn, :], job.x[t0:t0 + tn, :], (), X)
            s = sm.next()
            SQ = sqr.next()
            O.tt('dve', SQ.t[:, :, :tn], L.t[:, :, :tn], L.t[:, :, :tn], ALU.mult, L, SQ)
            for kc in range(8):
                O.mm(pss.t[:tn, 0:2], SQ.t[:, kc, :tn], ones2.t[:, 0:2], kc == 0, kc == 7, (SQ, ones2, pss), pss)
            O.rstd(s.t[:tn, 2:3], pss.t[:tn, 0:1], LRUW, s.t[:tn, 1:2], (pss, s), s)
            y = Y.next()
            for n in range(4):
                cs = slice(n * 512, (n + 1) * 512)
                a_ = pA.next()
                b_ = pB.next()
                for kc in range(8):
                    O.mm(a_.t[:tn, :], A.t[:, kc, :tn], wo.t[:, kc, cs], kc == 0, kc == 7, (A, wo.B(kc), a_), a_)
                for kc in range(8):
                    O.mm(b_.t[:tn, :], L.t[:, kc, :tn], wo.t[:, 8 + kc, cs], kc == 0, kc == 7, (L, wo.B(8 + kc), b_), b_)
                O.act(y.t[:tn, cs], a_.t[:tn, :], AF.Copy, a_, y.B(n))
                O.stt(y.t[:tn, cs], b_.t[:tn, :], s.t[:tn, 2:3], y.t[:tn, cs], ALU.mult, ALU.add,
                      (b_, s, y.B(n)), y.B(n))
                if n < len(inter) and inter[n] is not None:
                    inter[n]()
            c['X'], c['s'], c['y'] = X, s, y

        def s2a(c):
            grp, j, s0, tn = c['grp'], c['j'], c['s0'], c['tn']
            t0 = grp['g0'] + s0
            X, s, y = c['X'], c['s'], c['y']
            yb = [y.B(n) for n in range(4)]
            O.act(junk.t[:tn, :], y.t[:tn, :], AF.Square, yb, s, accum_out=s.t[:tn, 4:5])
            O.rstd(s.t[:tn, 6:7], s.t[:tn, 4:5], D, s.t[:tn, 5:6], s, s)
            O.stt(y.t[:tn, :], y.t[:tn, :], s.t[:tn, 6:7], gp.t[:tn, :], ALU.mult, ALU.mult, (yb, s, gp), yb)
            O.stt(X.t[:tn, :], y.t[:tn, :], 1.0, X.t[:tn, :], ALU.mult, ALU.add, (yb, X), X)
            O.dma('pool', scr.h1[t0:t0 + tn, :], X.t[:tn, :], X, ())

        def s2b(c):
            grp, j, s0, tn = c['grp'], c['j'], c['s0'], c['tn']
            X, s = c['X'], c['s']
            O.act(junk.t[:tn, :], X.t[:tn, :], AF.Square, X, s, accum_out=s.t[:tn, 8:9])
            O.rstd(s.t[:tn, 10:11], s.t[:tn, 8:9], D, s.t[:tn, 9:10], s, s)
            HN = hnb.next()
            O.act(HN.t[:tn, :], X.t[:tn, :], AF.Copy, (X, s), HN, scale=s.t[:tn, 10:11])
            c['HN'] = HN

        def s3(c):
            grp, j, s0, tn = c['grp'], c['j'], c['s0'], c['tn']
            if j == 0:
                grp['HG'] = hnTg.next()
            HG, HN = grp['HG'], c['HN']
            for kc in range(16):
                pT = pTa if kc < 8 else pTb
                O.tr(pT.t[:, kc % 8, 0:tn], HN.t[:tn, kc * 128:(kc + 1) * 128], ident.t[:tn, :tn],
                     (HN, ident), pT)
            O.cp('dve', HG.t[:, 0:8, s0:s0 + tn], pTa.t[:, :, 0:tn], pTa, HG.B(j))
            O.cp('act', HG.t[:, 8:16, s0:s0 + tn], pTb.t[:, :, 0:tn], pTb, HG.B(j))
            if j == grp['nsub'] - 1:
                g0, gn = grp['g0'], grp['gn']
                O.dma('pool', scr.hnT[:, :, g0:g0 + gn].rearrange("k p t -> p k t"), HG.t[:, :, :gn],
                      [HG.B(i) for i in range(grp['nsub'])], ())

        nw = len(work)
        for k in range(nw + 3):
            f2b = (lambda c=work[k - 2]: s2b(c)) if 0 <= k - 2 < nw else None
            f2a = (lambda c=work[k - 1]: s2a(c)) if 0 <= k - 1 < nw else None
            if k < nw:
                s1(work[k], [f2b, f2a])
            else:
                if f2b:
                    f2b()
                if f2a:
                    f2a()
            if 0 <= k - 3 < nw:
                s3(work[k - 3])
        S.flush()


def phase_F(nc, S, O, job, scr, gpost):
    NO, NOUT = job.N_own, job.N_out
    P = Ph(nc, "F" + job.name)
    with P.st:
        gp = P.sb([128, D], F32)
        O.dma('sp', gp.t[:], gpost[1:2, :].to_broadcast([128, D]), (), gp)
        fp = P.sb([128, 176], F32)
        O.dma('sp', fp.t[:], job.ffnP[:, :], (), fp)
        hb = P.ring([128, 16, 512], BF16, 2)
        wu = P.ring([128, 16, 2, 128], BF16, 3)
        wd = P.ring([128, 4, 512], BF16, 3)
        aT = P.sb([128, NJ, 512], BF16)
        Fr = P.ring([128, D], F32, 4)
        h1t = P.ring([128, D], F32, 4)
        cr = P.ring([128, 512], F32, 4)
        slr = P.ring([128, 512], F32, 2)
        junk = P.sb([128, D], BF16)
        sm = P.ring([128, 4], F32, 4)
        pG = P.pring([128, 512], F32, 2)
        pU = P.pring([128, 512], F32, 2)
        pD = [P.ps([128, 512], F32) for _ in range(4)]
        evi = 0
        blocks = tail_balanced(NOUT, 510)

        def loadH(bi):
            b0, bn = blocks[bi]
            H = hb.next()
            lo = max(b0 - 1, 0)
            hi = min(b0 + bn + 1, NO)
            if b0 == 0:
                O.memset('pool', H.t[:, :, 0:1], 0.0, H)
            if b0 + bn + 1 > NO:
                O.memset('pool', H.t[:, :, bn + 1:bn + 2], 0.0, H)
            O.dma('sp', H.t[:, :, lo - (b0 - 1):hi - (b0 - 1)],
                  scr.hnT[:, :, lo:hi].rearrange("k p t -> p k t"), H, H)
            return H

        Hnext = loadH(0)
        pending = []
        for bi, (b0, bn) in enumerate(blocks):
            H = Hnext
            nb = bn + 2
            for j in range(NJ):
                Wu = wu.next()
                O.dma('sp', Wu.t[:], scr.wb_up[j], (), Wu)
                G = pG.next()
                U = pU.next()
                for kc in range(16):
                    O.mm(G.t[:, :nb], Wu.t[:, kc, 0, :], H.t[:, kc, :nb], kc == 0, kc == 15, (Wu, H, G), G)
                for kc in range(16):
                    O.mm(U.t[:, :nb], Wu.t[:, kc, 1, :], H.t[:, kc, :nb], kc == 0, kc == 15, (Wu, H, U), U)
                c0 = cr.next()
                O.ts('dve', c0.t[:, :bn], G.t[:, 0:bn], fp.t[:, j * 3:j * 3 + 1], fp.t[:, 132 + j:133 + j],
                     ALU.mult, ALU.add, (G, fp), c0)
                c1 = cr.next()
                O.stt(c1.t[:, :bn], G.t[:, 1:bn + 1], fp.t[:, j * 3 + 1:j * 3 + 2], c0.t[:, :bn], ALU.mult, ALU.add,
                      (G, fp, c0), c1)
                c2 = cr.next()
                O.stt(c2.t[:, :bn], G.t[:, 2:bn + 2], fp.t[:, j * 3 + 2:j * 3 + 3], c1.t[:, :bn], ALU.mult, ALU.add,
                      (G, fp, c1), c2)
                sl = slr.next()
                O.act(sl.t[:, :bn], c2.t[:, :bn], AF.Silu, c2, sl)
                O.tt('dve', aT.t[:, j, :bn], sl.t[:, :bn], U.t[:, 1:bn + 1], ALU.mult, (sl, U), aT.B(j))
                if pending and j % 3 == 2:
                    pending.pop(0)()
            while pending:
                pending.pop(0)()
            if bi + 1 < len(blocks):
                Hnext = loadH(bi + 1)
            subs = chunks(bn, 128)
            Fs = [Fr.next() for _ in subs]
            H1s = []
            for si, (s0, sn) in enumerate(subs):
                H1 = h1t.next()
                O.dma('sp', H1.t[:sn, :], scr.h1[b0 + s0:b0 + s0 + sn, :], (), H1)
                H1s.append(H1)
            for n in range(4):
                cs = slice(n * 512, (n + 1) * 512)
                for q in range(11):
                    Wd = wd.next()
                    O.dma('sp', Wd.t[:], scr.wb_down[q * 4:(q + 1) * 4, :, cs].rearrange("k p c -> p k c"), (), Wd)
                    for kk in range(4):
                        kc = q * 4 + kk
                        for si, (s0, sn) in enumerate(subs):
                            O.mm(pD[si].t[:sn, :], aT.t[:, kc, s0:s0 + sn], Wd.t[:, kk, :], kc == 0, kc == NJ - 1,
                                 (aT.B(kc), Wd, pD[si]), pD[si])
                for si, (s0, sn) in enumerate(subs):
                    evi += 1
                    O.cp('act' if evi % 2 else 'dve', Fs[si].t[:sn, cs], pD[si].t[:sn, :], pD[si], Fs[si].B(n))
            def chain(si, s0, sn, Ft, H1, b0=b0):
                fb = [Ft.B(n) for n in range(4)]
                s = sm.next()
                O.act(junk.t[:sn, :], Ft.t[:sn, :], AF.Square, fb, s, accum_out=s.t[:sn, 0:1])
                O.rstd(s.t[:sn, 2:3], s.t[:sn, 0:1], D, s.t[:sn, 1:2], s, s)
                O.stt(Ft.t[:sn, :], Ft.t[:sn, :], s.t[:sn, 2:3], gp.t[:sn, :], ALU.mult, ALU.mult, (fb, s, gp), fb)
                O.stt(Ft.t[:sn, :], Ft.t[:sn, :], 1.0, H1.t[:sn, :], ALU.mult, ALU.add, (fb, H1), fb)
                O.dma('pool', job.out[b0 + s0:b0 + s0 + sn, :], Ft.t[:sn, :], fb, ())

            pending = [(lambda si=si, s0=s0, sn=sn, Ft=Fs[si], H1=H1s[si]: chain(si, s0, sn, Ft, H1))
                       for si, (s0, sn) in enumerate(subs)]
        for f in pending:
            f()
        S.flush()


def _rope_tables(n_tokens):
    rows = n_tokens // 64
    t_row = np.repeat(np.arange(rows), 64).astype(np.float32)
    t_col = np.tile(np.arange(64), rows).astype(np.float32)
    inv = (np.float32(10000.0) ** (-(np.arange(0, 64, 2, dtype=np.float32) / np.float32(64)))).astype(np.float32)
    ang = np.concatenate([t_row[:, None] * inv, t_col[:, None] * inv], axis=-1)
    ang = np.concatenate([np.zeros((NMETA, 64), np.float32), ang], axis=0).astype(np.float32)
    c = np.cos(ang).astype(np.float32).reshape(-1, 2, 32)
    s = np.sin(ang).astype(np.float32).reshape(-1, 2, 32)
    C = np.stack([c, c], axis=2).reshape(-1, 128)
    Sg = np.stack([s, -s], axis=2).reshape(-1, 128)
    return np.ascontiguousarray(C), np.ascontiguousarray(Sg)


def _pcol(v, k):
    return np.ascontiguousarray(np.asarray(v, np.float32).reshape(k, 128).T)


def _lru_pack(conv_w, conv_b, b_r, b_i, lam, flip):
    z = np.zeros((1, LRUW), np.float32)
    w5 = np.concatenate([conv_w, z], axis=0)
    if flip:
        w5 = w5[::-1]
        b_r, b_i, lam = b_r[::-1], b_i[::-1], lam[::-1]
    out = np.zeros((128, 96), np.float32)
    out[:, 0:40] = np.stack([_pcol(w5[k], 8) for k in range(5)], axis=-1).reshape(128, 40)
    out[:, 40:48] = _pcol(conv_b, 8)
    out[:, 48:64] = np.concatenate([_pcol(b_r[d], 8) for d in range(2)], axis=1)
    out[:, 64:80] = np.concatenate([_pcol(b_i[d], 8) for d in range(2)], axis=1)
    out[:, 80:96] = np.concatenate([_pcol(lam[d], 8) for d in range(2)], axis=1)
    return out


def _gates_pack(w_r, w_i, flip):
    if flip:
        w_r, w_i = w_r[::-1], w_i[::-1]
    return np.ascontiguousarray(np.stack([w_r, w_i], axis=0).reshape(32 * 128, 128).astype(np.float32))


def _ffn_pack(conv_w, conv_b, flip):
    if flip:
        conv_w = conv_w[::-1]
    out = np.zeros((128, 176), np.float32)
    out[:, 0:132] = np.stack([_pcol(conv_w[k], NJ) for k in range(3)], axis=-1).reshape(128, 132)
    out[:, 132:176] = _pcol(conv_b, NJ)
    return out


_PROG_CACHE = {}


def kernel(x_prompt, x_sample, meta_tokens, norm_pre_mix, w_in, q_norm, k_norm, lru_conv_w, lru_conv_b,
           lru_w_r, lru_b_r, lru_w_i, lru_b_i, lru_lambda, attn_out_norm, lru_out_norm, w_out,
           norm_post_mix, norm_pre_ffn, w_up, ffn_conv_w, ffn_conv_b, w_down, norm_post_ffn):
    f = lambda a: np.asarray(a, dtype=np.float32)
    x_prompt, x_sample, meta = f(x_prompt), f(x_sample), f(meta_tokens)
    Bp, Sp, _ = x_prompt.shape
    Bs, Ss, _ = x_sample.shape
    ncores = 8
    assert Bp == ncores and Bs * 2 == ncores
    LA = Sp + NMETA
    LB = Ss + NMETA
    assert LB % 2 == 0
    NBh = LB // 2
    NB_own = NBh + 1
    key = (LA, LB)
    if key not in _PROG_CACHE:
        _PROG_CACHE[key] = build_program(LA, LB, NB_own, NBh)
    nc = _PROG_CACHE[key]

    CA, SA = _rope_tables(Sp)
    CB, SB = _rope_tables(Ss)
    gcols = np.concatenate([_pcol(f(norm_pre_mix)[0], 16),
                            _pcol(np.concatenate([f(attn_out_norm)[0], f(lru_out_norm)[0]]), 16),
                            _pcol(f(norm_pre_ffn)[0], 16)], axis=1)
    gpost = np.ascontiguousarray(np.stack([f(norm_post_mix)[0], f(norm_post_ffn)[0]], axis=0))
    qkn = np.ascontiguousarray(np.stack([f(q_norm)[0], f(k_norm)[0]], axis=0))
    lp = [_lru_pack(f(lru_conv_w)[0], f(lru_conv_b)[0], f(lru_b_r)[0], f(lru_b_i)[0], f(lru_lambda)[0], fl)
          for fl in (False, True)]
    gp = [_gates_pack(f(lru_w_r)[0], f(lru_w_i)[0], fl) for fl in (False, True)]
    fpk = [_ffn_pack(f(ffn_conv_w)[0], f(ffn_conv_b)[0], fl) for fl in (False, True)]
    shared = {"w_in": np.ascontiguousarray(f(w_in)[0]), "w_out": np.ascontiguousarray(f(w_out)[0]),
              "w_up": np.ascontiguousarray(f(w_up)[0]), "w_down": np.ascontiguousarray(f(w_down)[0]),
              "gcols": gcols, "gpost": gpost, "qkn": qkn}
    in_maps = []
    for c in range(ncores):
        s, half = c // 2, c % 2
        xa = np.concatenate([meta, x_prompt[c]], axis=0)
        xb = np.concatenate([meta, x_sample[s]], axis=0)
        cb, sb = CB, SB
        if half:
            xb, cb, sb = xb[::-1], CB[::-1], SB[::-1]
        m = dict(shared)
        m.update({"xA": np.ascontiguousarray(xa), "cosA": CA, "sinA": SA, "lruPA": lp[0], "gatesA": gp[0],
                  "ffnPA": fpk[0],
                  "xB": np.ascontiguousarray(xb), "cosB": np.ascontiguousarray(cb), "sinB": np.ascontiguousarray(sb),
                  "lruPB": lp[half], "gatesB": gp[half], "ffnPB": fpk[half]})
        in_maps.append(m)
    res = run_bass_kernel_spmd(nc, in_maps, core_ids=list(range(ncores)))
    y_prompt = np.empty((Bp, Sp, D), np.float32)
    y_sample = np.empty((Bs, Ss, D), np.float32)
    for c in range(ncores):
        r = res.results[c]
        s, half = c // 2, c % 2
        y_prompt[c] = r["yA"][NMETA:]
        yb = r["yB"]
        if half == 0:
            y_sample[s, 0:NBh - NMETA] = yb[NMETA:]
        else:
            y_sample[s, NBh - NMETA:] = yb[::-1]
    return (y_prompt, y_sample)
```
